# Optimizing a Trainium2 kernel written in Bass

```python
import math
import jax, jax.numpy as jnp
from jax import lax
import numpy as np

D_MODEL = 1024
BATCH = 4
SEQ = 8192
DEPTH = 4

CTX_LEN = 256
GRID_W = 64
N_MIXERS = 3
N_DA = (DEPTH + 2) // 3
N_SW = (DEPTH + 1) // 3
N_SSD = DEPTH // 3
N_MOD = 6
NORM_EPS = 1e-6
ROPE_THETA = 10000.0
NEG_INF = -1e30

DA_HEADS = 8
DA_HEAD_DIM = 64
DA_WIDTH = 2 * DA_HEADS * DA_HEAD_DIM
DA_SCALE = DA_HEAD_DIM ** -0.5
DA_Q_BLOCK = 128

SW_HEADS = 16
SW_KV_HEADS = 4
SW_GROUP = SW_HEADS // SW_KV_HEADS
SW_HEAD_DIM = 64
SW_Q_WIDTH = SW_HEADS * SW_HEAD_DIM
SW_KV_WIDTH = SW_KV_HEADS * SW_HEAD_DIM
SW_WINDOW = 128
SW_BLOCK = 128
SW_SCALE = SW_HEAD_DIM ** -0.5

SSD_INNER = 2 * D_MODEL
SSD_HEAD_DIM = 64
SSD_HEADS = SSD_INNER // SSD_HEAD_DIM
SSD_GROUPS = 4
SSD_HPG = SSD_HEADS // SSD_GROUPS
SSD_STATE = 128
SSD_CONV = 5
SSD_CHUNK = 128
SSD_BC = SSD_GROUPS * SSD_STATE
SSD_CONV_CH = SSD_INNER + 2 * SSD_BC
SSD_IN = SSD_INNER + SSD_CONV_CH + 2 * SSD_HEADS

FFN_HIDDEN = -(-8 * D_MODEL // (3 * 256)) * 256

kernel_name = 'hybrid_interleaved_diffusion_trunk'


def rmsnorm(t, g):
    tf = t.astype(jnp.float32)
    tf = tf * lax.rsqrt(jnp.mean(tf * tf, axis=-1, keepdims=True) + NORM_EPS)
    return (tf * g.astype(jnp.float32)).astype(t.dtype)


def modulate(t, shift, scale):
    return t * (1.0 + scale) + shift


def rope_tables(rows, cols, head_dim):
    n_freq = head_dim // 4
    freqs = ROPE_THETA ** (-jnp.arange(n_freq, dtype=jnp.float32) / n_freq)
    ang = jnp.concatenate([rows[:, None].astype(jnp.float32) * freqs,
                           cols[:, None].astype(jnp.float32) * freqs], axis=-1)
    return jnp.cos(ang), jnp.sin(ang)


def apply_rope(t, cos, sin):
    half = t.shape[-1] // 2
    shp = (1, cos.shape[0]) + (1,) * (t.ndim - 3) + (half,)
    cs = cos.reshape(shp).astype(t.dtype)
    sn = sin.reshape(shp).astype(t.dtype)
    t1, t2 = t[..., :half], t[..., half:]
    return jnp.concatenate([t1 * cs - t2 * sn, t2 * cs + t1 * sn], axis=-1)


def diff_attend(q, k, v, lam):
    s = jnp.einsum('bqhmd,bkhmd->bhmqk', q, k).astype(jnp.float32) * DA_SCALE
    p = jax.nn.softmax(s, axis=-1)
    w = (p[:, :, 0] - lam * p[:, :, 1]).astype(v.dtype)
    return jnp.einsum('bhqk,bkhv->bqhv', w, v)


def diff_attn_mixer(u_ctx, u_lat, w_qkv, w_o, lam_p, subln_g, lam_init, cos, sin, want_ctx):
    b, S, _ = u_lat.shape
    Lc = u_ctx.shape[1]
    qkv = u_lat @ w_qkv
    q = apply_rope(qkv[..., :DA_WIDTH].reshape(b, S, DA_HEADS, 2, DA_HEAD_DIM), cos, sin)
    k = apply_rope(qkv[..., DA_WIDTH:2 * DA_WIDTH].reshape(b, S, DA_HEADS, 2, DA_HEAD_DIM), cos, sin)
    v = qkv[..., 2 * DA_WIDTH:].reshape(b, S, DA_HEADS, 2 * DA_HEAD_DIM)
    kv_c = u_ctx @ w_qkv[:, DA_WIDTH:]
    k_c = kv_c[..., :DA_WIDTH].reshape(b, Lc, DA_HEADS, 2, DA_HEAD_DIM)
    v_c = kv_c[..., DA_WIDTH:].reshape(b, Lc, DA_HEADS, 2 * DA_HEAD_DIM)
    lp = lam_p.astype(jnp.float32)
    lam = jnp.exp(jnp.sum(lp[0] * lp[1])) - jnp.exp(jnp.sum(lp[2] * lp[3])) + lam_init
    k_all = jnp.concatenate([k_c, k], axis=1)
    v_all = jnp.concatenate([v_c, v], axis=1)
    q_blocks = jnp.moveaxis(q.reshape(b, S // DA_Q_BLOCK, DA_Q_BLOCK, DA_HEADS, 2, DA_HEAD_DIM), 1, 0)
    o = lax.map(lambda qb: diff_attend(qb, k_all, v_all, lam), q_blocks)
    o = jnp.moveaxis(o, 0, 1).reshape(b, S, DA_HEADS, 2 * DA_HEAD_DIM)

    def finish(o_heads):
        o_n = rmsnorm(o_heads, subln_g) * (1.0 - lam_init)
        return o_n.reshape(o_n.shape[0], o_n.shape[1], DA_WIDTH) @ w_o

    o_lat = finish(o)
    o_ctx = None
    if want_ctx:
        q_c = (u_ctx @ w_qkv[:, :DA_WIDTH]).reshape(b, Lc, DA_HEADS, 2, DA_HEAD_DIM)
        o_ctx = finish(diff_attend(q_c, k_c, v_c, lam))
    return o_ctx, o_lat


def sink_softmax(s, sink):
    col = jnp.broadcast_to(sink, s.shape[:-1] + (1,))
    return jax.nn.softmax(jnp.concatenate([col, s], axis=-1), axis=-1)[..., 1:]


def band_blocks(t):
    b, S, hk, d = t.shape
    tb = jnp.pad(t.reshape(b, S // SW_BLOCK, SW_BLOCK, hk, d), ((0, 0), (1, 1), (0, 0), (0, 0), (0, 0)))
    return jnp.concatenate([tb[:, :-2], tb[:, 1:-1], tb[:, 2:]], axis=2)


def window_gqa_mixer(u_ctx, u_lat, w_qkv, w_o, sink, cos, sin, want_ctx):
    b, S, _ = u_lat.shape
    Lc = u_ctx.shape[1]
    nb = S // SW_BLOCK
    qkv = u_lat @ w_qkv
    q = apply_rope(qkv[..., :SW_Q_WIDTH].reshape(b, S, SW_KV_HEADS, SW_GROUP, SW_HEAD_DIM), cos, sin)
    k = apply_rope(qkv[..., SW_Q_WIDTH:SW_Q_WIDTH + SW_KV_WIDTH].reshape(b, S, SW_KV_HEADS, SW_HEAD_DIM), cos, sin)
    v = qkv[..., SW_Q_WIDTH + SW_KV_WIDTH:].reshape(b, S, SW_KV_HEADS, SW_HEAD_DIM)
    kv_c = u_ctx @ w_qkv[:, SW_Q_WIDTH:]
    k_c = kv_c[..., :SW_KV_WIDTH].reshape(b, Lc, SW_KV_HEADS, SW_HEAD_DIM)
    v_c = kv_c[..., SW_KV_WIDTH:].reshape(b, Lc, SW_KV_HEADS, SW_HEAD_DIM)
    sink_hg = sink.astype(jnp.float32).reshape(SW_KV_HEADS, SW_GROUP)[:, :, None, None]
    blk = jnp.arange(nb)[:, None, None]
    qpos = blk * SW_BLOCK + jnp.arange(SW_BLOCK)[None, :, None]
    kpos = (blk - 1) * SW_BLOCK + jnp.arange(3 * SW_BLOCK)[None, None, :]
    valid = (jnp.abs(qpos - kpos) <= SW_WINDOW) & (kpos >= 0) & (kpos < S)
    qb = q.reshape(b, nb, SW_BLOCK, SW_KV_HEADS, SW_GROUP, SW_HEAD_DIM)
    k_band, v_band = band_blocks(k), band_blocks(v)
    s_loc = jnp.einsum('bnqhgd,bnkhd->bnhgqk', qb, k_band).astype(jnp.float32) * SW_SCALE
    s_loc = jnp.where(valid[None, :, None, None], s_loc, NEG_INF)
    s_ctx = jnp.einsum('bnqhgd,bchd->bnhgqc', qb, k_c).astype(jnp.float32) * SW_SCALE
    p = sink_softmax(jnp.concatenate([s_ctx, s_loc], axis=-1), sink_hg).astype(v.dtype)
    o = (jnp.einsum('bnhgqc,bchd->bnqhgd', p[..., :Lc], v_c)
         + jnp.einsum('bnhgqk,bnkhd->bnqhgd', p[..., Lc:], v_band))
    o_lat = o.reshape(b, S, SW_Q_WIDTH) @ w_o
    o_ctx = None
    if want_ctx:
        q_c = (u_ctx @ w_qkv[:, :SW_Q_WIDTH]).reshape(b, Lc, SW_KV_HEADS, SW_GROUP, SW_HEAD_DIM)
        s_c = jnp.einsum('bqhgd,bkhd->bhgqk', q_c, k_c).astype(jnp.float32) * SW_SCALE
        p_c = sink_softmax(s_c, sink_hg).astype(v.dtype)
        o_ctx = jnp.einsum('bhgqk,bkhd->bqhgd', p_c, v_c).reshape(b, Lc, SW_Q_WIDTH) @ w_o
    return o_ctx, o_lat


def dwconv_centred(u, w, bias):
    ch = u.shape[-1]
    out = lax.conv_general_dilated(u, w[:, None, :].astype(u.dtype), window_strides=(1,),
                                   padding=((SSD_CONV // 2, SSD_CONV // 2),),
                                   dimension_numbers=('NWC', 'WIO', 'NWC'), feature_group_count=ch)
    return out + bias.astype(u.dtype)


def ssd_scan(x, dt, A, B, C, init_state, need_y):
    b, L, H, P = x.shape
    nc = L // SSD_CHUNK
    shp = (b, nc, SSD_CHUNK, SSD_GROUPS, SSD_HPG)
    xdt = (x.astype(jnp.float32) * dt[..., None]).reshape(shp + (P,))
    a_cs = jnp.cumsum((dt * A).reshape(shp), axis=2)
    Bc = B.astype(jnp.float32).reshape(b, nc, SSD_CHUNK, SSD_GROUPS, SSD_STATE)
    Cc = C.astype(jnp.float32).reshape(b, nc, SSD_CHUNK, SSD_GROUPS, SSD_STATE)
    decay_to_end = jnp.exp(a_cs[:, :, -1:] - a_cs)
    chunk_states = jnp.einsum('bclgn,bclgh,bclghp->bcghpn', Bc, decay_to_end, xdt)
    chunk_decay = jnp.exp(a_cs[:, :, -1])

    def step(state, inp):
        dec, st = inp
        return state * dec[..., None, None] + st, state

    final, prev = lax.scan(step, init_state.reshape(b, SSD_GROUPS, SSD_HPG, P, SSD_STATE),
                           (jnp.moveaxis(chunk_decay, 1, 0), jnp.moveaxis(chunk_states, 1, 0)))
    final = final.reshape(b, H, P, SSD_STATE)
    if not need_y:
        return None, final
    a_t = jnp.moveaxis(a_cs, 2, -1)
    seg = a_t[..., :, None] - a_t[..., None, :]
    tril = jnp.tril(jnp.ones((SSD_CHUNK, SSD_CHUNK), dtype=bool))
    decay_in = jnp.exp(jnp.where(tril, seg, -jnp.inf))
    cb = jnp.einsum('bclgn,bcsgn->bcgls', Cc, Bc)
    y_diag = jnp.einsum('bcgls,bcghls,bcsghp->bclghp', cb, decay_in, xdt)
    y_off = jnp.einsum('bclgn,bcghpn,bclgh->bclghp', Cc, jnp.moveaxis(prev, 0, 1), jnp.exp(a_cs))
    return (y_diag + y_off).reshape(b, L, H, P).astype(x.dtype), final


def ssd_mixer(u_ctx, u_lat, w_in, conv_w, conv_b, a_log, dt_bias, d_skip, norm_g, w_out, want_ctx):
    A = -jnp.exp(a_log.astype(jnp.float32))

    def project(u):
        b, L, _ = u.shape
        zxbcdt = u @ w_in
        z = zxbcdt[..., :SSD_INNER]
        xbc = jax.nn.silu(dwconv_centred(zxbcdt[..., SSD_INNER:SSD_INNER + SSD_CONV_CH], conv_w, conv_b))
        dt = jax.nn.softplus(zxbcdt[..., SSD_INNER + SSD_CONV_CH:].astype(jnp.float32).reshape(b, L, 2, SSD_HEADS)
                             + dt_bias.astype(jnp.float32))
        xs = xbc[..., :SSD_INNER].reshape(b, L, SSD_HEADS, SSD_HEAD_DIM)
        Bm = xbc[..., SSD_INNER:SSD_INNER + SSD_BC].reshape(b, L, SSD_GROUPS, SSD_STATE)
        Cm = xbc[..., SSD_INNER + SSD_BC:].reshape(b, L, SSD_GROUPS, SSD_STATE)
        return z, xs, Bm, Cm, dt

    def flip(t):
        return jnp.flip(t, axis=1)

    def finish(y_f, y_b, xs, z):
        b, L = z.shape[:2]
        y = y_f + flip(y_b) + xs * d_skip.astype(xs.dtype)[:, None]
        y = y.reshape(b, L, SSD_INNER) * jax.nn.silu(z)
        y = rmsnorm(y.reshape(b, L, SSD_GROUPS, SSD_INNER // SSD_GROUPS), norm_g.reshape(SSD_GROUPS, -1))
        return y.reshape(b, L, SSD_INNER) @ w_out

    z_c, x_c, B_c, C_c, dt_c = project(u_ctx)
    z_l, x_l, B_l, C_l, dt_l = project(u_lat)
    zero = jnp.zeros((u_ctx.shape[0], SSD_HEADS, SSD_HEAD_DIM, SSD_STATE), jnp.float32)
    yf_c, sf_c = ssd_scan(x_c, dt_c[:, :, 0], A[0], B_c, C_c, zero, want_ctx)
    yb_c, sb_c = ssd_scan(flip(x_c), flip(dt_c[:, :, 1]), A[1], flip(B_c), flip(C_c), zero, want_ctx)
    yf_l, _ = ssd_scan(x_l, dt_l[:, :, 0], A[0], B_l, C_l, sf_c, True)
    yb_l, _ = ssd_scan(flip(x_l), flip(dt_l[:, :, 1]), A[1], flip(B_l), flip(C_l), sb_c, True)
    o_lat = finish(yf_l, yb_l, x_l, z_l)
    o_ctx = finish(yf_c, yb_c, x_c, z_c) if want_ctx else None
    return o_ctx, o_lat


def swiglu(u, w_in, w_out):
    gu = u @ w_in
    return (jax.nn.silu(gu[..., :FFN_HIDDEN]) * gu[..., FFN_HIDDEN:]) @ w_out


def setup_inputs(seed: int = 0) -> dict:
    key = jax.random.key(seed)
    ks = jax.random.split(key, 24)

    def nrm(k, shape, scale):
        return jax.random.normal(k, shape, jnp.float32) * scale

    D = D_MODEL
    dt0 = jnp.exp(jax.random.uniform(ks[20], (N_SSD, 2, SSD_HEADS), jnp.float32,
                                     minval=math.log(1e-3), maxval=math.log(1e-1)))
    return {
        'x': nrm(ks[0], (BATCH, SEQ, D), 1.0),
        'c': nrm(ks[1], (BATCH, D), 1.0),
        'ctx': nrm(ks[2], (BATCH, CTX_LEN, D), 1.0),
        'c_ctx': nrm(ks[3], (D,), 1.0),
        'ada_w': nrm(ks[4], (DEPTH, D, N_MOD * D), 0.5 * D ** -0.5),
        'ada_b': nrm(ks[5], (DEPTH, N_MOD * D), 0.02),
        'norm_g': 1.0 + nrm(ks[6], (DEPTH, 4, D), 0.05),
        'ffn_w_in': nrm(ks[7], (DEPTH, D, 2 * FFN_HIDDEN), D ** -0.5),
        'ffn_w_out': nrm(ks[8], (DEPTH, FFN_HIDDEN, D), FFN_HIDDEN ** -0.5),
        'da_w_qkv': nrm(ks[9], (N_DA, D, 3 * DA_WIDTH), D ** -0.5),
        'da_w_o': nrm(ks[10], (N_DA, DA_WIDTH, D), DA_WIDTH ** -0.5),
        'da_lambda': nrm(ks[11], (N_DA, 4, DA_HEAD_DIM), 0.1),
        'da_subln': 1.0 + nrm(ks[12], (N_DA, 2 * DA_HEAD_DIM), 0.05),
        'sw_w_qkv': nrm(ks[13], (N_SW, D, SW_Q_WIDTH + 2 * SW_KV_WIDTH), D ** -0.5),
        'sw_w_o': nrm(ks[14], (N_SW, SW_Q_WIDTH, D), SW_Q_WIDTH ** -0.5),
        'sw_sink': nrm(ks[15], (N_SW, SW_HEADS), 0.5),
        'ssd_w_in': nrm(ks[16], (N_SSD, D, SSD_IN), D ** -0.5),
        'ssd_conv_w': nrm(ks[17], (N_SSD, SSD_CONV, SSD_CONV_CH), SSD_CONV ** -0.5),
        'ssd_conv_b': nrm(ks[18], (N_SSD, SSD_CONV_CH), 0.02),
        'ssd_a_log': jnp.log(jax.random.uniform(ks[19], (N_SSD, 2, SSD_HEADS), jnp.float32, minval=1.0, maxval=16.0)),
        'ssd_dt_bias': dt0 + jnp.log(-jnp.expm1(-dt0)),
        'ssd_d_skip': 1.0 + nrm(ks[21], (N_SSD, SSD_HEADS), 0.1),
        'ssd_norm': 1.0 + nrm(ks[22], (N_SSD, SSD_INNER), 0.05),
        'ssd_w_out': nrm(ks[23], (N_SSD, SSD_INNER, D), SSD_INNER ** -0.5),
    }


def reference(x, c, ctx, c_ctx, ada_w, ada_b, norm_g, ffn_w_in, ffn_w_out, da_w_qkv, da_w_o, da_lambda,
              da_subln, sw_w_qkv, sw_w_o, sw_sink, ssd_w_in, ssd_conv_w, ssd_conv_b, ssd_a_log, ssd_dt_bias,
              ssd_d_skip, ssd_norm, ssd_w_out):
    b, S, D = x.shape
    n_rows = S // GRID_W
    rows = jnp.repeat(jnp.arange(n_rows), GRID_W)
    cols = jnp.tile(jnp.arange(GRID_W), n_rows)
    cos_da, sin_da = rope_tables(rows, cols, DA_HEAD_DIM)
    cos_sw, sin_sw = rope_tables(rows, cols, SW_HEAD_DIM)
    act_lat = jax.nn.silu(c)
    act_ctx = jax.nn.silu(c_ctx)
    h_lat, h_ctx = x, ctx
    for i in range(DEPTH):
        want_ctx = i < DEPTH - 1
        j = i // N_MIXERS
        kind = i % N_MIXERS
        m_lat = (act_lat @ ada_w[i] + ada_b[i]).reshape(b, N_MOD, 1, D)
        m_ctx = (act_ctx @ ada_w[i] + ada_b[i]).reshape(N_MOD, D)
        u_lat = modulate(rmsnorm(h_lat, norm_g[i, 0]), m_lat[:, 0], m_lat[:, 1])
        u_ctx = modulate(rmsnorm(h_ctx, norm_g[i, 0]), m_ctx[0], m_ctx[1])
        if kind == 0:
            lam_init = 0.8 - 0.6 * math.exp(-0.3 * i)
            o_ctx, o_lat = diff_attn_mixer(u_ctx, u_lat, da_w_qkv[j], da_w_o[j], da_lambda[j], da_subln[j],
                                           lam_init, cos_da, sin_da, want_ctx)
        elif kind == 1:
            o_ctx, o_lat = window_gqa_mixer(u_ctx, u_lat, sw_w_qkv[j], sw_w_o[j], sw_sink[j],
                                            cos_sw, sin_sw, want_ctx)
        else:
            o_ctx, o_lat = ssd_mixer(u_ctx, u_lat, ssd_w_in[j], ssd_conv_w[j], ssd_conv_b[j], ssd_a_log[j],
                                     ssd_dt_bias[j], ssd_d_skip[j], ssd_norm[j], ssd_w_out[j], want_ctx)
        h_lat = h_lat + m_lat[:, 2] * rmsnorm(o_lat, norm_g[i, 1])
        u_lat = modulate(rmsnorm(h_lat, norm_g[i, 2]), m_lat[:, 3], m_lat[:, 4])
        h_lat = h_lat + m_lat[:, 5] * rmsnorm(swiglu(u_lat, ffn_w_in[i], ffn_w_out[i]), norm_g[i, 3])
        if want_ctx:
            h_ctx = h_ctx + m_ctx[2] * rmsnorm(o_ctx, norm_g[i, 1])
            u_ctx_f = modulate(rmsnorm(h_ctx, norm_g[i, 2]), m_ctx[3], m_ctx[4])
            h_ctx = h_ctx + m_ctx[5] * rmsnorm(swiglu(u_ctx_f, ffn_w_in[i], ffn_w_out[i]), norm_g[i, 3])
    return h_lat
```

```python
import contextlib
import math

import ml_dtypes
import numpy as np

import concourse.bass as bass
import concourse.mybir as mybir
from concourse.bass_utils import run_bass_kernel_spmd

F32 = mybir.dt.float32
I32 = mybir.dt.int32
BF16 = mybir.dt.bfloat16
AF = mybir.ActivationFunctionType
ALU = mybir.AluOpType
AX = mybir.AxisListType

D = 1024
SEQ = 8192
HALF = 4096
CTX = 256
DEPTH = 4
import os
NLAYERS = int(os.environ.get('KDEPTH', '4'))
EPS = 1e-6
FFN = 2816
NFC = FFN // 128
DA_SCALE = 64 ** -0.5
SEM_LIM = 8000
SB_LO, SB_HI = 16512, 227328
DEBUG_OUT = set()
import os
STOP = int(os.environ.get('KSTOP', '99'))
NCORES = int(os.environ.get('KCORES', '8'))
DEBUG_RES = {}


class _Op:
    __slots__ = ("eng", "fn", "dma", "grp", "waits", "sig", "idx", "needs", "pos")

    def __init__(self, eng, fn, dma, grp):
        self.eng = eng
        self.fn = fn
        self.dma = dma
        self.grp = grp
        self.waits = []
        self.sig = None
        self.idx = None
        self.needs = False
        self.pos = 0


def _ap(x):
    return x[0] if isinstance(x, tuple) else x


def _keys(x):
    if isinstance(x, tuple):
        k = x[1]
        return list(k) if isinstance(k, (list, tuple)) else [k]
    if isinstance(x, str):
        return [x]
    return [x.name]


class Prog:
    ENGS = ("pe", "act", "dve", "pool", "sp")

    def __init__(self, nc):
        self.nc = nc
        self.ops = {e: [] for e in self.ENGS}
        self.lastw = {}
        self.readers = {}
        self.grp_cnt = {}
        self.grp_unit = {}
        self.slot_of = {}
        self.waited = {e: {} for e in self.ENGS}
        self.sb_top = SB_LO
        self.hi_top = SB_HI
        self.live_hi = []
        self.live = []
        self.freed = []
        self.inh = {}
        self.subkeys = {}
        self.psum = nc.alloc_psum_tensor("psall", [128, 8, 512], F32)

    def sb(self, name, shape, dt, hi=False):
        n = 1
        for s in shape[1:]:
            n *= s
        nbytes = n * (4 if dt == F32 else 2)
        nbytes = (nbytes + 63) // 64 * 64
        if hi:
            self.hi_top -= nbytes
            off = self.hi_top
        else:
            off = self.sb_top
            self.sb_top += nbytes
        assert self.sb_top <= self.hi_top, f"SBUF overflow allocating {name}: {self.sb_top} {self.hi_top}"
        t = self.nc.alloc_sbuf_tensor_at(name, list(shape), dt, offset=off)
        key = t.name
        inh = []
        for (k0, a0, b0) in self.freed:
            if a0 < off + nbytes and off < b0:
                w = self.lastw.get(k0)
                if w is not None:
                    inh.append(w)
                inh.extend(self.readers.get(k0, ()))
                inh.extend(self.inh.get(k0, ()))
                for kx in self.subkeys.get(k0, ()):
                    w = self.lastw.get(kx)
                    if w is not None:
                        inh.append(w)
                    inh.extend(self.readers.get(kx, ()))
        if inh:
            red = {}
            for d in inh:
                kk = ("d", d.grp) if d.dma else ("e", d.eng)
                if kk not in red or d.pos > red[kk].pos:
                    red[kk] = d
            self.inh[key] = list(red.values())
        if hi:
            self.live_hi.append((key, off, off + nbytes))
        else:
            self.live.append((key, off, off + nbytes))
        return t

    def release_hi(self):
        self.freed.extend(self.live_hi)
        self.live_hi = []
        self.hi_top = SB_HI

    def mark(self):
        return (self.sb_top, len(self.live))

    def release(self, m):
        self.freed.extend(self.live[m[1]:])
        del self.live[m[1]:]
        self.sb_top = m[0]

    def pb(self, b, n=1, dt=F32):
        keys = [f"ps{b + i}" for i in range(n)]
        if n == 1:
            ap = self.psum[:, b, :]
        else:
            ap = self.psum[:, b:b + n, :]
        if dt != F32:
            ap = ap.bitcast(dt)
        return ap, keys

    NSLOT = int(os.environ.get("KNSLOT", "56"))

    def add(self, eng, fn, reads, writes, dma=False, grp=None, unit=16):
        if dma:
            if unit == 16:
                if grp not in self.slot_of:
                    self.slot_of[grp] = f"slot{len(self.slot_of) % self.NSLOT}"
                grp = self.slot_of[grp]
            self.grp_unit[grp] = unit
        op = _Op(eng, fn, dma, grp)
        rk, wk = [], []
        for r in reads:
            if r is not None and not isinstance(r, (int, float)):
                rk.extend(_keys(r))
        deps = []
        for w in writes:
            if w is not None:
                wk.extend(_keys(w))
                nm = _ap(w).name
                for d in self.inh.get(nm, ()):
                    deps.append((d, False))
        for x in list(reads) + list(writes):
            if isinstance(x, tuple):
                nm = _ap(x).name
                sk = self.subkeys.setdefault(nm, set())
                for k in _keys(x):
                    sk.add(k)
        for k in rk:
            w = self.lastw.get(k)
            if w is not None:
                deps.append((w, True))
        for k in wk:
            w = self.lastw.get(k)
            if w is not None:
                deps.append((w, False))
            for r in self.readers.get(k, ()):
                deps.append((r, False))
        need_e, need_d = {}, {}
        for d, raw in deps:
            if d.dma:
                need_d[d.grp] = self.grp_cnt[d.grp]
            else:
                if d.eng == eng and not dma and (eng == "pe" or not raw):
                    continue
                cur = need_e.get(d.eng)
                if cur is None or d.pos > cur.pos:
                    need_e[d.eng] = d
        self.ops[eng].append(op)
        op.pos = len(self.ops[eng])
        if dma:
            self.grp_cnt[grp] = self.grp_cnt.get(grp, 0) + 1
            op.idx = self.grp_cnt[grp]
        wd = self.waited[eng]
        for g, cnt in need_d.items():
            if wd.get(("d", g), 0) < cnt:
                wd[("d", g)] = cnt
                op.waits.append(("d", g, cnt))
        for e2, d in need_e.items():
            if wd.get(("e", e2), 0) < d.pos:
                wd[("e", e2)] = d.pos
                d.needs = True
                op.waits.append(("e", e2, d))
        for k in rk:
            self.readers.setdefault(k, []).append(op)
        for k in wk:
            self.lastw[k] = op
            self.readers[k] = []
        return op

    def emit(self):
        nc = self.nc
        with contextlib.ExitStack() as es:
            esems = {}
            nalloc = 0
            for e in self.ENGS:
                n = 0
                for op in self.ops[e]:
                    if not op.dma and op.needs:
                        n += 1
                        op.sig = n
                nsem = (n + SEM_LIM - 1) // SEM_LIM
                esems[e] = [es.enter_context(nc.semaphore(f"pg_{e}_{i}")) for i in range(nsem)]
                nalloc += nsem
            gsems = {}
            for g in self.grp_cnt:
                gsems[g] = es.enter_context(nc.semaphore(f"dg_{len(gsems)}"))
                nalloc += 1
            assert nalloc <= 100, f"too many semaphores {nalloc}"
            block = es.enter_context(nc.Block())

            def run(engname):
                def body(eng):
                    for op in self.ops[engname]:
                        for w in op.waits:
                            if w[0] == "d":
                                eng.wait_ge(gsems[w[1]], self.grp_unit[w[1]] * w[2])
                            else:
                                s = w[2].sig - 1
                                eng.wait_ge(esems[w[1]][s // SEM_LIM], s % SEM_LIM + 1)
                        ins = op.fn(eng)
                        if op.dma:
                            ins.then_inc(gsems[op.grp], self.grp_unit[op.grp])
                        elif op.sig is not None:
                            s = op.sig - 1
                            ins.then_inc(esems[engname][s // SEM_LIM], 1)
                    for g in sorted({op.grp for op in self.ops[engname] if op.dma}, key=str):
                        eng.wait_ge(gsems[g], self.grp_unit[g] * self.grp_cnt[g])
                return body

            if self.ops["sp"]:
                block.sync(run("sp"))
            if self.ops["act"]:
                block.scalar(run("act"))
            if self.ops["dve"]:
                block.vector(run("dve"))
            if self.ops["pool"]:
                block.gpsimd(run("pool"))
            if self.ops["pe"]:
                block.tensor(run("pe"))


class Ops:
    def __init__(self, P):
        self.P = P

    def mm(self, out, lhsT, rhs, start=True, stop=True, **kw):
        o, l, r = _ap(out), _ap(lhsT), _ap(rhs)
        return self.P.add("pe", lambda e: e.matmul(o, l, r, start=start, stop=stop, **kw), [lhsT, rhs], [out])

    def tr(self, out, in_, ident):
        o, i, d = _ap(out), _ap(in_), _ap(ident)
        return self.P.add("pe", lambda e: e.transpose(o, i, d), [in_, ident], [out])

    def act(self, out, in_, func, bias=0.0, scale=1.0, accum_out=None):
        o, i, b, s, a = _ap(out), _ap(in_), _ap(bias), _ap(scale), _ap(accum_out)
        kw = {}
        if a is not None:
            kw["accum_out"] = a
        return self.P.add("act", lambda e: e.activation(o, i, func, bias=b, scale=s, **kw),
                          [in_, bias, scale], [out, accum_out])

    def tt(self, eng, out, in0, in1, op):
        o, a, b = _ap(out), _ap(in0), _ap(in1)
        return self.P.add(eng, lambda e: e.tensor_tensor(o, a, b, op), [in0, in1], [out])

    def ts(self, eng, out, in0, s1, s2, op0, op1=None):
        o, a, x1, x2 = _ap(out), _ap(in0), _ap(s1), _ap(s2)
        kw = {}
        if op1 is not None:
            kw["op1"] = op1
        return self.P.add(eng, lambda e: e.tensor_scalar(o, a, x1, x2, op0, **kw), [in0, s1, s2], [out])

    def stt(self, eng, out, in0, scalar, in1, op0, op1):
        o, a, s, b = _ap(out), _ap(in0), _ap(scalar), _ap(in1)
        return self.P.add(eng, lambda e: e.scalar_tensor_tensor(o, a, s, b, op0, op1), [in0, scalar, in1], [out])

    def cp(self, eng, out, in_):
        o, i = _ap(out), _ap(in_)
        if eng == "act":
            return self.P.add(eng, lambda e: e.copy(o, i), [in_], [out])
        return self.P.add(eng, lambda e: e.tensor_copy(o, i), [in_], [out])

    def red(self, eng, out, in_, op):
        o, i = _ap(out), _ap(in_)
        return self.P.add(eng, lambda e: e.tensor_reduce(o, i, AX.X, op), [in_], [out])

    def memset(self, eng, out, val):
        o = _ap(out)
        return self.P.add(eng, lambda e: e.memset(o, val), [], [out])

    def recip(self, out, in_, eng="dve"):
        o, i = _ap(out), _ap(in_)
        return self.P.add(eng, lambda e: e.reciprocal(o, i), [in_], [out])

    def gather(self, out, src, idx):
        o, i, x = _ap(out), _ap(src), _ap(idx)
        grp = _keys(out)[0]
        return self.P.add("pool", lambda e: e.indirect_dma_start(
            out=o, out_offset=None, in_=i, in_offset=bass.IndirectOffsetOnAxis(ap=x, axis=0)),
            [src, idx], [out], dma=True, grp=grp)

    def allgather(self, out, in_, groups):
        o, i = _ap(out), _ap(in_)
        return self.P.add("pool", lambda e: e.collective_compute(
            "AllGather", ALU.bypass, replica_groups=groups, ins=[i.opt()], outs=[o.opt()]),
            [in_], [out], dma=True, grp="cc", unit=1)

    def dma(self, eng, out, in_, grp=None):
        o, i = _ap(out), _ap(in_)
        if grp is None:
            grp = _keys(out)[0] if str(o.space).endswith("SB") else _keys(in_)[0]
        return self.P.add(eng, lambda e: e.dma_start(out=o, in_=i), [in_], [out], dma=True, grp=grp)


def _consts():
    ident = np.eye(128, dtype=np.float32)
    perm = np.zeros((128, 128), np.float32)
    for p in range(128):
        d = p % 64
        if d < 32:
            perm[p + 32, p] = -1.0
        else:
            perm[p - 32, p] = 1.0
    nf = 16
    freqs = (np.float32(10000.0) ** (-np.arange(nf, dtype=np.float32) / np.float32(nf))).astype(np.float32)
    rows = np.repeat(np.arange(SEQ // 64), 64).astype(np.float32)
    cols = np.tile(np.arange(64), SEQ // 64).astype(np.float32)
    ang = np.concatenate([rows[:, None] * freqs, cols[:, None] * freqs], axis=-1).astype(np.float32)
    cosT = np.ascontiguousarray(np.tile(np.cos(ang).astype(np.float32).T, (4, 1)))
    sinT = np.ascontiguousarray(np.tile(np.sin(ang).astype(np.float32).T, (4, 1)))
    ii = np.arange(128)[:, None]
    jj = np.arange(128)[None, :]
    m_prev = (ii >= jj).astype(np.float32)
    m_next = (ii <= jj).astype(np.float32)
    tri = (ii <= jj).astype(np.float32)
    nm_f = np.where(ii > jj, -30000.0, 0.0).astype(np.float32)
    nm_b = np.where(ii < jj, -30000.0, 0.0).astype(np.float32)
    triT = (ii >= jj).astype(np.float32)
    esel = np.zeros((32, 32, 128), np.float32)
    for k in range(32):
        esel[k, k, :] = 1.0
    cmat = np.concatenate([ident, perm, np.ones((128, 128), np.float32), m_prev, m_next, tri, nm_f, nm_b], axis=1)
    return {
        "cbf": cmat.astype(ml_dtypes.bfloat16),
        "cf32": np.concatenate([ident, np.ones((128, 128), np.float32), tri, triT, ident[::-1].copy()], axis=1),
        "esel": esel.reshape(32, 4096).astype(ml_dtypes.bfloat16),
        "cosT": cosT, "sinT": sinT,
    }


class Layer:
    def __init__(self, li, parent):
        self.li = li
        self.kind = li % 3
        self.want_ctx = li < DEPTH - 1
        self.last = li == NLAYERS - 1
        self.nc, self.P, self.O = parent.nc, parent.P, parent.O
        self.ins = parent.ins
        self.pre = f"L{li}_"
        self.cbf, self.cf32 = parent.cbf, parent.cf32
        self.src_seq, self.src_ctx, self.src_rev = parent.src_seq, parent.src_ctx, parent.src_rev
        self.hfm = parent.hfm
        self.build()

    def din(self, name, shape, dt=F32):
        name = self.pre + name
        t = self.nc.dram_tensor(name, list(shape), dt, kind="ExternalInput").ap()
        self.ins[name] = t
        return t

    def dout(self, name, shape, dt=F32):
        return self.nc.dram_tensor(name, list(shape), dt, kind="ExternalOutput").ap()

    def dscr(self, name, shape, dt):
        return self.nc.dram_tensor(self.pre + name, list(shape), dt, kind="Internal").ap()

    def load_tile(self, xt, which, t):
        O = self.O
        if which == "ctx":
            O.dma("sp", xt[:], (self.src_ctx[0][t * 128:(t + 1) * 128, :], self.src_ctx[1]))
            return
        if which == "halo":
            if t == 0:
                O.dma("sp", xt[:], (self.src_seq[0][0:128, :], self.src_seq[1]))
                return
            t = HALF // 128
        if self.src_rev is None or t < HALF // 128:
            O.dma("sp", xt[:], (self.src_seq[0][t * 128:(t + 1) * 128, :], self.src_seq[1]))
            return
        j = t - HALF // 128
        rv = self.src_rev[j // 4]
        o = (j % 4) * 128
        O.dma("sp", self.xa[:], (rv[0][512 + o:512 + o + 128, :], rv[1]))
        O.dma("sp", self.xb[:], (rv[0][o:o + 128, :], rv[1]))
        O.act(xt[:], self.xa[:], AF.Copy, scale=self.hfm[:, 0:1])
        O.stt("dve", xt[:], self.xb[:], self.hfm[:, 1:2], xt[:], ALU.mult, ALU.add)

    def build(self):
        P, O = self.P, self.O
        mL = P.mark()
        if self.src_rev is not None:
            self.xa = P.sb("blend_a", [128, D], F32, hi=True)
            self.xb = P.sb("blend_b", [128, D], F32, hi=True)
        self.vecT = self.din("vecT", [128, 96])
        self.rowv = self.din("rowv", [4, D])
        self.ada_w = self.din("ada_w", [D, 6 * D])
        self.ffn_w_in = self.din("ffn_w_in", [D, 2 * FFN])
        self.ffn_w_out = self.din("ffn_w_out", [FFN, D])
        if self.last:
            self.hout = (self.dout("hout", [HALF, D]), "hout")
        else:
            self.hout = (self.dscr("hown", [HALF, D], F32), self.pre + "hown")
            self.hrev = [(self.dscr(f"hrev{c}", [512, D], F32), self.pre + f"hrev{c}") for c in range(8)]
        if self.want_ctx:
            self.hctx_out = (self.dscr("hctxo", [CTX, D], F32), self.pre + "hctxo")
        self.hmid = self.dscr("hmid", [HALF + CTX, D], F32)
        self.gates = self.dscr("gates", [4, 128, D], F32)
        self.ident = self.cbf[:, 0:128]
        self.perm = self.cbf[:, 128:256]
        self.ones_bf = self.cbf[:, 256:384]
        self.ones_f = self.cf32[:, 128:256]
        self.mods = P.sb("mods", [128, 8, 8], F32)
        self.stat = P.sb("stat", [128, 64], F32)
        self.stat_i = 0

        self.phase_adaln()
        if self.kind == 0:
            self.da_mixer()
        elif self.kind == 1:
            self.sw_mixer()
        else:
            self.ssd_mixer()
        P.release_hi()
        self.phase_ffn()
        P.release(mL)

    def newstat(self, n=1):
        if self.stat_i + n > 64:
            self.stat_i = 0
        s = self.stat[:, self.stat_i:self.stat_i + n]
        key = f"stat{self.stat_i}"
        self.stat_i += n
        return (s, key)

    def rstd_from_ss(self, ss, n):
        O = self.O
        sd = self.newstat()
        O.act(sd, ss, AF.Sqrt, bias=self.eps_t, scale=1.0 / n)
        rs = self.newstat()
        O.recip(rs, sd)
        return rs

    def phase_adaln(self):
        P, O = self.P, self.O
        eps_t = P.sb("eps_t", [128, 1], F32)
        O.memset("dve", eps_t[:], EPS)
        self.eps_t = eps_t[:, 0:1]
        m = P.mark()
        vec = P.sb("vec", [128, 96], F32)
        O.dma("sp", vec[:], self.vecT)
        act16 = P.sb("act16", [128, 16], BF16)
        O.act(act16[:], vec[:, 80:96], AF.Silu)
        actrep = P.sb("actrep", [128, 16, 128], BF16)
        for j in range(16):
            O.cp("dve", (actrep[:, j, :], f"actrep{j}"), act16[:, j:j + 1].to_broadcast([128, 128]))
        bb = [P.sb(f"bb{i}", [128, D], F32) for i in range(2)]
        gb = [P.sb(f"gb{i}", [128, D], F32) for i in range(2)]
        for i in range(2):
            O.dma("sp", bb[i][:], self.rowv[i:i + 1, :].broadcast_to([128, D]))
            O.dma("sp", gb[i][:], self.rowv[2 + i:3 + i, :].broadcast_to([128, D]))
        wm = [P.sb(f"adaw{i}", [128, 8, D], BF16) for i in range(2)]
        gt = [P.sb(f"gatet{i}", [128, D], F32) for i in range(2)]
        pm, pmk = P.pb(0)
        pm = pm.rearrange("p (s t) -> p s t", t=2)
        slot = 0
        gi = 0
        aw = self.ada_w.rearrange("(k p) n -> p k n", p=128)
        for mod in range(6):
            w = wm[mod % 2]
            O.dma("pool", w[:], aw[:, :, mod * D:(mod + 1) * D])
            if mod in (0, 1, 3, 4):
                for c in range(8):
                    for k in range(8):
                        O.mm((pm[:, slot, :], pmk), w[:, k, c * 128:(c + 1) * 128], act16[:, k::8],
                             start=(k == 0), stop=(k == 7))
                    slot += 1
            else:
                which = 0 if mod == 2 else 1
                for lc in range(2):
                    g = gt[gi % 2]
                    for cb in range(2):
                        pg = P.pb(1 + cb)
                        for k in range(8):
                            O.mm(pg, (actrep[:, lc * 8 + k, :], f"actrep{lc * 8 + k}"), w[:, k, cb * 512:(cb + 1) * 512],
                                 start=(k == 0), stop=(k == 7))
                        O.tt("dve", g[:, cb * 512:(cb + 1) * 512], pg, bb[which][:, cb * 512:(cb + 1) * 512], ALU.add)
                    O.tt("dve", g[:], g[:], gb[which][:], ALU.mult)
                    O.dma("sp", (self.gates[which * 2 + lc], f"gates{which * 2 + lc}"), g[:])
                    gi += 1
        mods = self.mods
        pmv = P.pb(0)[0].rearrange("p (s t) -> p s t", t=2)
        for lc in range(2):
            base = lc * 4
            O.tt("dve", mods[:, base + 0, :], (pmv[:, 0:8, lc], pmk), vec[:, 0:8], ALU.add)
            O.tt("dve", mods[:, base + 1, :], (pmv[:, 8:16, lc], pmk), vec[:, 8:16], ALU.add)
            O.stt("dve", mods[:, base + 1, :], mods[:, base + 1, :], 1.0, vec[:, 48:56], ALU.add, ALU.mult)
            O.tt("dve", mods[:, base + 2, :], (pmv[:, 16:24, lc], pmk), vec[:, 24:32], ALU.add)
            O.tt("dve", mods[:, base + 3, :], (pmv[:, 24:32, lc], pmk), vec[:, 32:40], ALU.add)
            O.stt("dve", mods[:, base + 3, :], mods[:, base + 3, :], 1.0, vec[:, 64:72], ALU.add, ALU.mult)
        P.release(m)

    def norm_uT(self, xt, xh, uT_slice_fn, mod_base, tb, ei):
        P, O = self.P, self.O
        ss = self.newstat()
        O.act(xh[:], xt[:], AF.Square, accum_out=ss)
        rs = self.rstd_from_ss(ss, D)
        O.ts("dve", xh[:], xt[:], rs, None, ALU.mult)
        pT = P.pb(tb, 1, BF16)
        pTv = pT[0].rearrange("p (k t) -> p k t", k=8)
        for k in range(8):
            O.tr((pTv[:, k, :], pT[1]), xh[:, k * 128:(k + 1) * 128], self.ident)
        S = self.mods[:, mod_base, :]
        G = self.mods[:, mod_base + 1, :]
        for k in range(8):
            dst = uT_slice_fn(k)
            if (k + ei) % 2 == 0:
                O.act(dst, (pTv[:, k, :], pT[1]), AF.Identity, bias=S[:, k:k + 1], scale=G[:, k:k + 1])
            else:
                O.ts("dve", dst, (pTv[:, k, :], pT[1]), G[:, k:k + 1], S[:, k:k + 1], ALU.mult, ALU.add)

    def da_mixer(self):
        P, O = self.P, self.O
        li = self.li
        lam_init = 0.8 - 0.6 * math.exp(-0.3 * li)
        w_qkv = self.din("w_qkv", [D, 3 * D])
        w_o = self.din("w_o", [D, D])
        lamv = self.din("lamv", [1, 256])
        subln = self.din("subln", [128, 1])
        cosT = self.din("cosT", [128, SEQ])
        sinT = self.din("sinT", [128, SEQ])
        NK = SEQ + CTX
        QT = self.dscr("QT", [8, 128, HALF], BF16)
        QTc = self.dscr("QTc", [8, 128, CTX], BF16)
        KT = self.dscr("KT", [8, 128, NK], BF16)
        Vs = self.dscr("Vs", [NK, D], BF16)

        lt = P.sb("lamt", [128, 256], F32)
        O.dma("sp", lt[:], lamv.broadcast_to([128, 256]))
        lp = P.sb("lamp", [128, 128], F32)
        O.tt("dve", lp[:, 0:64], lt[:, 0:64], lt[:, 64:128], ALU.mult)
        O.tt("dve", lp[:, 64:128], lt[:, 128:192], lt[:, 192:256], ALU.mult)
        lsc = P.sb("lsc", [128, 8], F32)
        O.red("dve", lsc[:, 0:1], lp[:, 0:64], ALU.add)
        O.red("dve", lsc[:, 1:2], lp[:, 64:128], ALU.add)
        O.act(lsc[:, 2:4], lsc[:, 0:2], AF.Exp)
        O.stt("dve", lsc[:, 4:5], lsc[:, 3:4], -lam_init, lsc[:, 2:3], ALU.add, ALU.subtract)
        neg_lam = lsc[:, 4:5]
        sg = P.sb("sublng", [128, 2], F32)
        O.dma("sp", sg[:, 0:1], subln)
        O.ts("dve", sg[:, 1:2], sg[:, 0:1], 1.0 - lam_init, None, ALU.mult)
        subg = sg[:, 1:2]

        m1 = P.mark()
        wq = P.sb("wqkv", [128, 8, 3 * D], BF16)
        wv = w_qkv.rearrange("(k p) n -> p k n", p=128)
        O.dma("pool", (wq[:], ["wqkv0", "wqkv1", "wqkv2"]), wv)
        xts = [P.sb(f"xt{i}", [128, D], F32) for i in range(3)]
        xhs = [P.sb(f"xh{i}", [128, D], BF16) for i in range(2)]
        uTs = [P.sb(f"uT{i}", [128, 8, 512], BF16) for i in range(2)]
        cs = [P.sb(f"cos{i}", [128, 512], F32) for i in range(2)]
        sn = [P.sb(f"sin{i}", [128, 512], F32) for i in range(2)]
        kb = [P.sb(f"kb{i}", [128, 512], BF16) for i in range(2)]
        t1 = [P.sb(f"ropt1_{i}", [128, 512], F32) for i in range(2)]
        t2 = [P.sb(f"ropt2_{i}", [128, 512], F32) for i in range(2)]
        kst = [P.sb(f"kst{i}", [128, 8, 512], BF16) for i in range(2)]
        qst = [P.sb(f"qst{i}", [128, 8, 512], BF16) for i in range(2)]
        vst = [P.sb(f"vst{i}", [128, D], BF16) for i in range(2)]

        groups = [("ctx", 0, CTX, 0 if self.want_ctx else None)]
        for g in range(SEQ // 512):
            groups.append(("lat", g * 512, 512, g * 512 if g < HALF // 512 else None))
        self._cnt = 0
        self._pbank = 0

        def qk_proj(gi, ub, ntok, rope, colbase, stage, dstT, dcol0):
            for h in range(8):
                pk = P.pb(2 + self._pbank % 4)
                self._pbank += 1
                pkv = (pk[0][:, :ntok], pk[1])
                for k in range(8):
                    O.mm(pkv, (wq[:, k, colbase + h * 128:colbase + (h + 1) * 128], f"wqkv{colbase // D}"),
                         ub[:, k, :ntok], start=(k == 0), stop=(k == 7))
                if not rope:
                    O.cp("act", stage[:, h, :ntok], pkv)
                    continue
                kbt = kb[h % 2]
                O.cp("act", kbt[:, :ntok], pkv)
                psw = P.pb(6 + h % 2)
                pswv = (psw[0][:, :ntok], psw[1])
                O.mm(pswv, self.perm, kbt[:, :ntok])
                O.tt("dve", t1[h % 2][:, :ntok], kbt[:, :ntok], cs[gi % 2][:, :ntok], ALU.mult)
                O.tt("dve", t2[h % 2][:, :ntok], pswv, sn[gi % 2][:, :ntok], ALU.mult)
                O.tt("pool", stage[:, h, :ntok], t1[h % 2][:, :ntok], t2[h % 2][:, :ntok], ALU.add)
            O.dma("sp", (_ap(dstT)[:, :, dcol0:dcol0 + ntok].rearrange("h p t -> p h t"), dstT[1]), stage[:, :, :ntok])

        for gi, (kind, r0, ntok, own_q0) in enumerate(groups):
            ub = uTs[gi % 2]
            nt = ntok // 128
            for t in range(nt):
                c = self._cnt
                self._cnt += 1
                xt = xts[c % 3]
                xh = xhs[c % 2]
                self.load_tile(xt, "ctx" if kind == "ctx" else "seq", r0 // 128 + t)
                self.norm_uT(xt, xh, lambda k, ub=ub, t=t: ub[:, k, t * 128:(t + 1) * 128],
                             4 if kind == "ctx" else 0, c % 2, c)
            rope = kind == "lat"
            if rope:
                O.dma("sp", cs[gi % 2][:, :ntok], cosT[:, r0:r0 + ntok])
                O.dma("sp", sn[gi % 2][:, :ntok], sinT[:, r0:r0 + ntok])
            kcol0 = 0 if kind == "ctx" else CTX + r0
            qk_proj(gi, ub, ntok, rope, D, kst[gi % 2], (KT, f"KT{gi}"), kcol0)
            if own_q0 is not None:
                if kind == "ctx":
                    qk_proj(gi, ub, ntok, rope, 0, qst[gi % 2], (QTc, "QTc"), 0)
                else:
                    qk_proj(gi, ub, ntok, rope, 0, qst[gi % 2], (QT, f"QT{own_q0 // 512}"), own_q0)
            for t in range(nt):
                vt = vst[t % 2]
                for cb in range(2):
                    pv = P.pb(2 + self._pbank % 4)
                    self._pbank += 1
                    for k in range(8):
                        O.mm(pv, ub[:, k, t * 128:(t + 1) * 128],
                             (wq[:, k, 2 * D + cb * 512:2 * D + (cb + 1) * 512], "wqkv2"),
                             start=(k == 0), stop=(k == 7))
                    O.cp("act" if cb == 0 else "dve", vt[:, cb * 512:(cb + 1) * 512], pv)
                O.dma("sp", (Vs[kcol0 + t * 128:kcol0 + (t + 1) * 128, :], f"Vs{gi}"), vt[:])
        P.release(m1)
        P.release_hi()
        ngrp = len(groups)

        m2 = P.mark()
        onT = P.sb("onT", [128, 8, HALF], BF16)
        onTc = P.sb("onTc", [128, 8, CTX], BF16)
        ktb = [P.sb(f"ktb{i}", [128, NK], BF16) for i in range(2)]
        vtb = [P.sb(f"vtb{i}", [128, NK // 128, 128], BF16) for i in range(2)]
        qtb = [P.sb(f"qtb{i}", [128, 512], BF16) for i in range(3)]
        ptb = [P.sb(f"ptb{i}", [128, 2, 512], BF16) for i in range(3)]
        fw = [P.sb(f"fin{i}", [128, 512], F32) for i in range(4)]
        kt_keys = [f"KT{gi}" for gi in range(ngrp)]
        vs_keys = [f"Vs{gi}" for gi in range(ngrp)]
        nkt = NK // 128
        qcount = 0
        for h in range(8):
            kt_sb = ktb[h % 2]
            v_sb = vtb[h % 2]
            O.dma("sp", kt_sb[:], (KT[h], kt_keys))
            O.dma("sp", v_sb[:], (Vs[:, h * 128:(h + 1) * 128].rearrange("(t p) v -> p t v", p=128), vs_keys))
            qgroups = [("lat", q0, 512, 0, nkt) for q0 in range(0, HALF, 512)]
            if self.want_ctx:
                qgroups.append(("ctx", 0, CTX, 0, CTX // 128))
            for (qk, q0, nq, kt0, kt1) in qgroups:
                qt = qtb[qcount % 3]
                qcount += 1
                if qk == "lat":
                    O.dma("sp", qt[:, :nq], (QT[h, :, q0:q0 + nq], f"QT{q0 // 512}"))
                else:
                    O.dma("sp", qt[:, :nq], (QTc[h], "QTc"))
                po = P.pb(4, 2)
                pd = P.pb(6, 2)
                for kt in range(kt0, kt1):
                    psn = P.pb(2 * (kt % 2), 2)
                    pt = ptb[kt % 3]
                    for mp in range(2):
                        O.mm((psn[0][:, mp, :nq], psn[1]), kt_sb[mp * 64:(mp + 1) * 64, kt * 128:(kt + 1) * 128],
                             qt[mp * 64:(mp + 1) * 64, :nq])
                    O.act(pt[:, :, :nq], (psn[0][:, :, :nq], psn[1]), AF.Exp, scale=DA_SCALE)
                    for mp in range(2):
                        O.mm((po[0][:, mp, :nq], po[1]), v_sb[:, kt, :], pt[:, mp, :nq],
                             start=(kt == kt0), stop=(kt == kt1 - 1))
                    for mp in range(2):
                        O.mm((pd[0][:, mp, :nq], pd[1]), self.ones_bf, pt[:, mp, :nq],
                             start=(kt == kt0), stop=(kt == kt1 - 1))
                r0t, w0t, w1t, ot = fw
                O.recip(r0t[:, :nq], (pd[0][:, 0, :nq], pd[1]))
                O.tt("dve", w0t[:, :nq], (po[0][:, 0, :nq], po[1]), r0t[:, :nq], ALU.mult)
                O.recip(r0t[:, :nq], (pd[0][:, 1, :nq], pd[1]))
                O.tt("dve", w1t[:, :nq], (po[0][:, 1, :nq], po[1]), r0t[:, :nq], ALU.mult)
                O.stt("dve", ot[:, :nq], w1t[:, :nq], neg_lam, w0t[:, :nq], ALU.mult, ALU.add)
                O.act(w0t[:, :nq], ot[:, :nq], AF.Square)
                pss = P.pb(0)
                O.mm((pss[0][:, :nq], pss[1]), self.ones_f, w0t[:, :nq])
                O.act(w1t[:, :nq], (pss[0][:, :nq], pss[1]), AF.Sqrt, bias=self.eps_t, scale=1.0 / 128)
                O.recip(r0t[:, :nq], w1t[:, :nq])
                O.tt("dve", ot[:, :nq], ot[:, :nq], r0t[:, :nq], ALU.mult)
                if qk == "lat":
                    dst = (onT[:, h, q0:q0 + nq], f"onT{h}")
                else:
                    dst = (onTc[:, h, :], f"onTc{h}")
                O.ts("dve", dst, ot[:, :nq], subg, None, ALU.mult)
        wo = P.sb("wo", [128, 8, D], BF16)
        O.dma("pool", wo[:], w_o.rearrange("(k p) n -> p k n", p=128))
        self.resid_phase(lambda h, t: (onT[:, h, t * 128:(t + 1) * 128], f"onT{h}"),
                         lambda h, t: (onTc[:, h, t * 128:(t + 1) * 128], f"onTc{h}"),
                         lambda h: wo[:, h, :], 8)
        P.release(m2)

    def resid_setup(self):
        P, O = self.P, self.O
        R = {}
        R["g1"] = [P.sb(f"g1_{i}", [128, D], F32) for i in range(2)]
        O.dma("sp", R["g1"][0][:], (self.gates[0], "gates0"))
        O.dma("sp", R["g1"][1][:], (self.gates[1], "gates1"))
        R["xts"] = [P.sb(f"rxt{i}", [128, D], F32) for i in range(3)]
        R["tmp"] = [P.sb(f"rtmp{i}", [128, D], F32) for i in range(2)]
        R["junk"] = P.sb("rjunk", [128, D], BF16)
        R["i"] = 0
        return R

    def resid_tile(self, R, kind, t, lhsT_fn, w_rows, nchunk, banks=(0, 2)):
        P, O = self.P, self.O
        i = R["i"]
        R["i"] += 1
        po = P.pb(banks[i % 2], 2)
        for cb in range(2):
            for h in range(nchunk):
                O.mm((po[0][:, cb, :], po[1]), lhsT_fn(h), _ap(w_rows(h))[:, cb * 512:(cb + 1) * 512],
                     start=(h == 0), stop=(h == nchunk - 1))
        xt = R["xts"][i % 3]
        self.load_tile(xt, "seq" if kind == "lat" else "ctx", t)
        ss = self.newstat()
        O.act(R["junk"][:].rearrange("p (a b) -> p a b", a=2), po, AF.Square, accum_out=ss)
        rs = self.rstd_from_ss(ss, D)
        tm = R["tmp"][i % 2]
        O.stt("dve", tm[:].rearrange("p (a b) -> p a b", a=2), po, rs,
              R["g1"][0 if kind == "lat" else 1][:].rearrange("p (a b) -> p a b", a=2), ALU.mult, ALU.mult)
        O.tt("pool", tm[:], tm[:], xt[:], ALU.add)
        row0 = t * 128 if kind == "lat" else HALF + t * 128
        O.dma("sp", (self.hmid[row0:row0 + 128, :], f"hmid{row0 // 512}"), tm[:])

    def resid_phase(self, lat_lhsT, ctx_lhsT, w_rows, nchunk):
        P = self.P
        m = P.mark()
        R = self.resid_setup()
        tiles = [("lat", t) for t in range(HALF // 128)]
        if self.want_ctx:
            tiles += [("ctx", t) for t in range(CTX // 128)]
        for (kind, t) in tiles:
            fn = lat_lhsT if kind == "lat" else ctx_lhsT
            self.resid_tile(R, kind, t, lambda h, fn=fn, t=t: fn(h, t), w_rows, nchunk)
        P.release(m)

    def sw_mixer(self):
        P, O = self.P, self.O
        w_qkv = self.din("w_qkv", [D, 1536])
        w_o = self.din("w_o", [D, D])
        sinkv = self.din("sinkv", [2, 8])
        emask_d = self.din("emask", [128, 2])
        NW = HALF + 256
        cosT = self.din("cosT", [128, NW])
        sinT = self.din("sinT", [128, NW])
        NKW = CTX + NW
        se = P.sb("sinkexp", [128, 8], F32)
        O.dma("sp", se[0:64, :], sinkv[0:1, :].broadcast_to([64, 8]))
        O.dma("sp", se[64:128, :], sinkv[1:2, :].broadcast_to([64, 8]))
        O.act(se[:], se[:], AF.Exp)
        emask = P.sb("emask_sb", [128, 2], F32)
        O.dma("sp", emask[:], emask_d)
        m_prev = self.cbf[:, 384:512]
        m_next = self.cbf[:, 512:640]
        m2 = P.mark()
        m3 = P.mark()
        QTs = P.sb("QTs", [128, 8, HALF + CTX], BF16)
        KTs = P.sb("KTs", [128, 4, NKW], BF16)
        Vs = P.sb("Vsw", [128, NKW // 128, 256], BF16)
        m1 = P.mark()
        wq = P.sb("wqkv", [128, 8, 1536], BF16)
        O.dma("pool", wq[:], w_qkv.rearrange("(k p) n -> p k n", p=128))
        wkd = P.sb("wkdup", [128, 8, 4, 128], BF16)
        for hk in range(4):
            for dup in range(2):
                O.cp("dve" if dup == 0 else "pool", (wkd[:, :, hk, dup * 64:(dup + 1) * 64], f"wkd{hk}"),
                     wq[:, :, D + hk * 64:D + (hk + 1) * 64])
        xts = [P.sb(f"xt{i}", [128, D], F32) for i in range(2)]
        xhs = [P.sb(f"xh{i}", [128, D], BF16) for i in range(2)]
        uTs = [P.sb(f"uT{i}", [128, 8, 512], BF16) for i in range(1)]
        cs = [P.sb(f"cos{i}", [128, 512], F32) for i in range(2)]
        sn = [P.sb(f"sin{i}", [128, 512], F32) for i in range(2)]
        kb = [P.sb(f"kb{i}", [128, 512], BF16) for i in range(2)]
        t1 = [P.sb(f"ropt1_{i}", [128, 512], F32) for i in range(2)]
        t2 = [P.sb(f"ropt2_{i}", [128, 512], F32) for i in range(2)]
        groups = [("ctx", "ctx", 0, CTX, None, 0, HALF)]
        for g in range(HALF // 512):
            groups.append(("lat", "seq", g * 512, 512, g * 512, CTX + 128 + g * 512, g * 512))
        groups.append(("lat", "halo", 0, 128, HALF, CTX, None))
        groups.append(("lat", "halo", 128, 128, HALF + 128, CTX + 128 + HALF, None))
        cnt = 0
        pbank = 0
        for gi, (kind, src, r0, ntok, rc0, kc0, qc0) in enumerate(groups):
            ub = uTs[0]
            for t in range(ntok // 128):
                xt = xts[cnt % 2]
                xh = xhs[cnt % 2]
                self.load_tile(xt, src, r0 // 128 + t)
                self.norm_uT(xt, xh, lambda k, ub=ub, t=t: ub[:, k, t * 128:(t + 1) * 128],
                             4 if kind == "ctx" else 0, cnt % 2, cnt)
                cnt += 1
            rope = kind == "lat"
            if rope:
                O.dma("sp", cs[gi % 2][:, :ntok], cosT[:, rc0:rc0 + ntok])
                O.dma("sp", sn[gi % 2][:, :ntok], sinT[:, rc0:rc0 + ntok])
            jobs = [("k", hk) for hk in range(4)]
            if qc0 is not None:
                jobs += [("q", c) for c in range(8)]
            for ji, (what, c) in enumerate(jobs):
                pk = P.pb(2 + pbank % 4)
                pbank += 1
                pkv = (pk[0][:, :ntok], pk[1])
                for k in range(8):
                    if what == "k":
                        lw = (wkd[:, k, c, :], f"wkd{c}")
                    else:
                        lw = wq[:, k, c * 128:(c + 1) * 128]
                    O.mm(pkv, lw, ub[:, k, :ntok], start=(k == 0), stop=(k == 7))
                if what == "k":
                    dst = (KTs[:, c, kc0:kc0 + ntok], f"KTs{gi}")
                else:
                    dst = (QTs[:, c, qc0:qc0 + ntok], f"QTs{gi}")
                if not rope:
                    O.cp("act", dst, pkv)
                    continue
                kbt = kb[ji % 2]
                O.cp("act", kbt[:, :ntok], pkv)
                psw = P.pb(6 + ji % 2)
                pswv = (psw[0][:, :ntok], psw[1])
                O.mm(pswv, self.perm, kbt[:, :ntok])
                O.tt("dve", t1[ji % 2][:, :ntok], kbt[:, :ntok], cs[gi % 2][:, :ntok], ALU.mult)
                O.tt("dve", t2[ji % 2][:, :ntok], pswv, sn[gi % 2][:, :ntok], ALU.mult)
                O.tt("pool", dst, t1[ji % 2][:, :ntok], t2[ji % 2][:, :ntok], ALU.add)
            for t in range(ntok // 128):
                pv = P.pb(2 + pbank % 4)
                pbank += 1
                pvv = (pv[0][:, :256], pv[1])
                for k in range(8):
                    O.mm(pvv, ub[:, k, t * 128:(t + 1) * 128], wq[:, k, D + 256:D + 512], start=(k == 0), stop=(k == 7))
                O.cp("act", (Vs[:, kc0 // 128 + t, :], f"Vsw{gi}"), pvv)
        P.release(m1)
        ngrp = len(groups)
        allk = [f"KTs{gi}" for gi in range(ngrp)]
        allv = [f"Vsw{gi}" for gi in range(ngrp)]
        allq = [f"QTs{gi}" for gi in range(ngrp) if groups[gi][6] is not None]

        P.release_hi()
        oT = P.sb("oTs", [128, 8, HALF + CTX], BF16, hi=True)
        ptb = [P.sb(f"ptb{i}", [128, 2, 256], BF16) for i in range(3)]
        dn = [P.sb(f"dn{i}", [128, 256], F32) for i in range(2)]
        qtiles = [("lat", t) for t in range(HALF // 128)] + [("ctx", t) for t in range(CTX // 128)]
        it = 0
        for (qk, t) in qtiles:
            if qk == "lat":
                ktl = [(0, None, None), (1, None, None), (2 + t, m_prev, 0 if t == 0 else None), (3 + t, None, None),
                       (4 + t, m_next, 1 if t == HALF // 128 - 1 else None)]
                qc = t * 128
            else:
                ktl = [(0, None, None), (1, None, None)]
                qc = HALF + t * 128
            for hk in range(4):
                po = P.pb(4 + 2 * (it % 2))
                pd = P.pb(5 + 2 * (it % 2))
                it += 1
                for ki, (kt, msk, em) in enumerate(ktl):
                    pt = ptb[ki % 3]
                    first, last = ki == 0, ki == len(ktl) - 1
                    pss = P.pb(2 * (ki % 2), 2)
                    for par in range(2):
                        O.mm((pss[0][:, par, 0:256], pss[1]),
                             (KTs[par * 64:(par + 1) * 64, hk, kt * 128:(kt + 1) * 128], allk),
                             (QTs[par * 64:(par + 1) * 64, hk * 2:hk * 2 + 2, qc:qc + 128], allq))
                    O.act(pt[:], (pss[0][:, :, 0:256], pss[1]), AF.Exp, scale=DA_SCALE)
                    if msk is not None:
                        ptv = pt[:].rearrange("p a (g q) -> p (a g) q", g=2)
                        O.tt("dve", ptv, ptv, msk.unsqueeze(1).to_broadcast([128, 4, 128]), ALU.mult)
                        if em is not None:
                            O.ts("dve", pt[:], pt[:], emask[:, em:em + 1], None, ALU.mult)
                    for par in range(2):
                        O.mm((po[0][par * 64:(par + 1) * 64, 0:256], po[1]),
                             (Vs[:, kt, hk * 64:(hk + 1) * 64], allv), pt[:, par, :], start=first, stop=last)
                    for par in range(2):
                        O.mm((pd[0][par * 64:(par + 1) * 64, 0:256], pd[1]),
                             self.ones_bf[:, 0:64], pt[:, par, :], start=first, stop=last)
                d = dn[hk % 2]
                dv = d[:].rearrange("p (g q) -> p g q", g=2)
                O.tt("dve", dv, (pd[0][:, 0:256].rearrange("p (g q) -> p g q", g=2), pd[1]),
                     se[:, hk * 2:hk * 2 + 2].unsqueeze(2).to_broadcast([128, 2, 128]), ALU.add)
                O.recip(d[:], d[:])
                O.tt("dve", (oT[:, hk * 2:hk * 2 + 2, qc:qc + 128], f"oT{qk}{t}"),
                     (po[0][:, 0:256].rearrange("p (g q) -> p g q", g=2), po[1]), dv, ALU.mult)
        P.release(m3)
        wo = P.sb("wo", [128, 8, D], BF16)
        O.dma("pool", wo[:], w_o.rearrange("(k p) n -> p k n", p=128))
        self.resid_phase(lambda h, t: (oT[:, h, t * 128:(t + 1) * 128], f"oTlat{t}"),
                         lambda h, t: (oT[:, h, HALF + t * 128:HALF + (t + 1) * 128], f"oTctx{t}"),
                         lambda h: wo[:, h, :], 8)
        P.release(m2)
        P.release_hi()

    def ssd_mixer(self):
        P, O = self.P, self.O
        NT = SEQ + CTX
        w_in = self.din("w_in", [D, 5184])
        convw = self.din("convw", [128, 24, 5])
        convb = self.din("convb", [128, 24])
        ssdrow = self.din("ssdrow", [1, 160])
        normw = self.din("normw", [1, 2048])
        w_out = self.din("w_out", [2048, D])
        esel_d = self.din("esel", [32, 4096], BF16)
        xs_d = self.dscr("xs_d", [NT, 2048], BF16)
        bs_d = self.dscr("bs_d", [NT, 512], BF16)
        bt_d = self.dscr("bt_d", [128, 4, NT], BF16)
        ct_d = self.dscr("ct_d", [128, 4, NT], BF16)
        dt_d = self.dscr("dt_d", [NT, 64], F32)
        zs_d = self.dscr("zs_d", [NT, 2048], BF16)
        yf_d = self.dscr("yf_d", [NT, 2048], F32)
        nm = [self.cbf[:, 768:896], self.cbf[:, 896:1024]]
        tri = [self.cf32[:, 256:384], self.cf32[:, 384:512]]

        cw = P.sb("convw_sb", [128, 24, 5], F32)
        cbv = P.sb("convb_sb", [128, 24], F32)
        O.dma("sp", cw[:], convw)
        O.dma("sp", cbv[:], convb)
        row = P.sb("ssdrow_sb", [128, 160], F32)
        O.dma("sp", row[:], ssdrow.broadcast_to([128, 160]))
        Abc = P.sb("Abc", [128, 64], F32)
        O.act(Abc[:], row[:, 0:64], AF.Exp)
        O.ts("dve", Abc[:], Abc[:], -1.0, None, ALU.mult)
        dtb = row[:, 64:128]
        dskip = row[:, 128:160]

        mA = P.mark()
        win = P.sb("w_in_sb", [128, 8, 5184], BF16)
        wv = w_in.rearrange("(k p) n -> p k n", p=128)
        pieces = [(0, 1024), (1024, 2048), (2048, 3072), (3072, 4096), (4096, 5184)]

        def wkey(col):
            for i, (a, b) in enumerate(pieces):
                if a <= col < b:
                    return f"win{i}"
        O.dma("pool", (win[:], [f"win{i}" for i in range(len(pieces))]), wv)
        uw = [P.sb(f"uw{i}", [128, 8, 516], BF16) for i in range(3)]
        xts = [P.sb(f"xt{i}", [128, D], F32) for i in range(3)]
        xhs = [P.sb(f"xh{i}", [128, D], BF16) for i in range(2)]
        raw = [P.sb(f"raw{i}", [128, 516], F32) for i in range(2)]
        acc = [P.sb(f"cacc{i}", [128, 512], F32) for i in range(2)]
        ctmp = P.sb("ctmp", [128, 512], F32)
        cvo = P.sb("cvo", [128, 24, 512], BF16)
        xstg = [P.sb(f"xstg{i}", [128, 2048], BF16) for i in range(2)]
        bstg = [P.sb(f"bstg{i}", [128, 512], BF16) for i in range(2)]
        zstg = [P.sb(f"zstg{i}", [128, 2048], BF16) for i in range(2)]
        dstg = [P.sb(f"dstg{i}", [128, 64], F32) for i in range(2)]
        dtm = [P.sb(f"dtm{i}", [128, 64], F32) for i in range(2)]
        self._cnt = 0

        def S1(src, r0, ntok, buf, first, last, prevbuf, nextbuf):
            for t in range(ntok // 128):
                c = self._cnt
                self._cnt += 1
                xt = xts[c % 3]
                xh = xhs[c % 2]
                self.load_tile(xt, src, r0 // 128 + t)
                self.norm_uT(xt, xh, lambda k, t=t: buf[:, k, 2 + t * 128:2 + (t + 1) * 128],
                             4 if src == "ctx" else 0, 4 + c % 2, c)
            if first:
                O.memset("pool", buf[:, :, 0:2], 0.0)
            else:
                O.cp("pool", prevbuf[:, :, 514:516], buf[:, :, 2:4])
            if last:
                O.memset("pool", buf[:, :, 2 + ntok:4 + ntok], 0.0)
            else:
                O.cp("pool", nextbuf[:, :, 0:2], buf[:, :, ntok:ntok + 2])

        def S2(r0, ntok, buf, need_c):
            W = ntok + 4
            half = W // 2
            nch = 24 if need_c else 20
            for c in range(nch):
                col0 = 2048 + c * 128
                pr = P.pb(2 * (c % 2), 2)
                for hh in range(2):
                    for k in range(8):
                        O.mm((pr[0][:, hh, 0:half], pr[1]), (win[:, k, col0:col0 + 128], wkey(col0)),
                             buf[:, k, hh * half:(hh + 1) * half], start=(k == 0), stop=(k == 7))
                rw = raw[c % 2]
                O.cp("act", rw[:, 0:W].rearrange("p (a b) -> p a b", a=2), (pr[0][:, :, 0:half], pr[1]))
                ac = acc[c % 2]
                if c % 3 != 2:
                    O.ts("dve", ac[:, :ntok], rw[:, 0:ntok], cw[:, c, 0:1], cbv[:, c:c + 1], ALU.mult, ALU.add)
                    for j in range(1, 5):
                        O.stt("dve", ac[:, :ntok], rw[:, j:j + ntok], cw[:, c, j:j + 1], ac[:, :ntok], ALU.mult, ALU.add)
                else:
                    O.ts("pool", ac[:, :ntok], rw[:, 0:ntok], cw[:, c, 0:1], cbv[:, c:c + 1], ALU.mult, ALU.add)
                    for j in range(1, 5):
                        O.ts("pool", ctmp[:, :ntok], rw[:, j:j + ntok], cw[:, c, j:j + 1], None, ALU.mult)
                        O.tt("pool", ac[:, :ntok], ac[:, :ntok], ctmp[:, :ntok], ALU.add)
                O.act((cvo[:, c, :ntok], f"cvo{c}"), ac[:, :ntok], AF.Silu)
            if need_c:
                O.dma("sp", (bt_d[:, :, r0:r0 + ntok], f"bt{r0 // 512}"),
                      (cvo[:, 16:20, :ntok], [f"cvo{c}" for c in range(16, 20)]), grp="cvo_st")
                O.dma("sp", (ct_d[:, :, r0:r0 + ntok], f"ct{r0 // 512}"),
                      (cvo[:, 20:24, :ntok], [f"cvo{c}" for c in range(20, 24)]), grp="cvo_st")
            for t in range(ntok // 128):
                xst = xstg[t % 2]
                bst = bstg[t % 2]
                tb = [P.pb(4, 1, BF16), P.pb(5, 1, BF16), P.pb(6, 1, BF16)]
                tv = [(x[0].rearrange("p (k t) -> p k t", k=8), x[1]) for x in tb]
                for c in range(20):
                    O.tr((tv[c // 8][0][:, c % 8, :], tv[c // 8][1]), (cvo[:, c, t * 128:(t + 1) * 128], f"cvo{c}"), self.ident)
                O.cp("act", xst[:, 0:1024], tb[0])
                O.cp("dve", xst[:, 1024:2048], tb[1])
                O.cp("act", bst[:], (tb[2][0][:, 0:512], tb[2][1]))
                rr = r0 + t * 128
                O.dma("sp", (xs_d[rr:rr + 128, :], f"xs{rr // 512}"), xst[:])
                O.dma("sp", (bs_d[rr:rr + 128, :], f"bs{rr // 512}"), bst[:])
                pdt = P.pb(7)
                pdv = (pdt[0][:, 0:64], pdt[1])
                for k in range(8):
                    O.mm(pdv, buf[:, k, 2 + t * 128:2 + (t + 1) * 128], (win[:, k, 5120:5184], "win4"),
                         start=(k == 0), stop=(k == 7))
                dm = dtm[t % 2]
                O.tt("dve", dm[:], pdv, dtb, ALU.add)
                O.act(dm[:], dm[:], AF.Exp)
                ds = dstg[t % 2]
                O.act(ds[:], dm[:], AF.Ln, bias=1.0)
                O.dma("sp", (dt_d[rr:rr + 128, :], f"dt{rr // 512}"), ds[:])
                if need_c:
                    zt = zstg[t % 2]
                    for cb4 in range(4):
                        pz = P.pb(7 if cb4 % 2 == 0 else 6)
                        for k in range(8):
                            O.mm(pz, buf[:, k, 2 + t * 128:2 + (t + 1) * 128],
                                 (win[:, k, cb4 * 512:(cb4 + 1) * 512], wkey(cb4 * 512)), start=(k == 0), stop=(k == 7))
                        O.act(zt[:, cb4 * 512:(cb4 + 1) * 512], pz, AF.Silu)
                    O.dma("sp", (zs_d[rr:rr + 128, :], f"zs{rr // 512}"), zt[:])

        ng = SEQ // 512
        for g in range(ng + 1):
            if g < ng:
                S1("seq", g * 512, 512, uw[g % 3], g == 0, g == ng - 1, uw[(g - 1) % 3], uw[(g + 1) % 3])
            if g >= 1:
                S2((g - 1) * 512, 512, uw[(g - 1) % 3], (g - 1) < HALF // 512)
        S1("ctx", 0, CTX, uw[0], True, True, None, None)
        S2(SEQ, CTX, uw[0], True)
        P.release(mA)
        P.release_hi()

        mB = P.mark()
        esel = P.sb("esel_sb", [32, 32, 128], BF16)
        O.dma("sp", esel[:], esel_d.rearrange("k (t s) -> k t s", t=32))
        nwb = P.sb("normw_bc", [128, 2048], F32)
        O.dma("sp", nwb[:], normw.broadcast_to([128, 2048]))
        wout = P.sb("wout_ssd", [128, 16, D], BF16)
        O.dma("pool", wout[:], w_out.rearrange("(k p) n -> p k n", p=128))
        st = P.sb("sst", [128, 4, 512], F32)
        stb = P.sb("sstb", [128, 4, 512], BF16)
        xs_t = [P.sb(f"xs_t{i}", [128, 2048], BF16) for i in range(2)]
        bs_t = [P.sb(f"bs_t{i}", [128, 512], BF16) for i in range(2)]
        dt_t = [P.sb(f"dt_t{i}", [128, 64], F32) for i in range(2)]
        bt_t = [P.sb(f"bt_t{i}", [128, 4, 128], BF16) for i in range(2)]
        ct_t = [P.sb(f"ct_t{i}", [128, 4, 128], BF16) for i in range(2)]
        sm = P.sb("ssm", [128, 8, 32], F32)
        hilo = P.sb("hilo", [32, 2, 128], BF16)
        xdt = P.sb("xdt", [128, 2048], BF16)
        xdtd = P.sb("xdtd", [128, 2048], BF16)
        cbs = P.sb("cbs", [128, 512], F32)
        dec = [P.sb(f"dec{i}", [128, 128], F32) for i in range(3)]
        wts = [P.sb(f"wts{i}", [128, 128], BF16) for i in range(3)]
        ytmp = P.sb("ytmp", [128, 512], F32)
        ybuf = [P.sb(f"ybuf{i}", [128, 2048], F32) for i in range(2)]
        yf_t = P.sb("yf_t", [128, 2048], F32)
        zs_t = P.sb("zs_t", [128, 2048], BF16)
        tmp2 = P.sb("tmp2k", [128, 2048], F32)
        ynb = P.sb("ynb", [128, 2048], BF16)
        ynT = P.sb("ynT", [128, 16, 128], BF16)
        fjunk = P.sb("sjunk", [128, 512], BF16)
        R = self.resid_setup()
        self._ci = 0

        def core(dirn, row0, need_y, yb):
            i = self._ci
            self._ci += 1
            g5 = row0 // 512
            xt_, bt_, dtt = xs_t[i % 2], bs_t[i % 2], dt_t[i % 2]
            O.dma("sp", xt_[:], (xs_d[row0:row0 + 128, :], f"xs{g5}"))
            O.dma("sp", bt_[:], (bs_d[row0:row0 + 128, :], f"bs{g5}"))
            O.dma("sp", dtt[:], (dt_d[row0:row0 + 128, :], f"dt{g5}"))
            dtv = dtt[:, dirn * 32:(dirn + 1) * 32]
            a, acs, cdec, dtmp, dte, nacs, eacs = [sm[:, j, :] for j in range(7)]
            O.tt("dve", a, dtv, Abc[:, dirn * 32:(dirn + 1) * 32], ALU.mult)
            pm = P.pb(0)
            O.mm((pm[0][:, 0:32], pm[1]), tri[dirn], a)
            O.mm((pm[0][:, 32:64], pm[1]), self.ones_f, a)
            O.cp("act", acs, (pm[0][:, 0:32], pm[1]))
            O.act(cdec, (pm[0][:, 32:64], pm[1]), AF.Exp)
            O.tt("dve", dtmp, (pm[0][:, 32:64], pm[1]), acs, ALU.subtract)
            O.act(dte, dtmp, AF.Exp)
            x3 = xt_[:].rearrange("p (h q) -> p h q", h=32)
            O.tt("dve", xdt[:].rearrange("p (h q) -> p h q", h=32), x3,
                 dtv.unsqueeze(2).to_broadcast([128, 32, 64]), ALU.mult)
            O.tt("pool", xdtd[:].rearrange("p (h q) -> p h q", h=32), xdt[:].rearrange("p (h q) -> p h q", h=32),
                 dte.unsqueeze(2).to_broadcast([128, 32, 64]), ALU.mult)
            if need_y:
                btt, ctt = bt_t[i % 2], ct_t[i % 2]
                O.dma("sp", btt[:], (bt_d[:, :, row0:row0 + 128], f"bt{g5}"))
                O.dma("sp", ctt[:], (ct_d[:, :, row0:row0 + 128], f"ct{g5}"))
                O.mm((pm[0][0:32, 64:192], pm[1]), a, tri[dirn])
                O.cp("act", hilo[:, 0, :], (pm[0][0:32, 64:192], pm[1]))
                O.tt("dve", hilo[:, 1, :], (pm[0][0:32, 64:192], pm[1]), hilo[:, 0, :], ALU.subtract)
                O.ts("dve", nacs, acs, -1.0, None, ALU.mult)
                O.act(eacs, acs, AF.Exp)
                pcb = P.pb(3)
                for g in range(4):
                    O.mm((pcb[0][:, g * 128:(g + 1) * 128], pcb[1]), btt[:, g, :], ctt[:, g, :])
                O.cp("act", cbs[:], pcb)
                for g in range(4):
                    pyd = P.pb(6)
                    pyo = P.pb(7)
                    O.mm(pyo, ctt[:, g, :], stb[:, g, :])
                    for hh in range(8):
                        h = g * 8 + hh
                        psg = P.pb(4 + (h // 4) % 2)
                        pr = (psg[0][:, (h % 4) * 128:(h % 4 + 1) * 128], psg[1])
                        O.mm(pr, esel[:, h, :], hilo[:, 0, :], start=True, stop=False)
                        O.mm(pr, esel[:, h, :], hilo[:, 1, :], start=False, stop=False)
                        O.mm(pr, self.ident, nm[dirn], start=False, stop=True)
                        dc = dec[h % 3]
                        O.act(dc[:], pr, AF.Exp, bias=nacs[:, h:h + 1])
                        wt = wts[h % 3]
                        O.tt("dve", wt[:], dc[:], cbs[:, g * 128:(g + 1) * 128], ALU.mult)
                        O.mm((pyd[0][:, hh * 64:(hh + 1) * 64], pyd[1]), wt[:], xdt[:, h * 64:(h + 1) * 64])
                    O.tt("dve", ytmp[:].rearrange("p (h q) -> p h q", h=8),
                         (pyo[0].rearrange("p (h q) -> p h q", h=8), pyo[1]),
                         eacs[:, g * 8:(g + 1) * 8].unsqueeze(2).to_broadcast([128, 8, 64]), ALU.mult)
                    O.tt("dve", yb[:, g * 512:(g + 1) * 512], ytmp[:], pyd, ALU.add)
            for g in range(4):
                pcs = P.pb(1 + g % 2)
                O.mm(pcs, bt_[:, g * 128:(g + 1) * 128], xdtd[:, g * 512:(g + 1) * 512])
                sv = st[:, g, :].rearrange("p (h q) -> p h q", h=8)
                O.tt("pool", (sv, f"sst{g}"), (sv, f"sst{g}"),
                     cdec[:, g * 8:(g + 1) * 8].unsqueeze(2).to_broadcast([128, 8, 64]), ALU.mult)
                O.tt("dve", (st[:, g, :], f"sst{g}"), (st[:, g, :], f"sst{g}"), pcs, ALU.add)
                O.cp("act", (stb[:, g, :], f"sstb{g}"), (st[:, g, :], f"sst{g}"))

        def finalize(kind, t, row0, yb, xt_):
            g5 = row0 // 512
            O.dma("sp", yf_t[:], (yf_d[row0:row0 + 128, :], f"yf{g5}"))
            O.dma("sp", zs_t[:], (zs_d[row0:row0 + 128, :], f"zs{g5}"))
            O.tt("pool", yb[:], yb[:], yf_t[:], ALU.add)
            O.tt("dve", tmp2[:].rearrange("p (h q) -> p h q", h=32), xt_[:].rearrange("p (h q) -> p h q", h=32),
                 dskip.unsqueeze(2).to_broadcast([128, 32, 64]), ALU.mult)
            O.tt("pool", yb[:], yb[:], tmp2[:], ALU.add)
            O.tt("dve", yb[:], yb[:], zs_t[:], ALU.mult)
            ss4 = self.newstat(4)
            for g in range(4):
                O.act(fjunk[:], yb[:, g * 512:(g + 1) * 512], AF.Square, accum_out=(ss4[0][:, g:g + 1], ss4[1]))
            sd4 = self.newstat(4)
            O.act(sd4, ss4, AF.Sqrt, bias=self.eps_t, scale=1.0 / 512)
            rs4 = self.newstat(4)
            O.recip(rs4, sd4)
            for g in range(4):
                O.ts("dve" if g % 2 == 0 else "pool", yb[:, g * 512:(g + 1) * 512], yb[:, g * 512:(g + 1) * 512],
                     (rs4[0][:, g:g + 1], rs4[1]), None, ALU.mult)
            O.tt("dve", ynb[:], yb[:], nwb[:], ALU.mult)
            tb = [P.pb(4, 1, BF16), P.pb(5, 1, BF16)]
            tv = [(x[0].rearrange("p (k t) -> p k t", k=8), x[1]) for x in tb]
            for c in range(16):
                O.tr((tv[c // 8][0][:, c % 8, :], tv[c // 8][1]), ynb[:, c * 128:(c + 1) * 128], self.ident)
            O.cp("act", ynT[:, 0:8, :], tv[0])
            O.cp("dve", ynT[:, 8:16, :], tv[1])
            self.resid_tile(R, kind, t, lambda c: ynT[:, c, :], lambda c: wout[:, c, :], 16, banks=(6, 6))

        def zero_state():
            for g in range(4):
                O.memset("pool", (st[:, g, :], f"sst{g}"), 0.0)
                O.memset("pool", (stb[:, g, :], f"sstb{g}"), 0.0)

        zero_state()
        chain_f = [("ctx", t, SEQ + t * 128) for t in range(CTX // 128)] + [("lat", t, t * 128) for t in range(HALF // 128)]
        for n, (kind, t, row0) in enumerate(chain_f):
            yb = ybuf[n % 2]
            core(0, row0, True, yb)
            O.dma("sp", (yf_d[row0:row0 + 128, :], f"yf{row0 // 512}"), yb[:])
        zero_state()
        chain_b = [("ctx", t, SEQ + t * 128) for t in reversed(range(CTX // 128))]
        chain_b += [("oth", t, HALF + t * 128) for t in reversed(range(HALF // 128))]
        chain_b += [("lat", t, t * 128) for t in reversed(range(HALF // 128))]
        for n, (kind, t, row0) in enumerate(chain_b):
            yb = ybuf[n % 2]
            i = self._ci
            core(1, row0, kind != "oth", yb)
            if kind != "oth":
                finalize(kind, t, row0, yb, xs_t[i % 2])
        P.release(mB)

    def phase_ffn(self):
        P, O = self.P, self.O
        m = P.mark()
        win = P.sb("win", [128, 8, 2 * FFN], BF16)
        wout = P.sb("wout", [128, NFC, D], BF16)
        wiv = self.ffn_w_in.rearrange("(k p) n -> p k n", p=128)
        O.dma("pool", (win[:], [f"win{j}" for j in range(4)]), wiv)
        O.dma("pool", wout[:], self.ffn_w_out.rearrange("(k p) n -> p k n", p=128))
        g2 = [P.sb(f"g2_{i}", [128, D], F32) for i in range(2)]
        O.dma("sp", g2[0][:], (self.gates[2], "gates2"))
        O.dma("sp", g2[1][:], (self.gates[3], "gates3"))
        xts = [P.sb(f"fxt{i}", [128, D], F32) for i in range(2)]
        xhs = [P.sb(f"fxh{i}", [128, D], BF16) for i in range(2)]
        uT = P.sb("fuT", [128, 8, 512], BF16)
        hT = P.sb("fhT", [128, NFC, 512], BF16)
        sg = [P.sb(f"fsg{i}", [128, 512], F32) for i in range(2)]
        tmp = [P.sb(f"ftmp{i}", [128, D], F32) for i in range(2)]
        junk = P.sb("fjunk", [128, D], BF16)
        groups = [("lat", g * 512, 512) for g in range(HALF // 512)]
        if self.want_ctx:
            groups.append(("ctx", HALF, CTX))
        cnt = 0
        for gi, (kind, r0, ntok) in enumerate(groups):
            nt = ntok // 128
            for t in range(nt):
                xt = xts[cnt % 2]
                xh = xhs[cnt % 2]
                row0 = r0 + t * 128
                O.dma("sp", xt[:], (self.hmid[row0:row0 + 128, :], f"hmid{row0 // 512}"))
                self.norm_uT(xt, xh, lambda k, t=t: uT[:, k, t * 128:(t + 1) * 128],
                             2 if kind == "lat" else 6, 7 * (cnt % 2), cnt)
                cnt += 1
            for fc in range(NFC):
                pg = P.pb(1 + 2 * (fc % 2))
                pu = P.pb(2 + 2 * (fc % 2))
                pgv = (pg[0][:, :ntok], pg[1])
                puv = (pu[0][:, :ntok], pu[1])
                cg = fc * 128
                cu = FFN + fc * 128
                for k in range(8):
                    O.mm(pgv, (win[:, k, cg:cg + 128], f"win{cg // 1408}"), uT[:, k, :ntok], start=(k == 0), stop=(k == 7))
                for k in range(8):
                    O.mm(puv, (win[:, k, cu:cu + 128], f"win{cu // 1408}"), uT[:, k, :ntok], start=(k == 0), stop=(k == 7))
                st = sg[fc % 2]
                O.act(st[:, :ntok], pgv, AF.Silu)
                O.tt("dve", hT[:, fc, :ntok], st[:, :ntok], puv, ALU.mult)
            for t in range(nt):
                pf = P.pb(5, 2)
                for cb in range(2):
                    for fc in range(NFC):
                        O.mm((pf[0][:, cb, :], pf[1]), hT[:, fc, t * 128:(t + 1) * 128], wout[:, fc, cb * 512:(cb + 1) * 512],
                             start=(fc == 0), stop=(fc == NFC - 1))
                xt = xts[cnt % 2]
                cnt += 1
                row0 = r0 + t * 128
                O.dma("sp", xt[:], (self.hmid[row0:row0 + 128, :], f"hmid{row0 // 512}"))
                ss = self.newstat()
                O.act(junk[:].rearrange("p (a b) -> p a b", a=2), pf, AF.Square, accum_out=ss)
                rs = self.rstd_from_ss(ss, D)
                tm = tmp[t % 2]
                O.stt("dve", tm[:].rearrange("p (a b) -> p a b", a=2), pf, rs,
                      g2[0 if kind == "lat" else 1][:].rearrange("p (a b) -> p a b", a=2), ALU.mult, ALU.mult)
                O.tt("pool", tm[:], tm[:], xt[:], ALU.add)
                if kind == "lat":
                    O.dma("sp", (self.hout[0][row0:row0 + 128, :], self.hout[1]), tm[:])
                    if not self.last:
                        prv = P.pb(1, 2)
                        for cb in range(2):
                            O.mm((prv[0][:, cb, :], prv[1]), self.cf32[:, 512:640], tm[:, cb * 512:(cb + 1) * 512])
                        rvt = tmp[(t + 1) % 2]
                        O.cp("act", rvt[:].rearrange("p (a b) -> p a b", a=2), prv)
                        rr = HALF - 128 - row0
                        hr = self.hrev[rr // 512]
                        O.dma("sp", (hr[0][rr % 512:rr % 512 + 128, :], hr[1]), rvt[:])
                else:
                    O.dma("sp", (self.hctx_out[0][row0 - HALF:row0 - HALF + 128, :], self.hctx_out[1]), tm[:])
        P.release(m)


PAIRS = [[0, 1], [2, 3], [4, 5], [6, 7]]


class Fused:
    def __init__(self):
        nc = bass.Bass("TRN2", target_bir_lowering=False)
        self.nc = nc
        self.P = Prog(nc)
        self.O = Ops(self.P)
        self.ins = {}
        P, O = self.P, self.O

        def din(name, shape, dt=F32):
            t = nc.dram_tensor(name, list(shape), dt, kind="ExternalInput").ap()
            self.ins[name] = t
            return t
        cbf_d = din("cbf", [128, 1024], BF16)
        cf32_d = din("cf32", [128, 640])
        self.cbf = P.sb("cbf_sb", [128, 1024], BF16)
        self.cf32 = P.sb("cf32_sb", [128, 640], F32)
        O.dma("sp", self.cbf[:], cbf_d)
        O.dma("sp", self.cf32[:], cf32_d)
        hfm_d = din("hfmask", [128, 2])
        self.hfm = P.sb("hfm_sb", [128, 2], F32)
        O.dma("sp", self.hfm[:], hfm_d)
        self.src_seq = (din("x_in", [SEQ, D]), "x_in")
        self.src_ctx = (din("ctx_in", [CTX, D]), "ctx_in")
        self.src_rev = None
        for li in range(NLAYERS):
            L = Layer(li, self)
            if li < NLAYERS - 1:
                hags = []
                for c in range(8):
                    hag = nc.dram_tensor(f"hagrev{li}_{c}", [1024, D], F32, kind="Internal").ap()
                    O.allgather((hag, f"hagrev{li}_{c}"), L.hrev[c], PAIRS)
                    hags.append((hag, f"hagrev{li}_{c}"))
                self.src_seq = L.hout
                self.src_rev = hags
                self.src_ctx = L.hctx_out
        P.emit()


_CONSTS = None
_FUSED = None


def _want(hf):
    if hf == 0:
        return np.arange(SEQ)
    return np.arange(SEQ)[::-1].copy()


def _layer_inputs(li, b, hf, inp, C):
    f32 = np.float32
    kind, j = li % 3, li // 3
    pre = f"L{li}_"
    mp = {}
    want = _want(hf)
    vec = np.concatenate([inp["ada_b"][li].reshape(48, 128), inp["norm_g"][li].reshape(32, 128),
                          inp["c"][b].reshape(8, 128), inp["c_ctx"].reshape(8, 128)], axis=0)
    mp["vecT"] = np.ascontiguousarray(vec.T, dtype=f32)
    mp["rowv"] = np.ascontiguousarray(np.stack([inp["ada_b"][li][2 * D:3 * D], inp["ada_b"][li][5 * D:6 * D],
                                                inp["norm_g"][li][1], inp["norm_g"][li][3]]), dtype=f32)
    mp["ada_w"] = np.ascontiguousarray(inp["ada_w"][li], dtype=f32)
    mp["ffn_w_in"] = np.ascontiguousarray(inp["ffn_w_in"][li], dtype=f32)
    mp["ffn_w_out"] = np.ascontiguousarray(inp["ffn_w_out"][li], dtype=f32)
    if kind == 0:
        mp["w_qkv"] = np.ascontiguousarray(inp["da_w_qkv"][j], dtype=f32)
        mp["w_o"] = np.ascontiguousarray(inp["da_w_o"][j], dtype=f32)
        mp["lamv"] = np.ascontiguousarray(inp["da_lambda"][j].reshape(1, 256), dtype=f32)
        mp["subln"] = np.ascontiguousarray(inp["da_subln"][j].reshape(128, 1), dtype=f32)
        mp["cosT"] = np.ascontiguousarray(C["cosT"][:, want])
        mp["sinT"] = np.ascontiguousarray(C["sinT"][:, want])
    elif kind == 1:
        mp["w_qkv"] = np.ascontiguousarray(inp["sw_w_qkv"][j], dtype=f32)
        mp["w_o"] = np.ascontiguousarray(inp["sw_w_o"][j], dtype=f32)
        mp["sinkv"] = np.ascontiguousarray(inp["sw_sink"][j].reshape(4, 2, 2).transpose(2, 0, 1).reshape(2, 8), dtype=f32)
        em = np.ones((128, 2), f32)
        em[:, 0] = 0.0
        mp["emask"] = em
        wpos = np.concatenate([want[:HALF], want[:128], want[HALF:HALF + 128]])
        mp["cosT"] = np.ascontiguousarray(C["cosT"][:, wpos])
        mp["sinT"] = np.ascontiguousarray(C["sinT"][:, wpos])
    else:
        rev = hf == 1
        w = inp["ssd_w_in"][j]
        cwt = inp["ssd_conv_w"][j]
        alog, dtbias = inp["ssd_a_log"][j], inp["ssd_dt_bias"][j]
        if rev:
            w = np.concatenate([w[:, :5120], w[:, 5152:5184], w[:, 5120:5152]], axis=1)
            cwt = cwt[::-1]
            alog, dtbias = alog[::-1], dtbias[::-1]
        mp["w_in"] = np.ascontiguousarray(w, dtype=f32)
        mp["convw"] = np.ascontiguousarray(cwt.T.reshape(24, 128, 5).transpose(1, 0, 2), dtype=f32)
        mp["convb"] = np.ascontiguousarray(inp["ssd_conv_b"][j].reshape(24, 128).T, dtype=f32)
        mp["ssdrow"] = np.ascontiguousarray(np.concatenate([alog.reshape(64), dtbias.reshape(64),
                                                            inp["ssd_d_skip"][j].reshape(32)]).reshape(1, 160), dtype=f32)
        mp["normw"] = np.ascontiguousarray(inp["ssd_norm"][j].reshape(1, 2048), dtype=f32)
        mp["w_out"] = np.ascontiguousarray(inp["ssd_w_out"][j], dtype=f32)
        mp["esel"] = C["esel"]
    return {pre + k: v for k, v in mp.items()}


def kernel(**inp):
    global _CONSTS, _FUSED
    inp = {k: np.asarray(v) for k, v in inp.items()}
    if _CONSTS is None:
        _CONSTS = _consts()
    if _FUSED is None:
        _FUSED = Fused()
    C, Fz = _CONSTS, _FUSED
    in_maps = []
    for core in range(8):
        b, hf = core // 2, core % 2
        want = _want(hf)
        hfm = np.zeros((128, 2), np.float32)
        hfm[:, hf] = 1.0
        cx = inp["ctx"][b][::-1] if hf == 1 else inp["ctx"][b]
        mp = {"cbf": C["cbf"], "cf32": C["cf32"], "hfmask": hfm,
              "x_in": np.ascontiguousarray(inp["x"][b][want], dtype=np.float32),
              "ctx_in": np.ascontiguousarray(cx, dtype=np.float32)}
        for li in range(NLAYERS):
            mp.update(_layer_inputs(li, b, hf, inp, C))
        in_maps.append({k: mp[k] for k in Fz.ins})
    res = run_bass_kernel_spmd(Fz.nc, in_maps[:NCORES], core_ids=list(range(NCORES)))
    out = np.zeros((4, SEQ, D), np.float32)
    for core in range(NCORES):
        b, hf = core // 2, core % 2
        out[b, _want(hf)[:HALF]] = res.results[core]["hout"]
    return out
```

```python
import contextlib
import math

import ml_dtypes
import numpy as np

import concourse.bass as bass
import concourse.mybir as mybir
from concourse.bass_utils import run_bass_kernel_spmd

F32 = mybir.dt.float32
I32 = mybir.dt.int32
BF16 = mybir.dt.bfloat16
AF = mybir.ActivationFunctionType
ALU = mybir.AluOpType
AX = mybir.AxisListType

D = 1024
SEQ = 8192
HALF = 4096
CTX = 256
DEPTH = 4
import os
NLAYERS = int(os.environ.get('KDEPTH', '4'))
EPS = 1e-6
FFN = 2816
NFC = FFN // 128
DA_SCALE = 64 ** -0.5
SEM_LIM = 8000
SB_LO, SB_HI = 16512, 227328
DEBUG_OUT = set()
import os
STOP = int(os.environ.get('KSTOP', '99'))
NCORES = int(os.environ.get('KCORES', '8'))
DEBUG_RES = {}


class _Op:
    __slots__ = ("eng", "fn", "dma", "grp", "waits", "sig", "idx", "needs", "pos")

    def __init__(self, eng, fn, dma, grp):
        self.eng = eng
        self.fn = fn
        self.dma = dma
        self.grp = grp
        self.waits = []
        self.sig = None
        self.idx = None
        self.needs = False
        self.pos = 0


def _ap(x):
    return x[0] if isinstance(x, tuple) else x


def _keys(x):
    if isinstance(x, tuple):
        k = x[1]
        return list(k) if isinstance(k, (list, tuple)) else [k]
    if isinstance(x, str):
        return [x]
    return [x.name]


class Prog:
    ENGS = ("pe", "act", "dve", "pool", "sp")

    def __init__(self, nc):
        self.nc = nc
        self.ops = {e: [] for e in self.ENGS}
        self.lastw = {}
        self.readers = {}
        self.grp_cnt = {}
        self.grp_unit = {}
        self.slot_of = {}
        self.waited = {e: {} for e in self.ENGS}
        self.sb_top = SB_LO
        self.hi_top = SB_HI
        self.live_hi = []
        self.live = []
        self.freed = []
        self.inh = {}
        self.subkeys = {}
        self.psum = nc.alloc_psum_tensor("psall", [128, 8, 512], F32)

    def sb(self, name, shape, dt, hi=False):
        n = 1
        for s in shape[1:]:
            n *= s
        nbytes = n * (4 if dt == F32 else 2)
        nbytes = (nbytes + 63) // 64 * 64
        if hi:
            self.hi_top -= nbytes
            off = self.hi_top
        else:
            off = self.sb_top
            self.sb_top += nbytes
        assert self.sb_top <= self.hi_top, f"SBUF overflow allocating {name}: {self.sb_top} {self.hi_top}"
        t = self.nc.alloc_sbuf_tensor_at(name, list(shape), dt, offset=off)
        key = t.name
        inh = []
        for (k0, a0, b0) in self.freed:
            if a0 < off + nbytes and off < b0:
                w = self.lastw.get(k0)
                if w is not None:
                    inh.append(w)
                inh.extend(self.readers.get(k0, ()))
                inh.extend(self.inh.get(k0, ()))
                for kx in self.subkeys.get(k0, ()):
                    w = self.lastw.get(kx)
                    if w is not None:
                        inh.append(w)
                    inh.extend(self.readers.get(kx, ()))
        if inh:
            red = {}
            for d in inh:
                kk = ("d", d.grp) if d.dma else ("e", d.eng)
                if kk not in red or d.pos > red[kk].pos:
                    red[kk] = d
            self.inh[key] = list(red.values())
        if hi:
            self.live_hi.append((key, off, off + nbytes))
        else:
            self.live.append((key, off, off + nbytes))
        return t

    def release_hi(self):
        self.freed.extend(self.live_hi)
        self.live_hi = []
        self.hi_top = SB_HI

    def mark(self):
        return (self.sb_top, len(self.live))

    def release(self, m):
        self.freed.extend(self.live[m[1]:])
        del self.live[m[1]:]
        self.sb_top = m[0]

    def pb(self, b, n=1, dt=F32):
        keys = [f"ps{b + i}" for i in range(n)]
        if n == 1:
            ap = self.psum[:, b, :]
        else:
            ap = self.psum[:, b:b + n, :]
        if dt != F32:
            ap = ap.bitcast(dt)
        return ap, keys

    NSLOT = int(os.environ.get("KNSLOT", "56"))

    def add(self, eng, fn, reads, writes, dma=False, grp=None, unit=16):
        if dma:
            if unit == 16:
                if grp not in self.slot_of:
                    self.slot_of[grp] = f"slot{len(self.slot_of) % self.NSLOT}"
                grp = self.slot_of[grp]
            self.grp_unit[grp] = unit
        op = _Op(eng, fn, dma, grp)
        rk, wk = [], []
        for r in reads:
            if r is not None and not isinstance(r, (int, float)):
                rk.extend(_keys(r))
        deps = []
        for w in writes:
            if w is not None:
                wk.extend(_keys(w))
                nm = _ap(w).name
                for d in self.inh.get(nm, ()):
                    deps.append((d, False))
        for x in list(reads) + list(writes):
            if isinstance(x, tuple):
                nm = _ap(x).name
                sk = self.subkeys.setdefault(nm, set())
                for k in _keys(x):
                    sk.add(k)
        for k in rk:
            w = self.lastw.get(k)
            if w is not None:
                deps.append((w, True))
        for k in wk:
            w = self.lastw.get(k)
            if w is not None:
                deps.append((w, False))
            for r in self.readers.get(k, ()):
                deps.append((r, False))
        need_e, need_d = {}, {}
        for d, raw in deps:
            if d.dma:
                need_d[d.grp] = self.grp_cnt[d.grp]
            else:
                if d.eng == eng and not dma and (eng == "pe" or not raw):
                    continue
                cur = need_e.get(d.eng)
                if cur is None or d.pos > cur.pos:
                    need_e[d.eng] = d
        self.ops[eng].append(op)
        op.pos = len(self.ops[eng])
        if dma:
            self.grp_cnt[grp] = self.grp_cnt.get(grp, 0) + 1
            op.idx = self.grp_cnt[grp]
        wd = self.waited[eng]
        for g, cnt in need_d.items():
            if wd.get(("d", g), 0) < cnt:
                wd[("d", g)] = cnt
                op.waits.append(("d", g, cnt))
        for e2, d in need_e.items():
            if wd.get(("e", e2), 0) < d.pos:
                wd[("e", e2)] = d.pos
                d.needs = True
                op.waits.append(("e", e2, d))
        for k in rk:
            self.readers.setdefault(k, []).append(op)
        for k in wk:
            self.lastw[k] = op
            self.readers[k] = []
        return op

    def emit(self):
        nc = self.nc
        with contextlib.ExitStack() as es:
            esems = {}
            nalloc = 0
            for e in self.ENGS:
                n = 0
                for op in self.ops[e]:
                    if not op.dma and op.needs:
                        n += 1
                        op.sig = n
                nsem = (n + SEM_LIM - 1) // SEM_LIM
                esems[e] = [es.enter_context(nc.semaphore(f"pg_{e}_{i}")) for i in range(nsem)]
                nalloc += nsem
            gsems = {}
            for g in self.grp_cnt:
                gsems[g] = es.enter_context(nc.semaphore(f"dg_{len(gsems)}"))
                nalloc += 1
            assert nalloc <= 100, f"too many semaphores {nalloc}"
            block = es.enter_context(nc.Block())

            def run(engname):
                def body(eng):
                    for op in self.ops[engname]:
                        for w in op.waits:
                            if w[0] == "d":
                                eng.wait_ge(gsems[w[1]], self.grp_unit[w[1]] * w[2])
                            else:
                                s = w[2].sig - 1
                                eng.wait_ge(esems[w[1]][s // SEM_LIM], s % SEM_LIM + 1)
                        ins = op.fn(eng)
                        if op.dma:
                            ins.then_inc(gsems[op.grp], self.grp_unit[op.grp])
                        elif op.sig is not None:
                            s = op.sig - 1
                            ins.then_inc(esems[engname][s // SEM_LIM], 1)
                    for g in sorted({op.grp for op in self.ops[engname] if op.dma}, key=str):
                        eng.wait_ge(gsems[g], self.grp_unit[g] * self.grp_cnt[g])
                return body

            if self.ops["sp"]:
                block.sync(run("sp"))
            if self.ops["act"]:
                block.scalar(run("act"))
            if self.ops["dve"]:
                block.vector(run("dve"))
            if self.ops["pool"]:
                block.gpsimd(run("pool"))
            if self.ops["pe"]:
                block.tensor(run("pe"))


class Ops:
    def __init__(self, P):
        self.P = P

    def mm(self, out, lhsT, rhs, start=True, stop=True, **kw):
        o, l, r = _ap(out), _ap(lhsT), _ap(rhs)
        return self.P.add("pe", lambda e: e.matmul(o, l, r, start=start, stop=stop, **kw), [lhsT, rhs], [out])

    def tr(self, out, in_, ident):
        o, i, d = _ap(out), _ap(in_), _ap(ident)
        return self.P.add("pe", lambda e: e.transpose(o, i, d), [in_, ident], [out])

    def act(self, out, in_, func, bias=0.0, scale=1.0, accum_out=None):
        o, i, b, s, a = _ap(out), _ap(in_), _ap(bias), _ap(scale), _ap(accum_out)
        kw = {}
        if a is not None:
            kw["accum_out"] = a
        return self.P.add("act", lambda e: e.activation(o, i, func, bias=b, scale=s, **kw),
                          [in_, bias, scale], [out, accum_out])

    def tt(self, eng, out, in0, in1, op):
        o, a, b = _ap(out), _ap(in0), _ap(in1)
        return self.P.add(eng, lambda e: e.tensor_tensor(o, a, b, op), [in0, in1], [out])

    def ts(self, eng, out, in0, s1, s2, op0, op1=None):
        o, a, x1, x2 = _ap(out), _ap(in0), _ap(s1), _ap(s2)
        kw = {}
        if op1 is not None:
            kw["op1"] = op1
        return self.P.add(eng, lambda e: e.tensor_scalar(o, a, x1, x2, op0, **kw), [in0, s1, s2], [out])

    def stt(self, eng, out, in0, scalar, in1, op0, op1):
        o, a, s, b = _ap(out), _ap(in0), _ap(scalar), _ap(in1)
        return self.P.add(eng, lambda e: e.scalar_tensor_tensor(o, a, s, b, op0, op1), [in0, scalar, in1], [out])

    def cp(self, eng, out, in_):
        o, i = _ap(out), _ap(in_)
        if eng == "act":
            return self.P.add(eng, lambda e: e.copy(o, i), [in_], [out])
        return self.P.add(eng, lambda e: e.tensor_copy(o, i), [in_], [out])

    def red(self, eng, out, in_, op):
        o, i = _ap(out), _ap(in_)
        return self.P.add(eng, lambda e: e.tensor_reduce(o, i, AX.X, op), [in_], [out])

    def memset(self, eng, out, val):
        o = _ap(out)
        return self.P.add(eng, lambda e: e.memset(o, val), [], [out])

    def recip(self, out, in_, eng="dve"):
        o, i = _ap(out), _ap(in_)
        return self.P.add(eng, lambda e: e.reciprocal(o, i), [in_], [out])

    def gather(self, out, src, idx):
        o, i, x = _ap(out), _ap(src), _ap(idx)
        grp = _keys(out)[0]
        return self.P.add("pool", lambda e: e.indirect_dma_start(
            out=o, out_offset=None, in_=i, in_offset=bass.IndirectOffsetOnAxis(ap=x, axis=0)),
            [src, idx], [out], dma=True, grp=grp)

    def allgather(self, out, in_, groups):
        o, i = _ap(out), _ap(in_)
        return self.P.add("pool", lambda e: e.collective_compute(
            "AllGather", ALU.bypass, replica_groups=groups, ins=[i.opt()], outs=[o.opt()]),
            [in_], [out], dma=True, grp="cc", unit=1)

    def dma(self, eng, out, in_, grp=None):
        o, i = _ap(out), _ap(in_)
        if grp is None:
            grp = _keys(out)[0] if str(o.space).endswith("SB") else _keys(in_)[0]
        return self.P.add(eng, lambda e: e.dma_start(out=o, in_=i), [in_], [out], dma=True, grp=grp)


def _consts():
    ident = np.eye(128, dtype=np.float32)
    perm = np.zeros((128, 128), np.float32)
    for p in range(128):
        d = p % 64
        if d < 32:
            perm[p + 32, p] = -1.0
        else:
            perm[p - 32, p] = 1.0
    nf = 16
    freqs = (np.float32(10000.0) ** (-np.arange(nf, dtype=np.float32) / np.float32(nf))).astype(np.float32)
    rows = np.repeat(np.arange(SEQ // 64), 64).astype(np.float32)
    cols = np.tile(np.arange(64), SEQ // 64).astype(np.float32)
    ang = np.concatenate([rows[:, None] * freqs, cols[:, None] * freqs], axis=-1).astype(np.float32)
    cosT = np.ascontiguousarray(np.tile(np.cos(ang).astype(np.float32).T, (4, 1)))
    sinT = np.ascontiguousarray(np.tile(np.sin(ang).astype(np.float32).T, (4, 1)))
    ii = np.arange(128)[:, None]
    jj = np.arange(128)[None, :]
    m_prev = (ii >= jj).astype(np.float32)
    m_next = (ii <= jj).astype(np.float32)
    tri = (ii <= jj).astype(np.float32)
    nm_f = np.where(ii > jj, -30000.0, 0.0).astype(np.float32)
    nm_b = np.where(ii < jj, -30000.0, 0.0).astype(np.float32)
    triT = (ii >= jj).astype(np.float32)
    esel = np.zeros((32, 32, 128), np.float32)
    for k in range(32):
        esel[k, k, :] = 1.0
    cmat = np.concatenate([ident, perm, np.ones((128, 128), np.float32), m_prev, m_next, tri, nm_f, nm_b], axis=1)
    return {
        "cbf": cmat.astype(ml_dtypes.bfloat16),
        "cf32": np.concatenate([ident, np.ones((128, 128), np.float32), tri, triT, ident[::-1].copy()], axis=1),
        "esel": esel.reshape(32, 4096).astype(ml_dtypes.bfloat16),
        "cosT": cosT, "sinT": sinT,
    }


class Layer:
    def __init__(self, li, parent):
        self.li = li
        self.kind = li % 3
        self.want_ctx = li < DEPTH - 1
        self.last = li == NLAYERS - 1
        self.nc, self.P, self.O = parent.nc, parent.P, parent.O
        self.ins = parent.ins
        self.pre = f"L{li}_"
        self.cbf, self.cf32 = parent.cbf, parent.cf32
        self.src_seq, self.src_ctx, self.src_rev = parent.src_seq, parent.src_ctx, parent.src_rev
        self.hfm = parent.hfm
        self.build()

    def din(self, name, shape, dt=F32):
        name = self.pre + name
        t = self.nc.dram_tensor(name, list(shape), dt, kind="ExternalInput").ap()
        self.ins[name] = t
        return t

    def dout(self, name, shape, dt=F32):
        return self.nc.dram_tensor(name, list(shape), dt, kind="ExternalOutput").ap()

    def dscr(self, name, shape, dt):
        return self.nc.dram_tensor(self.pre + name, list(shape), dt, kind="Internal").ap()

    def load_tile(self, xt, which, t):
        O = self.O
        if which == "ctx":
            O.dma("sp", xt[:], (self.src_ctx[0][t * 128:(t + 1) * 128, :], self.src_ctx[1]))
            return
        if which == "halo":
            if t == 0:
                O.dma("sp", xt[:], (self.src_seq[0][0:128, :], self.src_seq[1]))
                return
            t = HALF // 128
        if self.src_rev is None or t < HALF // 128:
            O.dma("sp", xt[:], (self.src_seq[0][t * 128:(t + 1) * 128, :], self.src_seq[1]))
            return
        j = t - HALF // 128
        rv = self.src_rev[j // 4]
        o = (j % 4) * 128
        O.dma("sp", self.xa[:], (rv[0][512 + o:512 + o + 128, :], rv[1]))
        O.dma("sp", self.xb[:], (rv[0][o:o + 128, :], rv[1]))
        O.act(xt[:], self.xa[:], AF.Copy, scale=self.hfm[:, 0:1])
        O.stt("dve", xt[:], self.xb[:], self.hfm[:, 1:2], xt[:], ALU.mult, ALU.add)

    def build(self):
        P, O = self.P, self.O
        mL = P.mark()
        if self.src_rev is not None:
            self.xa = P.sb("blend_a", [128, D], F32, hi=True)
            self.xb = P.sb("blend_b", [128, D], F32, hi=True)
        self.vecT = self.din("vecT", [128, 96])
        self.rowv = self.din("rowv", [4, D])
        self.ada_w = self.din("ada_w", [D, 6 * D])
        self.ffn_w_in = self.din("ffn_w_in", [D, 2 * FFN])
        self.ffn_w_out = self.din("ffn_w_out", [FFN, D])
        if self.last:
            self.hout = (self.dout("hout", [HALF, D]), "hout")
        else:
            self.hout = (self.dscr("hown", [HALF, D], F32), self.pre + "hown")
            self.hrev = [(self.dscr(f"hrev{c}", [512, D], F32), self.pre + f"hrev{c}") for c in range(8)]
        if self.want_ctx:
            self.hctx_out = (self.dscr("hctxo", [CTX, D], F32), self.pre + "hctxo")
        self.hmid = self.dscr("hmid", [HALF + CTX, D], F32)
        self.gates = self.dscr("gates", [4, 128, D], F32)
        self.ident = self.cbf[:, 0:128]
        self.perm = self.cbf[:, 128:256]
        self.ones_bf = self.cbf[:, 256:384]
        self.ones_f = self.cf32[:, 128:256]
        self.mods = P.sb("mods", [128, 8, 8], F32)
        self.stat = P.sb("stat", [128, 64], F32)
        self.stat_i = 0

        self.phase_adaln()
        if self.kind == 0:
            self.da_mixer()
        elif self.kind == 1:
            self.sw_mixer()
        else:
            self.ssd_mixer()
        P.release_hi()
        self.phase_ffn()
        P.release(mL)

    def newstat(self, n=1):
        if self.stat_i + n > 64:
            self.stat_i = 0
        s = self.stat[:, self.stat_i:self.stat_i + n]
        key = f"stat{self.stat_i}"
        self.stat_i += n
        return (s, key)

    def rstd_from_ss(self, ss, n):
        O = self.O
        sd = self.newstat()
        O.act(sd, ss, AF.Sqrt, bias=self.eps_t, scale=1.0 / n)
        rs = self.newstat()
        O.recip(rs, sd)
        return rs

    def phase_adaln(self):
        P, O = self.P, self.O
        eps_t = P.sb("eps_t", [128, 1], F32)
        O.memset("dve", eps_t[:], EPS)
        self.eps_t = eps_t[:, 0:1]
        m = P.mark()
        vec = P.sb("vec", [128, 96], F32)
        O.dma("sp", vec[:], self.vecT)
        act16 = P.sb("act16", [128, 16], BF16)
        O.act(act16[:], vec[:, 80:96], AF.Silu)
        actrep = P.sb("actrep", [128, 16, 128], BF16)
        for j in range(16):
            O.cp("dve", (actrep[:, j, :], f"actrep{j}"), act16[:, j:j + 1].to_broadcast([128, 128]))
        bb = [P.sb(f"bb{i}", [128, D], F32) for i in range(2)]
        gb = [P.sb(f"gb{i}", [128, D], F32) for i in range(2)]
        for i in range(2):
            O.dma("sp", bb[i][:], self.rowv[i:i + 1, :].broadcast_to([128, D]))
            O.dma("sp", gb[i][:], self.rowv[2 + i:3 + i, :].broadcast_to([128, D]))
        wm = [P.sb(f"adaw{i}", [128, 8, D], BF16) for i in range(2)]
        gt = [P.sb(f"gatet{i}", [128, D], F32) for i in range(2)]
        pm, pmk = P.pb(0)
        pm = pm.rearrange("p (s t) -> p s t", t=2)
        slot = 0
        gi = 0
        aw = self.ada_w.rearrange("(k p) n -> p k n", p=128)
        for mod in range(6):
            w = wm[mod % 2]
            O.dma("pool", w[:], aw[:, :, mod * D:(mod + 1) * D])
            if mod in (0, 1, 3, 4):
                for c in range(8):
                    for k in range(8):
                        O.mm((pm[:, slot, :], pmk), w[:, k, c * 128:(c + 1) * 128], act16[:, k::8],
                             start=(k == 0), stop=(k == 7))
                    slot += 1
            else:
                which = 0 if mod == 2 else 1
                for lc in range(2):
                    g = gt[gi % 2]
                    for cb in range(2):
                        pg = P.pb(1 + cb)
                        for k in range(8):
                            O.mm(pg, (actrep[:, lc * 8 + k, :], f"actrep{lc * 8 + k}"), w[:, k, cb * 512:(cb + 1) * 512],
                                 start=(k == 0), stop=(k == 7))
                        O.tt("dve", g[:, cb * 512:(cb + 1) * 512], pg, bb[which][:, cb * 512:(cb + 1) * 512], ALU.add)
                    O.tt("dve", g[:], g[:], gb[which][:], ALU.mult)
                    O.dma("sp", (self.gates[which * 2 + lc], f"gates{which * 2 + lc}"), g[:])
                    gi += 1
        mods = self.mods
        pmv = P.pb(0)[0].rearrange("p (s t) -> p s t", t=2)
        for lc in range(2):
            base = lc * 4
            O.tt("dve", mods[:, base + 0, :], (pmv[:, 0:8, lc], pmk), vec[:, 0:8], ALU.add)
            O.tt("dve", mods[:, base + 1, :], (pmv[:, 8:16, lc], pmk), vec[:, 8:16], ALU.add)
            O.stt("dve", mods[:, base + 1, :], mods[:, base + 1, :], 1.0, vec[:, 48:56], ALU.add, ALU.mult)
            O.tt("dve", mods[:, base + 2, :], (pmv[:, 16:24, lc], pmk), vec[:, 24:32], ALU.add)
            O.tt("dve", mods[:, base + 3, :], (pmv[:, 24:32, lc], pmk), vec[:, 32:40], ALU.add)
            O.stt("dve", mods[:, base + 3, :], mods[:, base + 3, :], 1.0, vec[:, 64:72], ALU.add, ALU.mult)
        P.release(m)

    def norm_uT(self, xt, xh, uT_slice_fn, mod_base, tb, ei):
        P, O = self.P, self.O
        ss = self.newstat()
        O.act(xh[:], xt[:], AF.Square, accum_out=ss)
        rs = self.rstd_from_ss(ss, D)
        O.ts("dve", xh[:], xt[:], rs, None, ALU.mult)
        pT = P.pb(tb, 1, BF16)
        pTv = pT[0].rearrange("p (k t) -> p k t", k=8)
        for k in range(8):
            O.tr((pTv[:, k, :], pT[1]), xh[:, k * 128:(k + 1) * 128], self.ident)
        S = self.mods[:, mod_base, :]
        G = self.mods[:, mod_base + 1, :]
        for k in range(8):
            dst = uT_slice_fn(k)
            if (k + ei) % 2 == 0:
                O.act(dst, (pTv[:, k, :], pT[1]), AF.Identity, bias=S[:, k:k + 1], scale=G[:, k:k + 1])
            else:
                O.ts("dve", dst, (pTv[:, k, :], pT[1]), G[:, k:k + 1], S[:, k:k + 1], ALU.mult, ALU.add)

    def da_mixer(self):
        P, O = self.P, self.O
        li = self.li
        lam_init = 0.8 - 0.6 * math.exp(-0.3 * li)
        w_qkv = self.din("w_qkv", [D, 3 * D])
        w_o = self.din("w_o", [D, D])
        lamv = self.din("lamv", [1, 256])
        subln = self.din("subln", [128, 1])
        cosT = self.din("cosT", [128, SEQ])
        sinT = self.din("sinT", [128, SEQ])
        NK = SEQ + CTX
        QT = self.dscr("QT", [8, 128, HALF], BF16)
        QTc = self.dscr("QTc", [8, 128, CTX], BF16)
        KT = self.dscr("KT", [8, 128, NK], BF16)
        Vs = self.dscr("Vs", [NK, D], BF16)

        lt = P.sb("lamt", [128, 256], F32)
        O.dma("sp", lt[:], lamv.broadcast_to([128, 256]))
        lp = P.sb("lamp", [128, 128], F32)
        O.tt("dve", lp[:, 0:64], lt[:, 0:64], lt[:, 64:128], ALU.mult)
        O.tt("dve", lp[:, 64:128], lt[:, 128:192], lt[:, 192:256], ALU.mult)
        lsc = P.sb("lsc", [128, 8], F32)
        O.red("dve", lsc[:, 0:1], lp[:, 0:64], ALU.add)
        O.red("dve", lsc[:, 1:2], lp[:, 64:128], ALU.add)
        O.act(lsc[:, 2:4], lsc[:, 0:2], AF.Exp)
        O.stt("dve", lsc[:, 4:5], lsc[:, 3:4], -lam_init, lsc[:, 2:3], ALU.add, ALU.subtract)
        neg_lam = lsc[:, 4:5]
        sg = P.sb("sublng", [128, 2], F32)
        O.dma("sp", sg[:, 0:1], subln)
        O.ts("dve", sg[:, 1:2], sg[:, 0:1], 1.0 - lam_init, None, ALU.mult)
        subg = sg[:, 1:2]

        m1 = P.mark()
        wq = P.sb("wqkv", [128, 8, 3 * D], BF16)
        wv = w_qkv.rearrange("(k p) n -> p k n", p=128)
        O.dma("pool", (wq[:], ["wqkv0", "wqkv1", "wqkv2"]), wv)
        xts = [P.sb(f"xt{i}", [128, D], F32) for i in range(3)]
        xhs = [P.sb(f"xh{i}", [128, D], BF16) for i in range(2)]
        uTs = [P.sb(f"uT{i}", [128, 8, 512], BF16) for i in range(2)]
        cs = [P.sb(f"cos{i}", [128, 512], F32) for i in range(2)]
        sn = [P.sb(f"sin{i}", [128, 512], F32) for i in range(2)]
        kb = [P.sb(f"kb{i}", [128, 512], BF16) for i in range(2)]
        t1 = [P.sb(f"ropt1_{i}", [128, 512], F32) for i in range(2)]
        t2 = [P.sb(f"ropt2_{i}", [128, 512], F32) for i in range(2)]
        kst = [P.sb(f"kst{i}", [128, 8, 512], BF16) for i in range(2)]
        qst = [P.sb(f"qst{i}", [128, 8, 512], BF16) for i in range(2)]
        vst = [P.sb(f"vst{i}", [128, D], BF16) for i in range(2)]

        groups = [("ctx", 0, CTX, 0 if self.want_ctx else None)]
        for g in range(SEQ // 512):
            groups.append(("lat", g * 512, 512, g * 512 if g < HALF // 512 else None))
        self._cnt = 0
        self._pbank = 0

        def qk_proj(gi, ub, ntok, rope, colbase, stage, dstT, dcol0):
            for h in range(8):
                pk = P.pb(2 + self._pbank % 4)
                self._pbank += 1
                pkv = (pk[0][:, :ntok], pk[1])
                for k in range(8):
                    O.mm(pkv, (wq[:, k, colbase + h * 128:colbase + (h + 1) * 128], f"wqkv{colbase // D}"),
                         ub[:, k, :ntok], start=(k == 0), stop=(k == 7))
                if not rope:
                    O.cp("act", stage[:, h, :ntok], pkv)
                    continue
                kbt = kb[h % 2]
                O.cp("act", kbt[:, :ntok], pkv)
                psw = P.pb(6 + h % 2)
                pswv = (psw[0][:, :ntok], psw[1])
                O.mm(pswv, self.perm, kbt[:, :ntok])
                O.tt("dve", t1[h % 2][:, :ntok], kbt[:, :ntok], cs[gi % 2][:, :ntok], ALU.mult)
                O.tt("dve", t2[h % 2][:, :ntok], pswv, sn[gi % 2][:, :ntok], ALU.mult)
                O.tt("pool", stage[:, h, :ntok], t1[h % 2][:, :ntok], t2[h % 2][:, :ntok], ALU.add)
            O.dma("sp", (_ap(dstT)[:, :, dcol0:dcol0 + ntok].rearrange("h p t -> p h t"), dstT[1]), stage[:, :, :ntok])

        for gi, (kind, r0, ntok, own_q0) in enumerate(groups):
            ub = uTs[gi % 2]
            nt = ntok // 128
            for t in range(nt):
                c = self._cnt
                self._cnt += 1
                xt = xts[c % 3]
                xh = xhs[c % 2]
                self.load_tile(xt, "ctx" if kind == "ctx" else "seq", r0 // 128 + t)
                self.norm_uT(xt, xh, lambda k, ub=ub, t=t: ub[:, k, t * 128:(t + 1) * 128],
                             4 if kind == "ctx" else 0, c % 2, c)
            rope = kind == "lat"
            if rope:
                O.dma("sp", cs[gi % 2][:, :ntok], cosT[:, r0:r0 + ntok])
                O.dma("sp", sn[gi % 2][:, :ntok], sinT[:, r0:r0 + ntok])
            kcol0 = 0 if kind == "ctx" else CTX + r0
            qk_proj(gi, ub, ntok, rope, D, kst[gi % 2], (KT, f"KT{gi}"), kcol0)
            if own_q0 is not None:
                if kind == "ctx":
                    qk_proj(gi, ub, ntok, rope, 0, qst[gi % 2], (QTc, "QTc"), 0)
                else:
                    qk_proj(gi, ub, ntok, rope, 0, qst[gi % 2], (QT, f"QT{own_q0 // 512}"), own_q0)
            for t in range(nt):
                vt = vst[t % 2]
                for cb in range(2):
                    pv = P.pb(2 + self._pbank % 4)
                    self._pbank += 1
                    for k in range(8):
                        O.mm(pv, ub[:, k, t * 128:(t + 1) * 128],
                             (wq[:, k, 2 * D + cb * 512:2 * D + (cb + 1) * 512], "wqkv2"),
                             start=(k == 0), stop=(k == 7))
                    O.cp("act" if cb == 0 else "dve", vt[:, cb * 512:(cb + 1) * 512], pv)
                O.dma("sp", (Vs[kcol0 + t * 128:kcol0 + (t + 1) * 128, :], f"Vs{gi}"), vt[:])
        P.release(m1)
        P.release_hi()
        ngrp = len(groups)

        m2 = P.mark()
        onT = P.sb("onT", [128, 8, HALF], BF16)
        onTc = P.sb("onTc", [128, 8, CTX], BF16)
        ktb = [P.sb(f"ktb{i}", [128, NK], BF16) for i in range(2)]
        vtb = [P.sb(f"vtb{i}", [128, NK // 128, 128], BF16) for i in range(2)]
        qtb = [P.sb(f"qtb{i}", [128, 512], BF16) for i in range(3)]
        ptb = [P.sb(f"ptb{i}", [128, 2, 512], BF16) for i in range(3)]
        fw = [P.sb(f"fin{i}", [128, 512], F32) for i in range(4)]
        kt_keys = [f"KT{gi}" for gi in range(ngrp)]
        vs_keys = [f"Vs{gi}" for gi in range(ngrp)]
        nkt = NK // 128
        qcount = 0
        for h in range(8):
            kt_sb = ktb[h % 2]
            v_sb = vtb[h % 2]
            O.dma("sp", kt_sb[:], (KT[h], kt_keys))
            O.dma("sp", v_sb[:], (Vs[:, h * 128:(h + 1) * 128].rearrange("(t p) v -> p t v", p=128), vs_keys))
            qgroups = [("lat", q0, 512, 0, nkt) for q0 in range(0, HALF, 512)]
            if self.want_ctx:
                qgroups.append(("ctx", 0, CTX, 0, CTX // 128))
            for (qk, q0, nq, kt0, kt1) in qgroups:
                qt = qtb[qcount % 3]
                qcount += 1
                if qk == "lat":
                    O.dma("sp", qt[:, :nq], (QT[h, :, q0:q0 + nq], f"QT{q0 // 512}"))
                else:
                    O.dma("sp", qt[:, :nq], (QTc[h], "QTc"))
                po = P.pb(4, 2)
                pd = P.pb(6, 2)
                def issue_qk(kt):
                    psq = P.pb(2 * (kt % 2), 2)
                    for mp in range(2):
                        O.mm((psq[0][:, mp, :nq], psq[1]), kt_sb[mp * 64:(mp + 1) * 64, kt * 128:(kt + 1) * 128],
                             qt[mp * 64:(mp + 1) * 64, :nq])
                issue_qk(kt0)
                for kt in range(kt0, kt1):
                    if kt + 1 < kt1:
                        issue_qk(kt + 1)
                    psn = P.pb(2 * (kt % 2), 2)
                    pt = ptb[kt % 3]
                    O.act(pt[:, :, :nq], (psn[0][:, :, :nq], psn[1]), AF.Exp, scale=DA_SCALE)
                    for mp in range(2):
                        O.mm((po[0][:, mp, :nq], po[1]), v_sb[:, kt, :], pt[:, mp, :nq],
                             start=(kt == kt0), stop=(kt == kt1 - 1))
                    for mp in range(2):
                        O.mm((pd[0][:, mp, :nq], pd[1]), self.ones_bf, pt[:, mp, :nq],
                             start=(kt == kt0), stop=(kt == kt1 - 1))
                r0t, w0t, w1t, ot = fw
                O.recip(r0t[:, :nq], (pd[0][:, 0, :nq], pd[1]))
                O.tt("dve", w0t[:, :nq], (po[0][:, 0, :nq], po[1]), r0t[:, :nq], ALU.mult)
                O.recip(r0t[:, :nq], (pd[0][:, 1, :nq], pd[1]))
                O.tt("dve", w1t[:, :nq], (po[0][:, 1, :nq], po[1]), r0t[:, :nq], ALU.mult)
                O.stt("dve", ot[:, :nq], w1t[:, :nq], neg_lam, w0t[:, :nq], ALU.mult, ALU.add)
                O.act(w0t[:, :nq], ot[:, :nq], AF.Square)
                pss = P.pb(0)
                O.mm((pss[0][:, :nq], pss[1]), self.ones_f, w0t[:, :nq])
                O.act(w1t[:, :nq], (pss[0][:, :nq], pss[1]), AF.Sqrt, bias=self.eps_t, scale=1.0 / 128)
                O.recip(r0t[:, :nq], w1t[:, :nq])
                O.tt("dve", ot[:, :nq], ot[:, :nq], r0t[:, :nq], ALU.mult)
                if qk == "lat":
                    dst = (onT[:, h, q0:q0 + nq], f"onT{h}")
                else:
                    dst = (onTc[:, h, :], f"onTc{h}")
                O.ts("dve", dst, ot[:, :nq], subg, None, ALU.mult)
        wo = P.sb("wo", [128, 8, D], BF16)
        O.dma("pool", wo[:], w_o.rearrange("(k p) n -> p k n", p=128))
        self.resid_phase(lambda h, t: (onT[:, h, t * 128:(t + 1) * 128], f"onT{h}"),
                         lambda h, t: (onTc[:, h, t * 128:(t + 1) * 128], f"onTc{h}"),
                         lambda h: wo[:, h, :], 8)
        P.release(m2)

    def resid_setup(self):
        P, O = self.P, self.O
        R = {}
        R["g1"] = [P.sb(f"g1_{i}", [128, D], F32) for i in range(2)]
        O.dma("sp", R["g1"][0][:], (self.gates[0], "gates0"))
        O.dma("sp", R["g1"][1][:], (self.gates[1], "gates1"))
        R["xts"] = [P.sb(f"rxt{i}", [128, D], F32) for i in range(3)]
        R["tmp"] = [P.sb(f"rtmp{i}", [128, D], F32) for i in range(2)]
        R["junk"] = P.sb("rjunk", [128, D], BF16)
        R["i"] = 0
        return R

    def resid_tile(self, R, kind, t, lhsT_fn, w_rows, nchunk, banks=(0, 2)):
        P, O = self.P, self.O
        i = R["i"]
        R["i"] += 1
        po = P.pb(banks[i % 2], 2)
        for cb in range(2):
            for h in range(nchunk):
                O.mm((po[0][:, cb, :], po[1]), lhsT_fn(h), _ap(w_rows(h))[:, cb * 512:(cb + 1) * 512],
                     start=(h == 0), stop=(h == nchunk - 1))
        xt = R["xts"][i % 3]
        self.load_tile(xt, "seq" if kind == "lat" else "ctx", t)
        ss = self.newstat()
        O.act(R["junk"][:].rearrange("p (a b) -> p a b", a=2), po, AF.Square, accum_out=ss)
        rs = self.rstd_from_ss(ss, D)
        tm = R["tmp"][i % 2]
        O.stt("dve", tm[:].rearrange("p (a b) -> p a b", a=2), po, rs,
              R["g1"][0 if kind == "lat" else 1][:].rearrange("p (a b) -> p a b", a=2), ALU.mult, ALU.mult)
        O.tt("pool", tm[:], tm[:], xt[:], ALU.add)
        row0 = t * 128 if kind == "lat" else HALF + t * 128
        O.dma("sp", (self.hmid[row0:row0 + 128, :], f"hmid{row0 // 512}"), tm[:])

    def resid_phase(self, lat_lhsT, ctx_lhsT, w_rows, nchunk):
        P = self.P
        m = P.mark()
        R = self.resid_setup()
        tiles = [("lat", t) for t in range(HALF // 128)]
        if self.want_ctx:
            tiles += [("ctx", t) for t in range(CTX // 128)]
        for (kind, t) in tiles:
            fn = lat_lhsT if kind == "lat" else ctx_lhsT
            self.resid_tile(R, kind, t, lambda h, fn=fn, t=t: fn(h, t), w_rows, nchunk)
        P.release(m)

    def sw_mixer(self):
        P, O = self.P, self.O
        w_qkv = self.din("w_qkv", [D, 1536])
        w_o = self.din("w_o", [D, D])
        sinkv = self.din("sinkv", [2, 8])
        emask_d = self.din("emask", [128, 2])
        NW = HALF + 256
        cosT = self.din("cosT", [128, NW])
        sinT = self.din("sinT", [128, NW])
        NKW = CTX + NW
        se = P.sb("sinkexp", [128, 8], F32)
        O.dma("sp", se[0:64, :], sinkv[0:1, :].broadcast_to([64, 8]))
        O.dma("sp", se[64:128, :], sinkv[1:2, :].broadcast_to([64, 8]))
        O.act(se[:], se[:], AF.Exp)
        emask = P.sb("emask_sb", [128, 2], F32)
        O.dma("sp", emask[:], emask_d)
        m_prev = self.cbf[:, 384:512]
        m_next = self.cbf[:, 512:640]
        m2 = P.mark()
        m3 = P.mark()
        QTs = P.sb("QTs", [128, 8, HALF + CTX], BF16)
        KTs = P.sb("KTs", [128, 4, NKW], BF16)
        Vs = P.sb("Vsw", [128, NKW // 128, 256], BF16)
        m1 = P.mark()
        wq = P.sb("wqkv", [128, 8, 1536], BF16)
        O.dma("pool", wq[:], w_qkv.rearrange("(k p) n -> p k n", p=128))
        wkd = P.sb("wkdup", [128, 8, 4, 128], BF16)
        for hk in range(4):
            for dup in range(2):
                O.cp("dve" if dup == 0 else "pool", (wkd[:, :, hk, dup * 64:(dup + 1) * 64], f"wkd{hk}"),
                     wq[:, :, D + hk * 64:D + (hk + 1) * 64])
        xts = [P.sb(f"xt{i}", [128, D], F32) for i in range(2)]
        xhs = [P.sb(f"xh{i}", [128, D], BF16) for i in range(2)]
        uTs = [P.sb(f"uT{i}", [128, 8, 512], BF16) for i in range(1)]
        cs = [P.sb(f"cos{i}", [128, 512], F32) for i in range(2)]
        sn = [P.sb(f"sin{i}", [128, 512], F32) for i in range(2)]
        kb = [P.sb(f"kb{i}", [128, 512], BF16) for i in range(2)]
        t1 = [P.sb(f"ropt1_{i}", [128, 512], F32) for i in range(2)]
        t2 = [P.sb(f"ropt2_{i}", [128, 512], F32) for i in range(2)]
        groups = [("ctx", "ctx", 0, CTX, None, 0, HALF)]
        for g in range(HALF // 512):
            groups.append(("lat", "seq", g * 512, 512, g * 512, CTX + 128 + g * 512, g * 512))
        groups.append(("lat", "halo", 0, 128, HALF, CTX, None))
        groups.append(("lat", "halo", 128, 128, HALF + 128, CTX + 128 + HALF, None))
        cnt = 0
        pbank = 0
        for gi, (kind, src, r0, ntok, rc0, kc0, qc0) in enumerate(groups):
            ub = uTs[0]
            for t in range(ntok // 128):
                xt = xts[cnt % 2]
                xh = xhs[cnt % 2]
                self.load_tile(xt, src, r0 // 128 + t)
                self.norm_uT(xt, xh, lambda k, ub=ub, t=t: ub[:, k, t * 128:(t + 1) * 128],
                             4 if kind == "ctx" else 0, cnt % 2, cnt)
                cnt += 1
            rope = kind == "lat"
            if rope:
                O.dma("sp", cs[gi % 2][:, :ntok], cosT[:, rc0:rc0 + ntok])
                O.dma("sp", sn[gi % 2][:, :ntok], sinT[:, rc0:rc0 + ntok])
            jobs = [("k", hk) for hk in range(4)]
            if qc0 is not None:
                jobs += [("q", c) for c in range(8)]
            for ji, (what, c) in enumerate(jobs):
                pk = P.pb(2 + pbank % 4)
                pbank += 1
                pkv = (pk[0][:, :ntok], pk[1])
                for k in range(8):
                    if what == "k":
                        lw = (wkd[:, k, c, :], f"wkd{c}")
                    else:
                        lw = wq[:, k, c * 128:(c + 1) * 128]
                    O.mm(pkv, lw, ub[:, k, :ntok], start=(k == 0), stop=(k == 7))
                if what == "k":
                    dst = (KTs[:, c, kc0:kc0 + ntok], f"KTs{gi}")
                else:
                    dst = (QTs[:, c, qc0:qc0 + ntok], f"QTs{gi}")
                if not rope:
                    O.cp("act", dst, pkv)
                    continue
                kbt = kb[ji % 2]
                O.cp("act", kbt[:, :ntok], pkv)
                psw = P.pb(6 + ji % 2)
                pswv = (psw[0][:, :ntok], psw[1])
                O.mm(pswv, self.perm, kbt[:, :ntok])
                O.tt("dve", t1[ji % 2][:, :ntok], kbt[:, :ntok], cs[gi % 2][:, :ntok], ALU.mult)
                O.tt("dve", t2[ji % 2][:, :ntok], pswv, sn[gi % 2][:, :ntok], ALU.mult)
                O.tt("pool", dst, t1[ji % 2][:, :ntok], t2[ji % 2][:, :ntok], ALU.add)
            for t in range(ntok // 128):
                pv = P.pb(2 + pbank % 4)
                pbank += 1
                pvv = (pv[0][:, :256], pv[1])
                for k in range(8):
                    O.mm(pvv, ub[:, k, t * 128:(t + 1) * 128], wq[:, k, D + 256:D + 512], start=(k == 0), stop=(k == 7))
                O.cp("act", (Vs[:, kc0 // 128 + t, :], f"Vsw{gi}"), pvv)
        P.release(m1)
        ngrp = len(groups)
        allk = [f"KTs{gi}" for gi in range(ngrp)]
        allv = [f"Vsw{gi}" for gi in range(ngrp)]
        allq = [f"QTs{gi}" for gi in range(ngrp) if groups[gi][6] is not None]

        P.release_hi()
        oT = P.sb("oTs", [128, 8, HALF + CTX], BF16, hi=True)
        ptb = [P.sb(f"ptb{i}", [128, 2, 256], BF16) for i in range(3)]
        dn = [P.sb(f"dn{i}", [128, 256], F32) for i in range(2)]
        qtiles = [("lat", t) for t in range(HALF // 128)] + [("ctx", t) for t in range(CTX // 128)]
        it = 0
        for (qk, t) in qtiles:
            if qk == "lat":
                ktl = [(0, None, None), (1, None, None), (2 + t, m_prev, 0 if t == 0 else None), (3 + t, None, None),
                       (4 + t, m_next, 1 if t == HALF // 128 - 1 else None)]
                qc = t * 128
            else:
                ktl = [(0, None, None), (1, None, None)]
                qc = HALF + t * 128
            for hk in range(4):
                po = P.pb(4 + 2 * (it % 2))
                pd = P.pb(5 + 2 * (it % 2))
                it += 1
                for ki, (kt, msk, em) in enumerate(ktl):
                    pt = ptb[ki % 3]
                    first, last = ki == 0, ki == len(ktl) - 1
                    pss = P.pb(2 * (ki % 2), 2)
                    for par in range(2):
                        O.mm((pss[0][:, par, 0:256], pss[1]),
                             (KTs[par * 64:(par + 1) * 64, hk, kt * 128:(kt + 1) * 128], allk),
                             (QTs[par * 64:(par + 1) * 64, hk * 2:hk * 2 + 2, qc:qc + 128], allq))
                    O.act(pt[:], (pss[0][:, :, 0:256], pss[1]), AF.Exp, scale=DA_SCALE)
                    if msk is not None:
                        ptv = pt[:].rearrange("p a (g q) -> p (a g) q", g=2)
                        O.tt("dve", ptv, ptv, msk.unsqueeze(1).to_broadcast([128, 4, 128]), ALU.mult)
                        if em is not None:
                            O.ts("dve", pt[:], pt[:], emask[:, em:em + 1], None, ALU.mult)
                    for par in range(2):
                        O.mm((po[0][par * 64:(par + 1) * 64, 0:256], po[1]),
                             (Vs[:, kt, hk * 64:(hk + 1) * 64], allv), pt[:, par, :], start=first, stop=last)
                    for par in range(2):
                        O.mm((pd[0][par * 64:(par + 1) * 64, 0:256], pd[1]),
                             self.ones_bf[:, 0:64], pt[:, par, :], start=first, stop=last)
                d = dn[hk % 2]
                dv = d[:].rearrange("p (g q) -> p g q", g=2)
                O.tt("dve", dv, (pd[0][:, 0:256].rearrange("p (g q) -> p g q", g=2), pd[1]),
                     se[:, hk * 2:hk * 2 + 2].unsqueeze(2).to_broadcast([128, 2, 128]), ALU.add)
                O.recip(d[:], d[:])
                O.tt("dve", (oT[:, hk * 2:hk * 2 + 2, qc:qc + 128], f"oT{qk}{t}"),
                     (po[0][:, 0:256].rearrange("p (g q) -> p g q", g=2), po[1]), dv, ALU.mult)
        P.release(m3)
        wo = P.sb("wo", [128, 8, D], BF16)
        O.dma("pool", wo[:], w_o.rearrange("(k p) n -> p k n", p=128))
        self.resid_phase(lambda h, t: (oT[:, h, t * 128:(t + 1) * 128], f"oTlat{t}"),
                         lambda h, t: (oT[:, h, HALF + t * 128:HALF + (t + 1) * 128], f"oTctx{t}"),
                         lambda h: wo[:, h, :], 8)
        P.release(m2)
        P.release_hi()

    def ssd_mixer(self):
        P, O = self.P, self.O
        NT = SEQ + CTX
        w_in = self.din("w_in", [D, 5184])
        convw = self.din("convw", [128, 24, 5])
        convb = self.din("convb", [128, 24])
        ssdrow = self.din("ssdrow", [1, 160])
        normw = self.din("normw", [1, 2048])
        w_out = self.din("w_out", [2048, D])
        esel_d = self.din("esel", [32, 4096], BF16)
        xs_d = self.dscr("xs_d", [NT, 2048], BF16)
        bs_d = self.dscr("bs_d", [NT, 512], BF16)
        bt_d = self.dscr("bt_d", [128, 4, NT], BF16)
        ct_d = self.dscr("ct_d", [128, 4, NT], BF16)
        dt_d = self.dscr("dt_d", [NT, 64], F32)
        zs_d = self.dscr("zs_d", [NT, 2048], BF16)
        yf_d = self.dscr("yf_d", [NT, 2048], F32)
        nm = [self.cbf[:, 768:896], self.cbf[:, 896:1024]]
        tri = [self.cf32[:, 256:384], self.cf32[:, 384:512]]

        cw = P.sb("convw_sb", [128, 24, 5], F32)
        cbv = P.sb("convb_sb", [128, 24], F32)
        O.dma("sp", cw[:], convw)
        O.dma("sp", cbv[:], convb)
        row = P.sb("ssdrow_sb", [128, 160], F32)
        O.dma("sp", row[:], ssdrow.broadcast_to([128, 160]))
        Abc = P.sb("Abc", [128, 64], F32)
        O.act(Abc[:], row[:, 0:64], AF.Exp)
        O.ts("dve", Abc[:], Abc[:], -1.0, None, ALU.mult)
        dtb = row[:, 64:128]
        dskip = row[:, 128:160]

        mA = P.mark()
        win = P.sb("w_in_sb", [128, 8, 5184], BF16)
        wv = w_in.rearrange("(k p) n -> p k n", p=128)
        pieces = [(0, 1024), (1024, 2048), (2048, 3072), (3072, 4096), (4096, 5184)]

        def wkey(col):
            for i, (a, b) in enumerate(pieces):
                if a <= col < b:
                    return f"win{i}"
        O.dma("pool", (win[:], [f"win{i}" for i in range(len(pieces))]), wv)
        uw = [P.sb(f"uw{i}", [128, 8, 516], BF16) for i in range(3)]
        xts = [P.sb(f"xt{i}", [128, D], F32) for i in range(3)]
        xhs = [P.sb(f"xh{i}", [128, D], BF16) for i in range(2)]
        raw = [P.sb(f"raw{i}", [128, 516], F32) for i in range(2)]
        acc = [P.sb(f"cacc{i}", [128, 512], F32) for i in range(2)]
        ctmp = P.sb("ctmp", [128, 512], F32)
        cvo = P.sb("cvo", [128, 24, 512], BF16)
        xstg = [P.sb(f"xstg{i}", [128, 2048], BF16) for i in range(2)]
        bstg = [P.sb(f"bstg{i}", [128, 512], BF16) for i in range(2)]
        zstg = [P.sb(f"zstg{i}", [128, 2048], BF16) for i in range(2)]
        dstg = [P.sb(f"dstg{i}", [128, 64], F32) for i in range(2)]
        dtm = [P.sb(f"dtm{i}", [128, 64], F32) for i in range(2)]
        self._cnt = 0

        def S1(src, r0, ntok, buf, first, last, prevbuf, nextbuf):
            for t in range(ntok // 128):
                c = self._cnt
                self._cnt += 1
                xt = xts[c % 3]
                xh = xhs[c % 2]
                self.load_tile(xt, src, r0 // 128 + t)
                self.norm_uT(xt, xh, lambda k, t=t: buf[:, k, 2 + t * 128:2 + (t + 1) * 128],
                             4 if src == "ctx" else 0, 4 + c % 2, c)
            if first:
                O.memset("pool", buf[:, :, 0:2], 0.0)
            else:
                O.cp("pool", prevbuf[:, :, 514:516], buf[:, :, 2:4])
            if last:
                O.memset("pool", buf[:, :, 2 + ntok:4 + ntok], 0.0)
            else:
                O.cp("pool", nextbuf[:, :, 0:2], buf[:, :, ntok:ntok + 2])

        def S2(r0, ntok, buf, need_c):
            W = ntok + 4
            half = W // 2
            nch = 24 if need_c else 20
            for c in range(nch):
                col0 = 2048 + c * 128
                pr = P.pb(2 * (c % 2), 2)
                for hh in range(2):
                    for k in range(8):
                        O.mm((pr[0][:, hh, 0:half], pr[1]), (win[:, k, col0:col0 + 128], wkey(col0)),
                             buf[:, k, hh * half:(hh + 1) * half], start=(k == 0), stop=(k == 7))
                rw = raw[c % 2]
                O.cp("act", rw[:, 0:W].rearrange("p (a b) -> p a b", a=2), (pr[0][:, :, 0:half], pr[1]))
                ac = acc[c % 2]
                if c % 3 != 2:
                    O.ts("dve", ac[:, :ntok], rw[:, 0:ntok], cw[:, c, 0:1], cbv[:, c:c + 1], ALU.mult, ALU.add)
                    for j in range(1, 5):
                        O.stt("dve", ac[:, :ntok], rw[:, j:j + ntok], cw[:, c, j:j + 1], ac[:, :ntok], ALU.mult, ALU.add)
                else:
                    O.ts("pool", ac[:, :ntok], rw[:, 0:ntok], cw[:, c, 0:1], cbv[:, c:c + 1], ALU.mult, ALU.add)
                    for j in range(1, 5):
                        O.ts("pool", ctmp[:, :ntok], rw[:, j:j + ntok], cw[:, c, j:j + 1], None, ALU.mult)
                        O.tt("pool", ac[:, :ntok], ac[:, :ntok], ctmp[:, :ntok], ALU.add)
                O.act((cvo[:, c, :ntok], f"cvo{c}"), ac[:, :ntok], AF.Silu)
            if need_c:
                O.dma("sp", (bt_d[:, :, r0:r0 + ntok], f"bt{r0 // 512}"),
                      (cvo[:, 16:20, :ntok], [f"cvo{c}" for c in range(16, 20)]), grp="cvo_st")
                O.dma("sp", (ct_d[:, :, r0:r0 + ntok], f"ct{r0 // 512}"),
                      (cvo[:, 20:24, :ntok], [f"cvo{c}" for c in range(20, 24)]), grp="cvo_st")
            for t in range(ntok // 128):
                xst = xstg[t % 2]
                bst = bstg[t % 2]
                tb = [P.pb(4, 1, BF16), P.pb(5, 1, BF16), P.pb(6, 1, BF16)]
                tv = [(x[0].rearrange("p (k t) -> p k t", k=8), x[1]) for x in tb]
                for c in range(20):
                    O.tr((tv[c // 8][0][:, c % 8, :], tv[c // 8][1]), (cvo[:, c, t * 128:(t + 1) * 128], f"cvo{c}"), self.ident)
                O.cp("act", xst[:, 0:1024], tb[0])
                O.cp("dve", xst[:, 1024:2048], tb[1])
                O.cp("act", bst[:], (tb[2][0][:, 0:512], tb[2][1]))
                rr = r0 + t * 128
                O.dma("sp", (xs_d[rr:rr + 128, :], f"xs{rr // 512}"), xst[:])
                O.dma("sp", (bs_d[rr:rr + 128, :], f"bs{rr // 512}"), bst[:])
                pdt = P.pb(7)
                pdv = (pdt[0][:, 0:64], pdt[1])
                for k in range(8):
                    O.mm(pdv, buf[:, k, 2 + t * 128:2 + (t + 1) * 128], (win[:, k, 5120:5184], "win4"),
                         start=(k == 0), stop=(k == 7))
                dm = dtm[t % 2]
                O.tt("dve", dm[:], pdv, dtb, ALU.add)
                O.act(dm[:], dm[:], AF.Exp)
                ds = dstg[t % 2]
                O.act(ds[:], dm[:], AF.Ln, bias=1.0)
                O.dma("sp", (dt_d[rr:rr + 128, :], f"dt{rr // 512}"), ds[:])
                if need_c:
                    zt = zstg[t % 2]
                    for cb4 in range(4):
                        pz = P.pb(7 if cb4 % 2 == 0 else 6)
                        for k in range(8):
                            O.mm(pz, buf[:, k, 2 + t * 128:2 + (t + 1) * 128],
                                 (win[:, k, cb4 * 512:(cb4 + 1) * 512], wkey(cb4 * 512)), start=(k == 0), stop=(k == 7))
                        O.act(zt[:, cb4 * 512:(cb4 + 1) * 512], pz, AF.Silu)
                    O.dma("sp", (zs_d[rr:rr + 128, :], f"zs{rr // 512}"), zt[:])

        ng = SEQ // 512
        for g in range(ng + 1):
            if g < ng:
                S1("seq", g * 512, 512, uw[g % 3], g == 0, g == ng - 1, uw[(g - 1) % 3], uw[(g + 1) % 3])
            if g >= 1:
                S2((g - 1) * 512, 512, uw[(g - 1) % 3], (g - 1) < HALF // 512)
        S1("ctx", 0, CTX, uw[0], True, True, None, None)
        S2(SEQ, CTX, uw[0], True)
        P.release(mA)
        P.release_hi()

        mB = P.mark()
        esel = P.sb("esel_sb", [32, 32, 128], BF16)
        O.dma("sp", esel[:], esel_d.rearrange("k (t s) -> k t s", t=32))
        nwb = P.sb("normw_bc", [128, 2048], F32)
        O.dma("sp", nwb[:], normw.broadcast_to([128, 2048]))
        wout = P.sb("wout_ssd", [128, 16, D], BF16)
        O.dma("pool", wout[:], w_out.rearrange("(k p) n -> p k n", p=128))
        st = P.sb("sst", [128, 4, 512], F32)
        stb = P.sb("sstb", [128, 4, 512], BF16)
        xs_t = [P.sb(f"xs_t{i}", [128, 2048], BF16) for i in range(2)]
        bs_t = [P.sb(f"bs_t{i}", [128, 512], BF16) for i in range(2)]
        dt_t = [P.sb(f"dt_t{i}", [128, 64], F32) for i in range(2)]
        bt_t = [P.sb(f"bt_t{i}", [128, 4, 128], BF16) for i in range(2)]
        ct_t = [P.sb(f"ct_t{i}", [128, 4, 128], BF16) for i in range(2)]
        sm = P.sb("ssm", [128, 8, 32], F32)
        hilo = P.sb("hilo", [32, 2, 128], BF16)
        xdt = P.sb("xdt", [128, 2048], BF16)
        xdtd = P.sb("xdtd", [128, 2048], BF16)
        cbs = P.sb("cbs", [128, 512], F32)
        dec = [P.sb(f"dec{i}", [128, 128], F32) for i in range(3)]
        wts = [P.sb(f"wts{i}", [128, 128], BF16) for i in range(3)]
        ytmp = P.sb("ytmp", [128, 512], F32)
        ybuf = [P.sb(f"ybuf{i}", [128, 2048], F32) for i in range(2)]
        yf_t = P.sb("yf_t", [128, 2048], F32)
        zs_t = P.sb("zs_t", [128, 2048], BF16)
        tmp2 = P.sb("tmp2k", [128, 2048], F32)
        ynb = P.sb("ynb", [128, 2048], BF16)
        ynT = P.sb("ynT", [128, 16, 128], BF16)
        fjunk = P.sb("sjunk", [128, 512], BF16)
        R = self.resid_setup()
        self._ci = 0

        def core(dirn, row0, need_y, yb):
            i = self._ci
            self._ci += 1
            g5 = row0 // 512
            xt_, bt_, dtt = xs_t[i % 2], bs_t[i % 2], dt_t[i % 2]
            O.dma("sp", xt_[:], (xs_d[row0:row0 + 128, :], f"xs{g5}"))
            O.dma("sp", bt_[:], (bs_d[row0:row0 + 128, :], f"bs{g5}"))
            O.dma("sp", dtt[:], (dt_d[row0:row0 + 128, :], f"dt{g5}"))
            dtv = dtt[:, dirn * 32:(dirn + 1) * 32]
            a, acs, cdec, dtmp, dte, nacs, eacs = [sm[:, j, :] for j in range(7)]
            O.tt("dve", a, dtv, Abc[:, dirn * 32:(dirn + 1) * 32], ALU.mult)
            pm = P.pb(0)
            O.mm((pm[0][:, 0:32], pm[1]), tri[dirn], a)
            O.mm((pm[0][:, 32:64], pm[1]), self.ones_f, a)
            O.cp("act", acs, (pm[0][:, 0:32], pm[1]))
            O.act(cdec, (pm[0][:, 32:64], pm[1]), AF.Exp)
            O.tt("dve", dtmp, (pm[0][:, 32:64], pm[1]), acs, ALU.subtract)
            O.act(dte, dtmp, AF.Exp)
            x3 = xt_[:].rearrange("p (h q) -> p h q", h=32)
            O.tt("dve", xdt[:].rearrange("p (h q) -> p h q", h=32), x3,
                 dtv.unsqueeze(2).to_broadcast([128, 32, 64]), ALU.mult)
            O.tt("pool", xdtd[:].rearrange("p (h q) -> p h q", h=32), xdt[:].rearrange("p (h q) -> p h q", h=32),
                 dte.unsqueeze(2).to_broadcast([128, 32, 64]), ALU.mult)
            if need_y:
                btt, ctt = bt_t[i % 2], ct_t[i % 2]
                O.dma("sp", btt[:], (bt_d[:, :, row0:row0 + 128], f"bt{g5}"))
                O.dma("sp", ctt[:], (ct_d[:, :, row0:row0 + 128], f"ct{g5}"))
                O.mm((pm[0][0:32, 64:192], pm[1]), a, tri[dirn])
                O.cp("act", hilo[:, 0, :], (pm[0][0:32, 64:192], pm[1]))
                O.tt("dve", hilo[:, 1, :], (pm[0][0:32, 64:192], pm[1]), hilo[:, 0, :], ALU.subtract)
                O.ts("dve", nacs, acs, -1.0, None, ALU.mult)
                O.act(eacs, acs, AF.Exp)
                pcb = P.pb(3)
                for g in range(4):
                    O.mm((pcb[0][:, g * 128:(g + 1) * 128], pcb[1]), btt[:, g, :], ctt[:, g, :])
                O.cp("act", cbs[:], pcb)
                for g in range(4):
                    pyd = P.pb(6)
                    pyo = P.pb(7)
                    O.mm(pyo, ctt[:, g, :], stb[:, g, :])
                    for hh in range(8):
                        h = g * 8 + hh
                        psg = P.pb(4 + (h // 4) % 2)
                        pr = (psg[0][:, (h % 4) * 128:(h % 4 + 1) * 128], psg[1])
                        O.mm(pr, esel[:, h, :], hilo[:, 0, :], start=True, stop=False)
                        O.mm(pr, esel[:, h, :], hilo[:, 1, :], start=False, stop=False)
                        O.mm(pr, self.ident, nm[dirn], start=False, stop=True)
                        dc = dec[h % 3]
                        O.act(dc[:], pr, AF.Exp, bias=nacs[:, h:h + 1])
                        wt = wts[h % 3]
                        O.tt("dve", wt[:], dc[:], cbs[:, g * 128:(g + 1) * 128], ALU.mult)
                        O.mm((pyd[0][:, hh * 64:(hh + 1) * 64], pyd[1]), wt[:], xdt[:, h * 64:(h + 1) * 64])
                    O.tt("dve", ytmp[:].rearrange("p (h q) -> p h q", h=8),
                         (pyo[0].rearrange("p (h q) -> p h q", h=8), pyo[1]),
                         eacs[:, g * 8:(g + 1) * 8].unsqueeze(2).to_broadcast([128, 8, 64]), ALU.mult)
                    O.tt("dve", yb[:, g * 512:(g + 1) * 512], ytmp[:], pyd, ALU.add)
            for g in range(4):
                pcs = P.pb(1 + g % 2)
                O.mm(pcs, bt_[:, g * 128:(g + 1) * 128], xdtd[:, g * 512:(g + 1) * 512])
                sv = st[:, g, :].rearrange("p (h q) -> p h q", h=8)
                O.tt("pool", (sv, f"sst{g}"), (sv, f"sst{g}"),
                     cdec[:, g * 8:(g + 1) * 8].unsqueeze(2).to_broadcast([128, 8, 64]), ALU.mult)
                O.tt("dve", (st[:, g, :], f"sst{g}"), (st[:, g, :], f"sst{g}"), pcs, ALU.add)
                O.cp("act", (stb[:, g, :], f"sstb{g}"), (st[:, g, :], f"sst{g}"))

        def finalize(kind, t, row0, yb, xt_):
            g5 = row0 // 512
            O.dma("sp", yf_t[:], (yf_d[row0:row0 + 128, :], f"yf{g5}"))
            O.dma("sp", zs_t[:], (zs_d[row0:row0 + 128, :], f"zs{g5}"))
            O.tt("pool", yb[:], yb[:], yf_t[:], ALU.add)
            O.tt("dve", tmp2[:].rearrange("p (h q) -> p h q", h=32), xt_[:].rearrange("p (h q) -> p h q", h=32),
                 dskip.unsqueeze(2).to_broadcast([128, 32, 64]), ALU.mult)
            O.tt("pool", yb[:], yb[:], tmp2[:], ALU.add)
            O.tt("dve", yb[:], yb[:], zs_t[:], ALU.mult)
            ss4 = self.newstat(4)
            for g in range(4):
                O.act(fjunk[:], yb[:, g * 512:(g + 1) * 512], AF.Square, accum_out=(ss4[0][:, g:g + 1], ss4[1]))
            sd4 = self.newstat(4)
            O.act(sd4, ss4, AF.Sqrt, bias=self.eps_t, scale=1.0 / 512)
            rs4 = self.newstat(4)
            O.recip(rs4, sd4)
            for g in range(4):
                O.ts("dve" if g % 2 == 0 else "pool", yb[:, g * 512:(g + 1) * 512], yb[:, g * 512:(g + 1) * 512],
                     (rs4[0][:, g:g + 1], rs4[1]), None, ALU.mult)
            O.tt("dve", ynb[:], yb[:], nwb[:], ALU.mult)
            tb = [P.pb(4, 1, BF16), P.pb(5, 1, BF16)]
            tv = [(x[0].rearrange("p (k t) -> p k t", k=8), x[1]) for x in tb]
            for c in range(16):
                O.tr((tv[c // 8][0][:, c % 8, :], tv[c // 8][1]), ynb[:, c * 128:(c + 1) * 128], self.ident)
            O.cp("act", ynT[:, 0:8, :], tv[0])
            O.cp("dve", ynT[:, 8:16, :], tv[1])
            self.resid_tile(R, kind, t, lambda c: ynT[:, c, :], lambda c: wout[:, c, :], 16, banks=(6, 6))

        def zero_state():
            for g in range(4):
                O.memset("pool", (st[:, g, :], f"sst{g}"), 0.0)
                O.memset("pool", (stb[:, g, :], f"sstb{g}"), 0.0)

        zero_state()
        chain_f = [("ctx", t, SEQ + t * 128) for t in range(CTX // 128)] + [("lat", t, t * 128) for t in range(HALF // 128)]
        for n, (kind, t, row0) in enumerate(chain_f):
            yb = ybuf[n % 2]
            core(0, row0, True, yb)
            O.dma("sp", (yf_d[row0:row0 + 128, :], f"yf{row0 // 512}"), yb[:])
        zero_state()
        chain_b = [("ctx", t, SEQ + t * 128) for t in reversed(range(CTX // 128))]
        chain_b += [("oth", t, HALF + t * 128) for t in reversed(range(HALF // 128))]
        chain_b += [("lat", t, t * 128) for t in reversed(range(HALF // 128))]
        for n, (kind, t, row0) in enumerate(chain_b):
            yb = ybuf[n % 2]
            i = self._ci
            core(1, row0, kind != "oth", yb)
            if kind != "oth":
                finalize(kind, t, row0, yb, xs_t[i % 2])
        P.release(mB)

    def phase_ffn(self):
        P, O = self.P, self.O
        m = P.mark()
        win = P.sb("win", [128, 8, 2 * FFN], BF16)
        wout = P.sb("wout", [128, NFC, D], BF16)
        wiv = self.ffn_w_in.rearrange("(k p) n -> p k n", p=128)
        O.dma("pool", (win[:], [f"win{j}" for j in range(4)]), wiv)
        O.dma("pool", wout[:], self.ffn_w_out.rearrange("(k p) n -> p k n", p=128))
        g2 = [P.sb(f"g2_{i}", [128, D], F32) for i in range(2)]
        O.dma("sp", g2[0][:], (self.gates[2], "gates2"))
        O.dma("sp", g2[1][:], (self.gates[3], "gates3"))
        xts = [P.sb(f"fxt{i}", [128, D], F32) for i in range(2)]
        xhs = [P.sb(f"fxh{i}", [128, D], BF16) for i in range(2)]
        uT = P.sb("fuT", [128, 8, 512], BF16)
        hT = P.sb("fhT", [128, NFC, 512], BF16)
        sg = [P.sb(f"fsg{i}", [128, 512], F32) for i in range(2)]
        tmp = [P.sb(f"ftmp{i}", [128, D], F32) for i in range(2)]
        junk = P.sb("fjunk", [128, D], BF16)
        groups = [("lat", g * 512, 512) for g in range(HALF // 512)]
        if self.want_ctx:
            groups.append(("ctx", HALF, CTX))
        cnt = 0
        for gi, (kind, r0, ntok) in enumerate(groups):
            nt = ntok // 128
            for t in range(nt):
                xt = xts[cnt % 2]
                xh = xhs[cnt % 2]
                row0 = r0 + t * 128
                O.dma("sp", xt[:], (self.hmid[row0:row0 + 128, :], f"hmid{row0 // 512}"))
                self.norm_uT(xt, xh, lambda k, t=t: uT[:, k, t * 128:(t + 1) * 128],
                             2 if kind == "lat" else 6, 7 * (cnt % 2), cnt)
                cnt += 1
            for fc in range(NFC):
                pg = P.pb(1 + 2 * (fc % 2))
                pu = P.pb(2 + 2 * (fc % 2))
                pgv = (pg[0][:, :ntok], pg[1])
                puv = (pu[0][:, :ntok], pu[1])
                cg = fc * 128
                cu = FFN + fc * 128
                for k in range(8):
                    O.mm(pgv, (win[:, k, cg:cg + 128], f"win{cg // 1408}"), uT[:, k, :ntok], start=(k == 0), stop=(k == 7))
                for k in range(8):
                    O.mm(puv, (win[:, k, cu:cu + 128], f"win{cu // 1408}"), uT[:, k, :ntok], start=(k == 0), stop=(k == 7))
                st = sg[fc % 2]
                O.act(st[:, :ntok], pgv, AF.Silu)
                O.tt("dve", hT[:, fc, :ntok], st[:, :ntok], puv, ALU.mult)
            for t in range(nt):
                pf = P.pb(5, 2)
                for cb in range(2):
                    for fc in range(NFC):
                        O.mm((pf[0][:, cb, :], pf[1]), hT[:, fc, t * 128:(t + 1) * 128], wout[:, fc, cb * 512:(cb + 1) * 512],
                             start=(fc == 0), stop=(fc == NFC - 1))
                xt = xts[cnt % 2]
                cnt += 1
                row0 = r0 + t * 128
                O.dma("sp", xt[:], (self.hmid[row0:row0 + 128, :], f"hmid{row0 // 512}"))
                ss = self.newstat()
                O.act(junk[:].rearrange("p (a b) -> p a b", a=2), pf, AF.Square, accum_out=ss)
                rs = self.rstd_from_ss(ss, D)
                tm = tmp[t % 2]
                O.stt("dve", tm[:].rearrange("p (a b) -> p a b", a=2), pf, rs,
                      g2[0 if kind == "lat" else 1][:].rearrange("p (a b) -> p a b", a=2), ALU.mult, ALU.mult)
                O.tt("pool", tm[:], tm[:], xt[:], ALU.add)
                if kind == "lat":
                    O.dma("sp", (self.hout[0][row0:row0 + 128, :], self.hout[1]), tm[:])
                    if not self.last:
                        prv = P.pb(1, 2)
                        for cb in range(2):
                            O.mm((prv[0][:, cb, :], prv[1]), self.cf32[:, 512:640], tm[:, cb * 512:(cb + 1) * 512])
                        rvt = tmp[(t + 1) % 2]
                        O.cp("act", rvt[:].rearrange("p (a b) -> p a b", a=2), prv)
                        rr = HALF - 128 - row0
                        hr = self.hrev[rr // 512]
                        O.dma("sp", (hr[0][rr % 512:rr % 512 + 128, :], hr[1]), rvt[:])
                else:
                    O.dma("sp", (self.hctx_out[0][row0 - HALF:row0 - HALF + 128, :], self.hctx_out[1]), tm[:])
        P.release(m)


PAIRS = [[0, 1], [2, 3], [4, 5], [6, 7]]


class Fused:
    def __init__(self):
        nc = bass.Bass("TRN2", target_bir_lowering=False)
        self.nc = nc
        self.P = Prog(nc)
        self.O = Ops(self.P)
        self.ins = {}
        P, O = self.P, self.O

        def din(name, shape, dt=F32):
            t = nc.dram_tensor(name, list(shape), dt, kind="ExternalInput").ap()
            self.ins[name] = t
            return t
        cbf_d = din("cbf", [128, 1024], BF16)
        cf32_d = din("cf32", [128, 640])
        self.cbf = P.sb("cbf_sb", [128, 1024], BF16)
        self.cf32 = P.sb("cf32_sb", [128, 640], F32)
        O.dma("sp", self.cbf[:], cbf_d)
        O.dma("sp", self.cf32[:], cf32_d)
        hfm_d = din("hfmask", [128, 2])
        self.hfm = P.sb("hfm_sb", [128, 2], F32)
        O.dma("sp", self.hfm[:], hfm_d)
        self.src_seq = (din("x_in", [SEQ, D]), "x_in")
        self.src_ctx = (din("ctx_in", [CTX, D]), "ctx_in")
        self.src_rev = None
        for li in range(NLAYERS):
            L = Layer(li, self)
            if li < NLAYERS - 1:
                hags = []
                for c in range(8):
                    hag = nc.dram_tensor(f"hagrev{li}_{c}", [1024, D], F32, kind="Internal").ap()
                    O.allgather((hag, f"hagrev{li}_{c}"), L.hrev[c], PAIRS)
                    hags.append((hag, f"hagrev{li}_{c}"))
                self.src_seq = L.hout
                self.src_rev = hags
                self.src_ctx = L.hctx_out
        P.emit()


_CONSTS = None
_FUSED = None


def _want(hf):
    if hf == 0:
        return np.arange(SEQ)
    return np.arange(SEQ)[::-1].copy()


def _layer_inputs(li, b, hf, inp, C):
    f32 = np.float32
    kind, j = li % 3, li // 3
    pre = f"L{li}_"
    mp = {}
    want = _want(hf)
    vec = np.concatenate([inp["ada_b"][li].reshape(48, 128), inp["norm_g"][li].reshape(32, 128),
                          inp["c"][b].reshape(8, 128), inp["c_ctx"].reshape(8, 128)], axis=0)
    mp["vecT"] = np.ascontiguousarray(vec.T, dtype=f32)
    mp["rowv"] = np.ascontiguousarray(np.stack([inp["ada_b"][li][2 * D:3 * D], inp["ada_b"][li][5 * D:6 * D],
                                                inp["norm_g"][li][1], inp["norm_g"][li][3]]), dtype=f32)
    mp["ada_w"] = np.ascontiguousarray(inp["ada_w"][li], dtype=f32)
    mp["ffn_w_in"] = np.ascontiguousarray(inp["ffn_w_in"][li], dtype=f32)
    mp["ffn_w_out"] = np.ascontiguousarray(inp["ffn_w_out"][li], dtype=f32)
    if kind == 0:
        mp["w_qkv"] = np.ascontiguousarray(inp["da_w_qkv"][j], dtype=f32)
        mp["w_o"] = np.ascontiguousarray(inp["da_w_o"][j], dtype=f32)
        mp["lamv"] = np.ascontiguousarray(inp["da_lambda"][j].reshape(1, 256), dtype=f32)
        mp["subln"] = np.ascontiguousarray(inp["da_subln"][j].reshape(128, 1), dtype=f32)
        mp["cosT"] = np.ascontiguousarray(C["cosT"][:, want])
        mp["sinT"] = np.ascontiguousarray(C["sinT"][:, want])
    elif kind == 1:
        mp["w_qkv"] = np.ascontiguousarray(inp["sw_w_qkv"][j], dtype=f32)
        mp["w_o"] = np.ascontiguousarray(inp["sw_w_o"][j], dtype=f32)
        mp["sinkv"] = np.ascontiguousarray(inp["sw_sink"][j].reshape(4, 2, 2).transpose(2, 0, 1).reshape(2, 8), dtype=f32)
        em = np.ones((128, 2), f32)
        em[:, 0] = 0.0
        mp["emask"] = em
        wpos = np.concatenate([want[:HALF], want[:128], want[HALF:HALF + 128]])
        mp["cosT"] = np.ascontiguousarray(C["cosT"][:, wpos])
        mp["sinT"] = np.ascontiguousarray(C["sinT"][:, wpos])
    else:
        rev = hf == 1
        w = inp["ssd_w_in"][j]
        cwt = inp["ssd_conv_w"][j]
        alog, dtbias = inp["ssd_a_log"][j], inp["ssd_dt_bias"][j]
        if rev:
            w = np.concatenate([w[:, :5120], w[:, 5152:5184], w[:, 5120:5152]], axis=1)
            cwt = cwt[::-1]
            alog, dtbias = alog[::-1], dtbias[::-1]
        mp["w_in"] = np.ascontiguousarray(w, dtype=f32)
        mp["convw"] = np.ascontiguousarray(cwt.T.reshape(24, 128, 5).transpose(1, 0, 2), dtype=f32)
        mp["convb"] = np.ascontiguousarray(inp["ssd_conv_b"][j].reshape(24, 128).T, dtype=f32)
        mp["ssdrow"] = np.ascontiguousarray(np.concatenate([alog.reshape(64), dtbias.reshape(64),
                                                            inp["ssd_d_skip"][j].reshape(32)]).reshape(1, 160), dtype=f32)
        mp["normw"] = np.ascontiguousarray(inp["ssd_norm"][j].reshape(1, 2048), dtype=f32)
        mp["w_out"] = np.ascontiguousarray(inp["ssd_w_out"][j], dtype=f32)
        mp["esel"] = C["esel"]
    return {pre + k: v for k, v in mp.items()}


def kernel(**inp):
    global _CONSTS, _FUSED
    inp = {k: np.asarray(v) for k, v in inp.items()}
    if _CONSTS is None:
        _CONSTS = _consts()
    if _FUSED is None:
        _FUSED = Fused()
    C, Fz = _CONSTS, _FUSED
    in_maps = []
    for core in range(8):
        b, hf = core // 2, core % 2
        want = _want(hf)
        hfm = np.zeros((128, 2), np.float32)
        hfm[:, hf] = 1.0
        cx = inp["ctx"][b][::-1] if hf == 1 else inp["ctx"][b]
        mp = {"cbf": C["cbf"], "cf32": C["cf32"], "hfmask": hfm,
              "x_in": np.ascontiguousarray(inp["x"][b][want], dtype=np.float32),
              "ctx_in": np.ascontiguousarray(cx, dtype=np.float32)}
        for li in range(NLAYERS):
            mp.update(_layer_inputs(li, b, hf, inp, C))
        in_maps.append({k: mp[k] for k in Fz.ins})
    res = run_bass_kernel_spmd(Fz.nc, in_maps[:NCORES], core_ids=list(range(NCORES)))
    out = np.zeros((4, SEQ, D), np.float32)
    for core in range(NCORES):
        b, hf = core // 2, core % 2
        out[b, _want(hf)[:HALF]] = res.results[core]["hout"]
    return out
```

```python
import contextlib
import math

import ml_dtypes
import numpy as np

import concourse.bass as bass
import concourse.mybir as mybir
from concourse.bass_utils import run_bass_kernel_spmd

F32 = mybir.dt.float32
I32 = mybir.dt.int32
BF16 = mybir.dt.bfloat16
AF = mybir.ActivationFunctionType
ALU = mybir.AluOpType
AX = mybir.AxisListType

D = 1024
SEQ = 8192
HALF = 4096
CTX = 256
DEPTH = 4
import os
NLAYERS = int(os.environ.get('KDEPTH', '4'))
EPS = 1e-6
FFN = 2816
NFC = FFN // 128
DA_SCALE = 64 ** -0.5
SEM_LIM = 8000
SB_LO, SB_HI = 16512, 227328
DEBUG_OUT = set()
import os
STOP = int(os.environ.get('KSTOP', '99'))
NCORES = int(os.environ.get('KCORES', '8'))
DEBUG_RES = {}


class _Op:
    __slots__ = ("eng", "fn", "dma", "grp", "waits", "sig", "idx", "needs", "pos")

    def __init__(self, eng, fn, dma, grp):
        self.eng = eng
        self.fn = fn
        self.dma = dma
        self.grp = grp
        self.waits = []
        self.sig = None
        self.idx = None
        self.needs = False
        self.pos = 0


def _ap(x):
    return x[0] if isinstance(x, tuple) else x


def _keys(x):
    if isinstance(x, tuple):
        k = x[1]
        return list(k) if isinstance(k, (list, tuple)) else [k]
    if isinstance(x, str):
        return [x]
    return [x.name]


class Prog:
    ENGS = ("pe", "act", "dve", "pool", "sp")

    def __init__(self, nc):
        self.nc = nc
        self.ops = {e: [] for e in self.ENGS}
        self.lastw = {}
        self.readers = {}
        self.grp_cnt = {}
        self.grp_unit = {}
        self.slot_of = {}
        self.waited = {e: {} for e in self.ENGS}
        self.sb_top = SB_LO
        self.hi_top = SB_HI
        self.live_hi = []
        self.live = []
        self.freed = []
        self.inh = {}
        self.subkeys = {}
        self.psum = nc.alloc_psum_tensor("psall", [128, 8, 512], F32)

    def sb(self, name, shape, dt, hi=False):
        n = 1
        for s in shape[1:]:
            n *= s
        nbytes = n * (4 if dt == F32 else 2)
        nbytes = (nbytes + 63) // 64 * 64
        if hi:
            self.hi_top -= nbytes
            off = self.hi_top
        else:
            off = self.sb_top
            self.sb_top += nbytes
        assert self.sb_top <= self.hi_top, f"SBUF overflow allocating {name}: {self.sb_top} {self.hi_top}"
        t = self.nc.alloc_sbuf_tensor_at(name, list(shape), dt, offset=off)
        key = t.name
        inh = []
        for (k0, a0, b0) in self.freed:
            if a0 < off + nbytes and off < b0:
                w = self.lastw.get(k0)
                if w is not None:
                    inh.append(w)
                inh.extend(self.readers.get(k0, ()))
                inh.extend(self.inh.get(k0, ()))
                for kx in self.subkeys.get(k0, ()):
                    w = self.lastw.get(kx)
                    if w is not None:
                        inh.append(w)
                    inh.extend(self.readers.get(kx, ()))
        if inh:
            red = {}
            for d in inh:
                kk = ("d", d.grp) if d.dma else ("e", d.eng)
                if kk not in red or d.pos > red[kk].pos:
                    red[kk] = d
            self.inh[key] = list(red.values())
        if hi:
            self.live_hi.append((key, off, off + nbytes))
        else:
            self.live.append((key, off, off + nbytes))
        return t

    def release_hi(self):
        self.freed.extend(self.live_hi)
        self.live_hi = []
        self.hi_top = SB_HI

    def mark(self):
        return (self.sb_top, len(self.live))

    def release(self, m):
        self.freed.extend(self.live[m[1]:])
        del self.live[m[1]:]
        self.sb_top = m[0]

    def pb(self, b, n=1, dt=F32):
        keys = [f"ps{b + i}" for i in range(n)]
        if n == 1:
            ap = self.psum[:, b, :]
        else:
            ap = self.psum[:, b:b + n, :]
        if dt != F32:
            ap = ap.bitcast(dt)
        return ap, keys

    NSLOT = int(os.environ.get("KNSLOT", "56"))

    def add(self, eng, fn, reads, writes, dma=False, grp=None, unit=16):
        if dma:
            if unit == 16:
                if grp not in self.slot_of:
                    self.slot_of[grp] = f"slot{len(self.slot_of) % self.NSLOT}"
                grp = self.slot_of[grp]
            self.grp_unit[grp] = unit
        op = _Op(eng, fn, dma, grp)
        rk, wk = [], []
        for r in reads:
            if r is not None and not isinstance(r, (int, float)):
                rk.extend(_keys(r))
        deps = []
        for w in writes:
            if w is not None:
                wk.extend(_keys(w))
                nm = _ap(w).name
                for d in self.inh.get(nm, ()):
                    deps.append((d, False))
        for x in list(reads) + list(writes):
            if isinstance(x, tuple):
                nm = _ap(x).name
                sk = self.subkeys.setdefault(nm, set())
                for k in _keys(x):
                    sk.add(k)
        for k in rk:
            w = self.lastw.get(k)
            if w is not None:
                deps.append((w, True))
        for k in wk:
            w = self.lastw.get(k)
            if w is not None:
                deps.append((w, False))
            for r in self.readers.get(k, ()):
                deps.append((r, False))
        need_e, need_d = {}, {}
        for d, raw in deps:
            if d.dma:
                need_d[d.grp] = self.grp_cnt[d.grp]
            else:
                if d.eng == eng and not dma and (eng == "pe" or not raw):
                    continue
                cur = need_e.get(d.eng)
                if cur is None or d.pos > cur.pos:
                    need_e[d.eng] = d
        self.ops[eng].append(op)
        op.pos = len(self.ops[eng])
        if dma:
            self.grp_cnt[grp] = self.grp_cnt.get(grp, 0) + 1
            op.idx = self.grp_cnt[grp]
        wd = self.waited[eng]
        for g, cnt in need_d.items():
            if wd.get(("d", g), 0) < cnt:
                wd[("d", g)] = cnt
                op.waits.append(("d", g, cnt))
        for e2, d in need_e.items():
            if wd.get(("e", e2), 0) < d.pos:
                wd[("e", e2)] = d.pos
                d.needs = True
                op.waits.append(("e", e2, d))
        for k in rk:
            self.readers.setdefault(k, []).append(op)
        for k in wk:
            self.lastw[k] = op
            self.readers[k] = []
        return op

    def emit(self):
        nc = self.nc
        with contextlib.ExitStack() as es:
            esems = {}
            nalloc = 0
            for e in self.ENGS:
                n = 0
                for op in self.ops[e]:
                    if not op.dma and op.needs:
                        n += 1
                        op.sig = n
                nsem = (n + SEM_LIM - 1) // SEM_LIM
                esems[e] = [es.enter_context(nc.semaphore(f"pg_{e}_{i}")) for i in range(nsem)]
                nalloc += nsem
            gsems = {}
            for g in self.grp_cnt:
                gsems[g] = es.enter_context(nc.semaphore(f"dg_{len(gsems)}"))
                nalloc += 1
            assert nalloc <= 100, f"too many semaphores {nalloc}"
            block = es.enter_context(nc.Block())

            def run(engname):
                def body(eng):
                    for op in self.ops[engname]:
                        for w in op.waits:
                            if w[0] == "d":
                                eng.wait_ge(gsems[w[1]], self.grp_unit[w[1]] * w[2])
                            else:
                                s = w[2].sig - 1
                                eng.wait_ge(esems[w[1]][s // SEM_LIM], s % SEM_LIM + 1)
                        ins = op.fn(eng)
                        if op.dma:
                            ins.then_inc(gsems[op.grp], self.grp_unit[op.grp])
                        elif op.sig is not None:
                            s = op.sig - 1
                            ins.then_inc(esems[engname][s // SEM_LIM], 1)
                    for g in sorted({op.grp for op in self.ops[engname] if op.dma}, key=str):
                        eng.wait_ge(gsems[g], self.grp_unit[g] * self.grp_cnt[g])
                return body

            if self.ops["sp"]:
                block.sync(run("sp"))
            if self.ops["act"]:
                block.scalar(run("act"))
            if self.ops["dve"]:
                block.vector(run("dve"))
            if self.ops["pool"]:
                block.gpsimd(run("pool"))
            if self.ops["pe"]:
                block.tensor(run("pe"))


class Ops:
    def __init__(self, P):
        self.P = P

    def mm(self, out, lhsT, rhs, start=True, stop=True, **kw):
        o, l, r = _ap(out), _ap(lhsT), _ap(rhs)
        return self.P.add("pe", lambda e: e.matmul(o, l, r, start=start, stop=stop, **kw), [lhsT, rhs], [out])

    def tr(self, out, in_, ident):
        o, i, d = _ap(out), _ap(in_), _ap(ident)
        return self.P.add("pe", lambda e: e.transpose(o, i, d), [in_, ident], [out])

    def act(self, out, in_, func, bias=0.0, scale=1.0, accum_out=None):
        o, i, b, s, a = _ap(out), _ap(in_), _ap(bias), _ap(scale), _ap(accum_out)
        kw = {}
        if a is not None:
            kw["accum_out"] = a
        return self.P.add("act", lambda e: e.activation(o, i, func, bias=b, scale=s, **kw),
                          [in_, bias, scale], [out, accum_out])

    def tt(self, eng, out, in0, in1, op):
        o, a, b = _ap(out), _ap(in0), _ap(in1)
        return self.P.add(eng, lambda e: e.tensor_tensor(o, a, b, op), [in0, in1], [out])

    def ts(self, eng, out, in0, s1, s2, op0, op1=None):
        o, a, x1, x2 = _ap(out), _ap(in0), _ap(s1), _ap(s2)
        kw = {}
        if op1 is not None:
            kw["op1"] = op1
        return self.P.add(eng, lambda e: e.tensor_scalar(o, a, x1, x2, op0, **kw), [in0, s1, s2], [out])

    def stt(self, eng, out, in0, scalar, in1, op0, op1):
        o, a, s, b = _ap(out), _ap(in0), _ap(scalar), _ap(in1)
        return self.P.add(eng, lambda e: e.scalar_tensor_tensor(o, a, s, b, op0, op1), [in0, scalar, in1], [out])

    def cp(self, eng, out, in_):
        o, i = _ap(out), _ap(in_)
        if eng == "act":
            return self.P.add(eng, lambda e: e.copy(o, i), [in_], [out])
        return self.P.add(eng, lambda e: e.tensor_copy(o, i), [in_], [out])

    def red(self, eng, out, in_, op):
        o, i = _ap(out), _ap(in_)
        return self.P.add(eng, lambda e: e.tensor_reduce(o, i, AX.X, op), [in_], [out])

    def memset(self, eng, out, val):
        o = _ap(out)
        return self.P.add(eng, lambda e: e.memset(o, val), [], [out])

    def recip(self, out, in_, eng="dve"):
        o, i = _ap(out), _ap(in_)
        return self.P.add(eng, lambda e: e.reciprocal(o, i), [in_], [out])

    def gather(self, out, src, idx):
        o, i, x = _ap(out), _ap(src), _ap(idx)
        grp = _keys(out)[0]
        return self.P.add("pool", lambda e: e.indirect_dma_start(
            out=o, out_offset=None, in_=i, in_offset=bass.IndirectOffsetOnAxis(ap=x, axis=0)),
            [src, idx], [out], dma=True, grp=grp)

    def allgather(self, out, in_, groups):
        o, i = _ap(out), _ap(in_)
        return self.P.add("pool", lambda e: e.collective_compute(
            "AllGather", ALU.bypass, replica_groups=groups, ins=[i.opt()], outs=[o.opt()]),
            [in_], [out], dma=True, grp="cc", unit=1)

    def dma(self, eng, out, in_, grp=None):
        o, i = _ap(out), _ap(in_)
        if grp is None:
            grp = _keys(out)[0] if str(o.space).endswith("SB") else _keys(in_)[0]
        return self.P.add(eng, lambda e: e.dma_start(out=o, in_=i), [in_], [out], dma=True, grp=grp)


def _consts():
    ident = np.eye(128, dtype=np.float32)
    perm = np.zeros((128, 128), np.float32)
    for p in range(128):
        d = p % 64
        if d < 32:
            perm[p + 32, p] = -1.0
        else:
            perm[p - 32, p] = 1.0
    nf = 16
    freqs = (np.float32(10000.0) ** (-np.arange(nf, dtype=np.float32) / np.float32(nf))).astype(np.float32)
    rows = np.repeat(np.arange(SEQ // 64), 64).astype(np.float32)
    cols = np.tile(np.arange(64), SEQ // 64).astype(np.float32)
    ang = np.concatenate([rows[:, None] * freqs, cols[:, None] * freqs], axis=-1).astype(np.float32)
    cosT = np.ascontiguousarray(np.tile(np.cos(ang).astype(np.float32).T, (4, 1)))
    sinT = np.ascontiguousarray(np.tile(np.sin(ang).astype(np.float32).T, (4, 1)))
    ii = np.arange(128)[:, None]
    jj = np.arange(128)[None, :]
    m_prev = (ii >= jj).astype(np.float32)
    m_next = (ii <= jj).astype(np.float32)
    tri = (ii <= jj).astype(np.float32)
    nm_f = np.where(ii > jj, -30000.0, 0.0).astype(np.float32)
    nm_b = np.where(ii < jj, -30000.0, 0.0).astype(np.float32)
    triT = (ii >= jj).astype(np.float32)
    esel = np.zeros((32, 32, 128), np.float32)
    for k in range(32):
        esel[k, k, :] = 1.0
    cmat = np.concatenate([ident, perm, np.ones((128, 128), np.float32), m_prev, m_next, tri, nm_f, nm_b], axis=1)
    return {
        "cbf": cmat.astype(ml_dtypes.bfloat16),
        "cf32": np.concatenate([ident, np.ones((128, 128), np.float32), tri, triT, ident[::-1].copy()], axis=1),
        "esel": esel.reshape(32, 4096).astype(ml_dtypes.bfloat16),
        "cosT": cosT, "sinT": sinT,
    }


class Layer:
    def __init__(self, li, parent):
        self.li = li
        self.kind = li % 3
        self.want_ctx = li < DEPTH - 1
        self.last = li == NLAYERS - 1
        self.nc, self.P, self.O = parent.nc, parent.P, parent.O
        self.ins = parent.ins
        self.pre = f"L{li}_"
        self.cbf, self.cf32 = parent.cbf, parent.cf32
        self.src_seq, self.src_ctx, self.src_rev = parent.src_seq, parent.src_ctx, parent.src_rev
        self.hfm = parent.hfm
        self.build()

    def din(self, name, shape, dt=F32):
        name = self.pre + name
        t = self.nc.dram_tensor(name, list(shape), dt, kind="ExternalInput").ap()
        self.ins[name] = t
        return t

    def dout(self, name, shape, dt=F32):
        return self.nc.dram_tensor(name, list(shape), dt, kind="ExternalOutput").ap()

    def dscr(self, name, shape, dt):
        return self.nc.dram_tensor(self.pre + name, list(shape), dt, kind="Internal").ap()

    def load_tile(self, xt, which, t):
        O = self.O
        if which == "ctx":
            O.dma("sp", xt[:], (self.src_ctx[0][t * 128:(t + 1) * 128, :], self.src_ctx[1]))
            return
        if which == "halo":
            if t == 0:
                O.dma("sp", xt[:], (self.src_seq[0][0:128, :], self.src_seq[1]))
                return
            t = HALF // 128
        if self.src_rev is None or t < HALF // 128:
            O.dma("sp", xt[:], (self.src_seq[0][t * 128:(t + 1) * 128, :], self.src_seq[1]))
            return
        j = t - HALF // 128
        rv = self.src_rev[j // 4]
        o = (j % 4) * 128
        O.dma("sp", self.xa[:], (rv[0][512 + o:512 + o + 128, :], rv[1]))
        O.dma("sp", self.xb[:], (rv[0][o:o + 128, :], rv[1]))
        O.act(xt[:], self.xa[:], AF.Copy, scale=self.hfm[:, 0:1])
        O.stt("dve", xt[:], self.xb[:], self.hfm[:, 1:2], xt[:], ALU.mult, ALU.add)

    def build(self):
        P, O = self.P, self.O
        mL = P.mark()
        if self.src_rev is not None:
            self.xa = P.sb("blend_a", [128, D], F32, hi=True)
            self.xb = P.sb("blend_b", [128, D], F32, hi=True)
        self.vecT = self.din("vecT", [128, 96])
        self.rowv = self.din("rowv", [4, D])
        self.ada_w = self.din("ada_w", [D, 6 * D])
        self.ffn_w_in = self.din("ffn_w_in", [D, 2 * FFN])
        self.ffn_w_out = self.din("ffn_w_out", [FFN, D])
        if self.last:
            self.hout = (self.dout("hout", [HALF, D]), "hout")
        else:
            self.hout = (self.dscr("hown", [HALF, D], F32), self.pre + "hown")
            self.hrev = [(self.dscr(f"hrev{c}", [512, D], F32), self.pre + f"hrev{c}") for c in range(8)]
        if self.want_ctx:
            self.hctx_out = (self.dscr("hctxo", [CTX, D], F32), self.pre + "hctxo")
        self.hmid = self.dscr("hmid", [HALF + CTX, D], F32)
        self.gates = self.dscr("gates", [4, 128, D], F32)
        self.ident = self.cbf[:, 0:128]
        self.perm = self.cbf[:, 128:256]
        self.ones_bf = self.cbf[:, 256:384]
        self.ones_f = self.cf32[:, 128:256]
        self.mods = P.sb("mods", [128, 8, 8], F32)
        self.stat = P.sb("stat", [128, 64], F32)
        self.stat_i = 0

        self.phase_adaln()
        if self.kind == 0:
            self.da_mixer()
        elif self.kind == 1:
            self.sw_mixer()
        else:
            self.ssd_mixer()
        P.release_hi()
        self.phase_ffn()
        P.release(mL)

    def newstat(self, n=1):
        if self.stat_i + n > 64:
            self.stat_i = 0
        s = self.stat[:, self.stat_i:self.stat_i + n]
        key = f"stat{self.stat_i}"
        self.stat_i += n
        return (s, key)

    def rstd_from_ss(self, ss, n):
        O = self.O
        sd = self.newstat()
        O.act(sd, ss, AF.Sqrt, bias=self.eps_t, scale=1.0 / n)
        rs = self.newstat()
        O.recip(rs, sd)
        return rs

    def phase_adaln(self):
        P, O = self.P, self.O
        eps_t = P.sb("eps_t", [128, 1], F32)
        O.memset("dve", eps_t[:], EPS)
        self.eps_t = eps_t[:, 0:1]
        m = P.mark()
        vec = P.sb("vec", [128, 96], F32)
        O.dma("sp", vec[:], self.vecT)
        act16 = P.sb("act16", [128, 16], BF16)
        O.act(act16[:], vec[:, 80:96], AF.Silu)
        actrep = P.sb("actrep", [128, 16, 128], BF16)
        for j in range(16):
            O.cp("dve", (actrep[:, j, :], f"actrep{j}"), act16[:, j:j + 1].to_broadcast([128, 128]))
        bb = [P.sb(f"bb{i}", [128, D], F32) for i in range(2)]
        gb = [P.sb(f"gb{i}", [128, D], F32) for i in range(2)]
        for i in range(2):
            O.dma("sp", bb[i][:], self.rowv[i:i + 1, :].broadcast_to([128, D]))
            O.dma("sp", gb[i][:], self.rowv[2 + i:3 + i, :].broadcast_to([128, D]))
        wm = [P.sb(f"adaw{i}", [128, 8, D], BF16) for i in range(2)]
        gt = [P.sb(f"gatet{i}", [128, D], F32) for i in range(2)]
        pm, pmk = P.pb(0)
        pm = pm.rearrange("p (s t) -> p s t", t=2)
        slot = 0
        gi = 0
        aw = self.ada_w.rearrange("(k p) n -> p k n", p=128)
        for mod in range(6):
            w = wm[mod % 2]
            O.dma("pool", w[:], aw[:, :, mod * D:(mod + 1) * D])
            if mod in (0, 1, 3, 4):
                for c in range(8):
                    for k in range(8):
                        O.mm((pm[:, slot, :], pmk), w[:, k, c * 128:(c + 1) * 128], act16[:, k::8],
                             start=(k == 0), stop=(k == 7))
                    slot += 1
            else:
                which = 0 if mod == 2 else 1
                for lc in range(2):
                    g = gt[gi % 2]
                    for cb in range(2):
                        pg = P.pb(1 + cb)
                        for k in range(8):
                            O.mm(pg, (actrep[:, lc * 8 + k, :], f"actrep{lc * 8 + k}"), w[:, k, cb * 512:(cb + 1) * 512],
                                 start=(k == 0), stop=(k == 7))
                        O.tt("dve", g[:, cb * 512:(cb + 1) * 512], pg, bb[which][:, cb * 512:(cb + 1) * 512], ALU.add)
                    O.tt("dve", g[:], g[:], gb[which][:], ALU.mult)
                    O.dma("sp", (self.gates[which * 2 + lc], f"gates{which * 2 + lc}"), g[:])
                    gi += 1
        mods = self.mods
        pmv = P.pb(0)[0].rearrange("p (s t) -> p s t", t=2)
        for lc in range(2):
            base = lc * 4
            O.tt("dve", mods[:, base + 0, :], (pmv[:, 0:8, lc], pmk), vec[:, 0:8], ALU.add)
            O.tt("dve", mods[:, base + 1, :], (pmv[:, 8:16, lc], pmk), vec[:, 8:16], ALU.add)
            O.stt("dve", mods[:, base + 1, :], mods[:, base + 1, :], 1.0, vec[:, 48:56], ALU.add, ALU.mult)
            O.tt("dve", mods[:, base + 2, :], (pmv[:, 16:24, lc], pmk), vec[:, 24:32], ALU.add)
            O.tt("dve", mods[:, base + 3, :], (pmv[:, 24:32, lc], pmk), vec[:, 32:40], ALU.add)
            O.stt("dve", mods[:, base + 3, :], mods[:, base + 3, :], 1.0, vec[:, 64:72], ALU.add, ALU.mult)
        P.release(m)

    def norm_uT(self, xt, xh, uT_slice_fn, mod_base, tb, ei):
        P, O = self.P, self.O
        ss = self.newstat()
        O.act(xh[:], xt[:], AF.Square, accum_out=ss)
        rs = self.rstd_from_ss(ss, D)
        O.ts("dve", xh[:], xt[:], rs, None, ALU.mult)
        pT = P.pb(tb, 1, BF16)
        pTv = pT[0].rearrange("p (k t) -> p k t", k=8)
        for k in range(8):
            O.tr((pTv[:, k, :], pT[1]), xh[:, k * 128:(k + 1) * 128], self.ident)
        S = self.mods[:, mod_base, :]
        G = self.mods[:, mod_base + 1, :]
        for k in range(8):
            dst = uT_slice_fn(k)
            if (k + ei) % 2 == 0:
                O.act(dst, (pTv[:, k, :], pT[1]), AF.Identity, bias=S[:, k:k + 1], scale=G[:, k:k + 1])
            else:
                O.ts("dve", dst, (pTv[:, k, :], pT[1]), G[:, k:k + 1], S[:, k:k + 1], ALU.mult, ALU.add)

    def da_mixer(self):
        P, O = self.P, self.O
        li = self.li
        lam_init = 0.8 - 0.6 * math.exp(-0.3 * li)
        w_qkv = self.din("w_qkv", [D, 3 * D])
        w_o = self.din("w_o", [D, D])
        lamv = self.din("lamv", [1, 256])
        subln = self.din("subln", [128, 1])
        cosT = self.din("cosT", [128, SEQ])
        sinT = self.din("sinT", [128, SEQ])
        NK = SEQ + CTX
        QT = self.dscr("QT", [8, 128, HALF], BF16)
        QTc = self.dscr("QTc", [8, 128, CTX], BF16)
        KT = self.dscr("KT", [8, 128, NK], BF16)
        Vs = self.dscr("Vs", [NK, D], BF16)

        lt = P.sb("lamt", [128, 256], F32)
        O.dma("sp", lt[:], lamv.broadcast_to([128, 256]))
        lp = P.sb("lamp", [128, 128], F32)
        O.tt("dve", lp[:, 0:64], lt[:, 0:64], lt[:, 64:128], ALU.mult)
        O.tt("dve", lp[:, 64:128], lt[:, 128:192], lt[:, 192:256], ALU.mult)
        lsc = P.sb("lsc", [128, 8], F32)
        O.red("dve", lsc[:, 0:1], lp[:, 0:64], ALU.add)
        O.red("dve", lsc[:, 1:2], lp[:, 64:128], ALU.add)
        O.act(lsc[:, 2:4], lsc[:, 0:2], AF.Exp)
        O.stt("dve", lsc[:, 4:5], lsc[:, 3:4], -lam_init, lsc[:, 2:3], ALU.add, ALU.subtract)
        neg_lam = lsc[:, 4:5]
        sg = P.sb("sublng", [128, 2], F32)
        O.dma("sp", sg[:, 0:1], subln)
        O.ts("dve", sg[:, 1:2], sg[:, 0:1], 1.0 - lam_init, None, ALU.mult)
        subg = sg[:, 1:2]

        m1 = P.mark()
        wq = P.sb("wqkv", [128, 8, 3 * D], BF16)
        wv = w_qkv.rearrange("(k p) n -> p k n", p=128)
        O.dma("pool", (wq[:], ["wqkv0", "wqkv1", "wqkv2"]), wv)
        xts = [P.sb(f"xt{i}", [128, D], F32) for i in range(3)]
        xhs = [P.sb(f"xh{i}", [128, D], BF16) for i in range(2)]
        uTs = [P.sb(f"uT{i}", [128, 8, 512], BF16) for i in range(2)]
        cs = [P.sb(f"cos{i}", [128, 512], F32) for i in range(2)]
        sn = [P.sb(f"sin{i}", [128, 512], F32) for i in range(2)]
        kb = [P.sb(f"kb{i}", [128, 512], BF16) for i in range(2)]
        t1 = [P.sb(f"ropt1_{i}", [128, 512], F32) for i in range(2)]
        t2 = [P.sb(f"ropt2_{i}", [128, 512], F32) for i in range(2)]
        kst = [P.sb(f"kst{i}", [128, 8, 512], BF16) for i in range(2)]
        qst = [P.sb(f"qst{i}", [128, 8, 512], BF16) for i in range(2)]
        vst = [P.sb(f"vst{i}", [128, D], BF16) for i in range(2)]

        groups = [("ctx", 0, CTX, 0 if self.want_ctx else None)]
        for g in range(SEQ // 512):
            groups.append(("lat", g * 512, 512, g * 512 if g < HALF // 512 else None))
        self._cnt = 0
        self._pbank = 0

        def qk_proj(gi, ub, ntok, rope, colbase, stage, dstT, dcol0):
            for h in range(8):
                pk = P.pb(2 + self._pbank % 4)
                self._pbank += 1
                pkv = (pk[0][:, :ntok], pk[1])
                for k in range(8):
                    O.mm(pkv, (wq[:, k, colbase + h * 128:colbase + (h + 1) * 128], f"wqkv{colbase // D}"),
                         ub[:, k, :ntok], start=(k == 0), stop=(k == 7))
                if not rope:
                    O.cp("act", stage[:, h, :ntok], pkv)
                    continue
                kbt = kb[h % 2]
                O.cp("act", kbt[:, :ntok], pkv)
                psw = P.pb(6 + h % 2)
                pswv = (psw[0][:, :ntok], psw[1])
                O.mm(pswv, self.perm, kbt[:, :ntok])
                O.tt("dve", t1[h % 2][:, :ntok], kbt[:, :ntok], cs[gi % 2][:, :ntok], ALU.mult)
                O.tt("dve", t2[h % 2][:, :ntok], pswv, sn[gi % 2][:, :ntok], ALU.mult)
                O.tt("pool", stage[:, h, :ntok], t1[h % 2][:, :ntok], t2[h % 2][:, :ntok], ALU.add)
            O.dma("sp", (_ap(dstT)[:, :, dcol0:dcol0 + ntok].rearrange("h p t -> p h t"), dstT[1]), stage[:, :, :ntok])

        for gi, (kind, r0, ntok, own_q0) in enumerate(groups):
            ub = uTs[gi % 2]
            nt = ntok // 128
            for t in range(nt):
                c = self._cnt
                self._cnt += 1
                xt = xts[c % 3]
                xh = xhs[c % 2]
                self.load_tile(xt, "ctx" if kind == "ctx" else "seq", r0 // 128 + t)
                self.norm_uT(xt, xh, lambda k, ub=ub, t=t: ub[:, k, t * 128:(t + 1) * 128],
                             4 if kind == "ctx" else 0, c % 2, c)
            rope = kind == "lat"
            if rope:
                O.dma("sp", cs[gi % 2][:, :ntok], cosT[:, r0:r0 + ntok])
                O.dma("sp", sn[gi % 2][:, :ntok], sinT[:, r0:r0 + ntok])
            kcol0 = 0 if kind == "ctx" else CTX + r0
            qk_proj(gi, ub, ntok, rope, D, kst[gi % 2], (KT, f"KT{gi}"), kcol0)
            if own_q0 is not None:
                if kind == "ctx":
                    qk_proj(gi, ub, ntok, rope, 0, qst[gi % 2], (QTc, "QTc"), 0)
                else:
                    qk_proj(gi, ub, ntok, rope, 0, qst[gi % 2], (QT, f"QT{own_q0 // 512}"), own_q0)
            for t in range(nt):
                vt = vst[t % 2]
                for cb in range(2):
                    pv = P.pb(2 + self._pbank % 4)
                    self._pbank += 1
                    for k in range(8):
                        O.mm(pv, ub[:, k, t * 128:(t + 1) * 128],
                             (wq[:, k, 2 * D + cb * 512:2 * D + (cb + 1) * 512], "wqkv2"),
                             start=(k == 0), stop=(k == 7))
                    O.cp("act" if cb == 0 else "dve", vt[:, cb * 512:(cb + 1) * 512], pv)
                O.dma("sp", (Vs[kcol0 + t * 128:kcol0 + (t + 1) * 128, :], f"Vs{gi}"), vt[:])
        P.release(m1)
        P.release_hi()
        ngrp = len(groups)

        m2 = P.mark()
        onT = P.sb("onT", [128, 8, HALF], BF16)
        onTc = P.sb("onTc", [128, 8, CTX], BF16)
        ktb = [P.sb(f"ktb{i}", [128, NK], BF16) for i in range(2)]
        vtb = [P.sb(f"vtb{i}", [128, NK // 128, 128], BF16) for i in range(2)]
        qtb = [P.sb(f"qtb{i}", [128, 512], BF16) for i in range(3)]
        ptb = [P.sb(f"ptb{i}", [128, 2, 512], BF16) for i in range(3)]
        fw = [P.sb(f"fin{i}", [128, 512], F32) for i in range(4)]
        kt_keys = [f"KT{gi}" for gi in range(ngrp)]
        vs_keys = [f"Vs{gi}" for gi in range(ngrp)]
        nkt = NK // 128
        qcount = 0
        for h in range(8):
            kt_sb = ktb[h % 2]
            v_sb = vtb[h % 2]
            O.dma("sp", kt_sb[:], (KT[h], kt_keys))
            O.dma("sp", v_sb[:], (Vs[:, h * 128:(h + 1) * 128].rearrange("(t p) v -> p t v", p=128), vs_keys))
            qgroups = [("lat", q0, 512, 0, nkt) for q0 in range(0, HALF, 512)]
            if self.want_ctx:
                qgroups.append(("ctx", 0, CTX, 0, CTX // 128))
            for (qk, q0, nq, kt0, kt1) in qgroups:
                qt = qtb[qcount % 3]
                qcount += 1
                if qk == "lat":
                    O.dma("sp", qt[:, :nq], (QT[h, :, q0:q0 + nq], f"QT{q0 // 512}"))
                else:
                    O.dma("sp", qt[:, :nq], (QTc[h], "QTc"))
                po = P.pb(4, 2)
                pd = P.pb(6, 2)
                def issue_qk(kt):
                    psq = P.pb(2 * (kt % 2), 2)
                    for mp in range(2):
                        O.mm((psq[0][:, mp, :nq], psq[1]), kt_sb[mp * 64:(mp + 1) * 64, kt * 128:(kt + 1) * 128],
                             qt[mp * 64:(mp + 1) * 64, :nq])
                issue_qk(kt0)
                for kt in range(kt0, kt1):
                    if kt + 1 < kt1:
                        issue_qk(kt + 1)
                    psn = P.pb(2 * (kt % 2), 2)
                    pt = ptb[kt % 3]
                    O.act(pt[:, :, :nq], (psn[0][:, :, :nq], psn[1]), AF.Exp, scale=DA_SCALE)
                    for mp in range(2):
                        O.mm((po[0][:, mp, :nq], po[1]), v_sb[:, kt, :], pt[:, mp, :nq],
                             start=(kt == kt0), stop=(kt == kt1 - 1))
                    for mp in range(2):
                        O.mm((pd[0][:, mp, :nq], pd[1]), self.ones_bf, pt[:, mp, :nq],
                             start=(kt == kt0), stop=(kt == kt1 - 1))
                r0t, w0t, w1t, ot = fw
                O.recip(r0t[:, :nq], (pd[0][:, 0, :nq], pd[1]))
                O.tt("dve", w0t[:, :nq], (po[0][:, 0, :nq], po[1]), r0t[:, :nq], ALU.mult)
                O.recip(r0t[:, :nq], (pd[0][:, 1, :nq], pd[1]))
                O.tt("dve", w1t[:, :nq], (po[0][:, 1, :nq], po[1]), r0t[:, :nq], ALU.mult)
                O.stt("dve", ot[:, :nq], w1t[:, :nq], neg_lam, w0t[:, :nq], ALU.mult, ALU.add)
                O.act(w0t[:, :nq], ot[:, :nq], AF.Square)
                pss = P.pb(0)
                O.mm((pss[0][:, :nq], pss[1]), self.ones_f, w0t[:, :nq])
                O.act(w1t[:, :nq], (pss[0][:, :nq], pss[1]), AF.Sqrt, bias=self.eps_t, scale=1.0 / 128)
                O.recip(r0t[:, :nq], w1t[:, :nq])
                O.tt("dve", ot[:, :nq], ot[:, :nq], r0t[:, :nq], ALU.mult)
                if qk == "lat":
                    dst = (onT[:, h, q0:q0 + nq], f"onT{h}")
                else:
                    dst = (onTc[:, h, :], f"onTc{h}")
                O.ts("dve", dst, ot[:, :nq], subg, None, ALU.mult)
        wo = P.sb("wo", [128, 8, D], BF16)
        O.dma("pool", wo[:], w_o.rearrange("(k p) n -> p k n", p=128))
        self.resid_phase(lambda h, t: (onT[:, h, t * 128:(t + 1) * 128], f"onT{h}"),
                         lambda h, t: (onTc[:, h, t * 128:(t + 1) * 128], f"onTc{h}"),
                         lambda h: wo[:, h, :], 8)
        P.release(m2)

    def resid_setup(self):
        P, O = self.P, self.O
        R = {}
        R["g1"] = [P.sb(f"g1_{i}", [128, D], F32) for i in range(2)]
        O.dma("sp", R["g1"][0][:], (self.gates[0], "gates0"))
        O.dma("sp", R["g1"][1][:], (self.gates[1], "gates1"))
        R["xts"] = [P.sb(f"rxt{i}", [128, D], F32) for i in range(3)]
        R["tmp"] = [P.sb(f"rtmp{i}", [128, D], F32) for i in range(2)]
        R["junk"] = P.sb("rjunk", [128, D], BF16)
        R["i"] = 0
        return R

    def resid_tile(self, R, kind, t, lhsT_fn, w_rows, nchunk, banks=(0, 2)):
        P, O = self.P, self.O
        i = R["i"]
        R["i"] += 1
        po = P.pb(banks[i % 2], 2)
        for cb in range(2):
            for h in range(nchunk):
                O.mm((po[0][:, cb, :], po[1]), lhsT_fn(h), _ap(w_rows(h))[:, cb * 512:(cb + 1) * 512],
                     start=(h == 0), stop=(h == nchunk - 1))
        xt = R["xts"][i % 3]
        self.load_tile(xt, "seq" if kind == "lat" else "ctx", t)
        ss = self.newstat()
        O.act(R["junk"][:].rearrange("p (a b) -> p a b", a=2), po, AF.Square, accum_out=ss)
        rs = self.rstd_from_ss(ss, D)
        tm = R["tmp"][i % 2]
        O.stt("dve", tm[:].rearrange("p (a b) -> p a b", a=2), po, rs,
              R["g1"][0 if kind == "lat" else 1][:].rearrange("p (a b) -> p a b", a=2), ALU.mult, ALU.mult)
        O.tt("pool", tm[:], tm[:], xt[:], ALU.add)
        row0 = t * 128 if kind == "lat" else HALF + t * 128
        O.dma("sp", (self.hmid[row0:row0 + 128, :], f"hmid{row0 // 512}"), tm[:])

    def resid_phase(self, lat_lhsT, ctx_lhsT, w_rows, nchunk):
        P = self.P
        m = P.mark()
        R = self.resid_setup()
        tiles = [("lat", t) for t in range(HALF // 128)]
        if self.want_ctx:
            tiles += [("ctx", t) for t in range(CTX // 128)]
        for (kind, t) in tiles:
            fn = lat_lhsT if kind == "lat" else ctx_lhsT
            self.resid_tile(R, kind, t, lambda h, fn=fn, t=t: fn(h, t), w_rows, nchunk)
        P.release(m)

    def sw_mixer(self):
        P, O = self.P, self.O
        w_qkv = self.din("w_qkv", [D, 1536])
        w_o = self.din("w_o", [D, D])
        sinkv = self.din("sinkv", [2, 8])
        emask_d = self.din("emask", [128, 2])
        NW = HALF + 256
        cosT = self.din("cosT", [128, NW])
        sinT = self.din("sinT", [128, NW])
        NKW = CTX + NW
        se = P.sb("sinkexp", [128, 8], F32)
        O.dma("sp", se[0:64, :], sinkv[0:1, :].broadcast_to([64, 8]))
        O.dma("sp", se[64:128, :], sinkv[1:2, :].broadcast_to([64, 8]))
        O.act(se[:], se[:], AF.Exp)
        emask = P.sb("emask_sb", [128, 2], F32)
        O.dma("sp", emask[:], emask_d)
        m_prev = self.cbf[:, 384:512]
        m_next = self.cbf[:, 512:640]
        m2 = P.mark()
        m3 = P.mark()
        QTs = P.sb("QTs", [128, 8, HALF + CTX], BF16)
        KTs = P.sb("KTs", [128, 4, NKW], BF16)
        Vs = P.sb("Vsw", [128, NKW // 128, 256], BF16)
        m1 = P.mark()
        wq = P.sb("wqkv", [128, 8, 1536], BF16)
        O.dma("pool", wq[:], w_qkv.rearrange("(k p) n -> p k n", p=128))
        wkd = P.sb("wkdup", [128, 8, 4, 128], BF16)
        for hk in range(4):
            for dup in range(2):
                O.cp("dve" if dup == 0 else "pool", (wkd[:, :, hk, dup * 64:(dup + 1) * 64], f"wkd{hk}"),
                     wq[:, :, D + hk * 64:D + (hk + 1) * 64])
        xts = [P.sb(f"xt{i}", [128, D], F32) for i in range(2)]
        xhs = [P.sb(f"xh{i}", [128, D], BF16) for i in range(2)]
        uTs = [P.sb(f"uT{i}", [128, 8, 512], BF16) for i in range(1)]
        cs = [P.sb(f"cos{i}", [128, 512], F32) for i in range(2)]
        sn = [P.sb(f"sin{i}", [128, 512], F32) for i in range(2)]
        kb = [P.sb(f"kb{i}", [128, 512], BF16) for i in range(2)]
        t1 = [P.sb(f"ropt1_{i}", [128, 512], F32) for i in range(2)]
        t2 = [P.sb(f"ropt2_{i}", [128, 512], F32) for i in range(2)]
        groups = [("ctx", "ctx", 0, CTX, None, 0, HALF)]
        for g in range(HALF // 512):
            groups.append(("lat", "seq", g * 512, 512, g * 512, CTX + 128 + g * 512, g * 512))
        groups.append(("lat", "halo", 0, 128, HALF, CTX, None))
        groups.append(("lat", "halo", 128, 128, HALF + 128, CTX + 128 + HALF, None))
        cnt = 0
        pbank = 0
        for gi, (kind, src, r0, ntok, rc0, kc0, qc0) in enumerate(groups):
            ub = uTs[0]
            for t in range(ntok // 128):
                xt = xts[cnt % 2]
                xh = xhs[cnt % 2]
                self.load_tile(xt, src, r0 // 128 + t)
                self.norm_uT(xt, xh, lambda k, ub=ub, t=t: ub[:, k, t * 128:(t + 1) * 128],
                             4 if kind == "ctx" else 0, cnt % 2, cnt)
                cnt += 1
            rope = kind == "lat"
            if rope:
                O.dma("sp", cs[gi % 2][:, :ntok], cosT[:, rc0:rc0 + ntok])
                O.dma("sp", sn[gi % 2][:, :ntok], sinT[:, rc0:rc0 + ntok])
            jobs = [("k", hk) for hk in range(4)]
            if qc0 is not None:
                jobs += [("q", c) for c in range(8)]
            for ji, (what, c) in enumerate(jobs):
                pk = P.pb(2 + pbank % 4)
                pbank += 1
                pkv = (pk[0][:, :ntok], pk[1])
                for k in range(8):
                    if what == "k":
                        lw = (wkd[:, k, c, :], f"wkd{c}")
                    else:
                        lw = wq[:, k, c * 128:(c + 1) * 128]
                    O.mm(pkv, lw, ub[:, k, :ntok], start=(k == 0), stop=(k == 7))
                if what == "k":
                    dst = (KTs[:, c, kc0:kc0 + ntok], f"KTs{gi}")
                else:
                    dst = (QTs[:, c, qc0:qc0 + ntok], f"QTs{gi}")
                if not rope:
                    O.cp("act", dst, pkv)
                    continue
                kbt = kb[ji % 2]
                O.cp("act", kbt[:, :ntok], pkv)
                psw = P.pb(6 + ji % 2)
                pswv = (psw[0][:, :ntok], psw[1])
                O.mm(pswv, self.perm, kbt[:, :ntok])
                O.tt("dve", t1[ji % 2][:, :ntok], kbt[:, :ntok], cs[gi % 2][:, :ntok], ALU.mult)
                O.tt("dve", t2[ji % 2][:, :ntok], pswv, sn[gi % 2][:, :ntok], ALU.mult)
                O.tt("pool", dst, t1[ji % 2][:, :ntok], t2[ji % 2][:, :ntok], ALU.add)
            for t in range(ntok // 128):
                pv = P.pb(2 + pbank % 4)
                pbank += 1
                pvv = (pv[0][:, :256], pv[1])
                for k in range(8):
                    O.mm(pvv, ub[:, k, t * 128:(t + 1) * 128], wq[:, k, D + 256:D + 512], start=(k == 0), stop=(k == 7))
                O.cp("act", (Vs[:, kc0 // 128 + t, :], f"Vsw{gi}"), pvv)
        P.release(m1)
        ngrp = len(groups)
        allk = [f"KTs{gi}" for gi in range(ngrp)]
        allv = [f"Vsw{gi}" for gi in range(ngrp)]
        allq = [f"QTs{gi}" for gi in range(ngrp) if groups[gi][6] is not None]

        P.release_hi()
        oT = P.sb("oTs", [128, 8, HALF + CTX], BF16, hi=True)
        ptb = [P.sb(f"ptb{i}", [128, 2, 256], BF16) for i in range(3)]
        dn = [P.sb(f"dn{i}", [128, 256], F32) for i in range(2)]
        qtiles = [("lat", t) for t in range(HALF // 128)] + [("ctx", t) for t in range(CTX // 128)]
        it = 0
        for (qk, t) in qtiles:
            if qk == "lat":
                ktl = [(0, None, None), (1, None, None), (2 + t, m_prev, 0 if t == 0 else None), (3 + t, None, None),
                       (4 + t, m_next, 1 if t == HALF // 128 - 1 else None)]
                qc = t * 128
            else:
                ktl = [(0, None, None), (1, None, None)]
                qc = HALF + t * 128
            for hk in range(4):
                po = P.pb(4 + 2 * (it % 2))
                pd = P.pb(5 + 2 * (it % 2))
                it += 1
                for ki, (kt, msk, em) in enumerate(ktl):
                    pt = ptb[ki % 3]
                    first, last = ki == 0, ki == len(ktl) - 1
                    pss = P.pb(2 * (ki % 2), 2)
                    for par in range(2):
                        O.mm((pss[0][:, par, 0:256], pss[1]),
                             (KTs[par * 64:(par + 1) * 64, hk, kt * 128:(kt + 1) * 128], allk),
                             (QTs[par * 64:(par + 1) * 64, hk * 2:hk * 2 + 2, qc:qc + 128], allq))
                    O.act(pt[:], (pss[0][:, :, 0:256], pss[1]), AF.Exp, scale=DA_SCALE)
                    if msk is not None:
                        ptv = pt[:].rearrange("p a (g q) -> p (a g) q", g=2)
                        O.tt("dve", ptv, ptv, msk.unsqueeze(1).to_broadcast([128, 4, 128]), ALU.mult)
                        if em is not None:
                            O.ts("dve", pt[:], pt[:], emask[:, em:em + 1], None, ALU.mult)
                    for par in range(2):
                        O.mm((po[0][par * 64:(par + 1) * 64, 0:256], po[1]),
                             (Vs[:, kt, hk * 64:(hk + 1) * 64], allv), pt[:, par, :], start=first, stop=last)
                    for par in range(2):
                        O.mm((pd[0][par * 64:(par + 1) * 64, 0:256], pd[1]),
                             self.ones_bf[:, 0:64], pt[:, par, :], start=first, stop=last)
                d = dn[hk % 2]
                dv = d[:].rearrange("p (g q) -> p g q", g=2)
                O.tt("dve", dv, (pd[0][:, 0:256].rearrange("p (g q) -> p g q", g=2), pd[1]),
                     se[:, hk * 2:hk * 2 + 2].unsqueeze(2).to_broadcast([128, 2, 128]), ALU.add)
                O.recip(d[:], d[:])
                O.tt("dve", (oT[:, hk * 2:hk * 2 + 2, qc:qc + 128], f"oT{qk}{t}"),
                     (po[0][:, 0:256].rearrange("p (g q) -> p g q", g=2), po[1]), dv, ALU.mult)
        P.release(m3)
        wo = P.sb("wo", [128, 8, D], BF16)
        O.dma("pool", wo[:], w_o.rearrange("(k p) n -> p k n", p=128))
        self.resid_phase(lambda h, t: (oT[:, h, t * 128:(t + 1) * 128], f"oTlat{t}"),
                         lambda h, t: (oT[:, h, HALF + t * 128:HALF + (t + 1) * 128], f"oTctx{t}"),
                         lambda h: wo[:, h, :], 8)
        P.release(m2)
        P.release_hi()

    def ssd_mixer(self):
        P, O = self.P, self.O
        NT = SEQ + CTX
        w_in = self.din("w_in", [D, 5184])
        convw = self.din("convw", [128, 24, 5])
        convb = self.din("convb", [128, 24])
        ssdrow = self.din("ssdrow", [1, 160])
        normw = self.din("normw", [1, 2048])
        w_out = self.din("w_out", [2048, D])
        esel_d = self.din("esel", [32, 4096], BF16)
        xs_d = self.dscr("xs_d", [NT, 2048], BF16)
        bs_d = self.dscr("bs_d", [NT, 512], BF16)
        bt_d = self.dscr("bt_d", [128, 4, NT], BF16)
        ct_d = self.dscr("ct_d", [128, 4, NT], BF16)
        dt_d = self.dscr("dt_d", [NT, 64], F32)
        zs_d = self.dscr("zs_d", [NT, 2048], BF16)
        yf_d = self.dscr("yf_d", [NT, 2048], F32)
        nm = [self.cbf[:, 768:896], self.cbf[:, 896:1024]]
        tri = [self.cf32[:, 256:384], self.cf32[:, 384:512]]

        cw = P.sb("convw_sb", [128, 24, 5], F32)
        cbv = P.sb("convb_sb", [128, 24], F32)
        O.dma("sp", cw[:], convw)
        O.dma("sp", cbv[:], convb)
        row = P.sb("ssdrow_sb", [128, 160], F32)
        O.dma("sp", row[:], ssdrow.broadcast_to([128, 160]))
        Abc = P.sb("Abc", [128, 64], F32)
        O.act(Abc[:], row[:, 0:64], AF.Exp)
        O.ts("dve", Abc[:], Abc[:], -1.0, None, ALU.mult)
        dtb = row[:, 64:128]
        dskip = row[:, 128:160]

        mA = P.mark()
        win = P.sb("w_in_sb", [128, 8, 5184], BF16)
        wv = w_in.rearrange("(k p) n -> p k n", p=128)
        pieces = [(0, 1024), (1024, 2048), (2048, 3072), (3072, 4096), (4096, 5184)]

        def wkey(col):
            for i, (a, b) in enumerate(pieces):
                if a <= col < b:
                    return f"win{i}"
        O.dma("pool", (win[:], [f"win{i}" for i in range(len(pieces))]), wv)
        uw = [P.sb(f"uw{i}", [128, 8, 516], BF16) for i in range(3)]
        xts = [P.sb(f"xt{i}", [128, D], F32) for i in range(3)]
        xhs = [P.sb(f"xh{i}", [128, D], BF16) for i in range(2)]
        raw = [P.sb(f"raw{i}", [128, 516], F32) for i in range(2)]
        acc = [P.sb(f"cacc{i}", [128, 512], F32) for i in range(2)]
        ctmp = P.sb("ctmp", [128, 512], F32)
        cvo = P.sb("cvo", [128, 24, 512], BF16)
        xstg = [P.sb(f"xstg{i}", [128, 2048], BF16) for i in range(2)]
        bstg = [P.sb(f"bstg{i}", [128, 512], BF16) for i in range(2)]
        zstg = [P.sb(f"zstg{i}", [128, 2048], BF16) for i in range(2)]
        dstg = [P.sb(f"dstg{i}", [128, 64], F32) for i in range(2)]
        dtm = [P.sb(f"dtm{i}", [128, 64], F32) for i in range(2)]
        self._cnt = 0

        def S1(src, r0, ntok, buf, first, last, prevbuf, nextbuf):
            for t in range(ntok // 128):
                c = self._cnt
                self._cnt += 1
                xt = xts[c % 3]
                xh = xhs[c % 2]
                self.load_tile(xt, src, r0 // 128 + t)
                self.norm_uT(xt, xh, lambda k, t=t: buf[:, k, 2 + t * 128:2 + (t + 1) * 128],
                             4 if src == "ctx" else 0, 4 + c % 2, c)
            if first:
                O.memset("pool", buf[:, :, 0:2], 0.0)
            else:
                O.cp("pool", prevbuf[:, :, 514:516], buf[:, :, 2:4])
            if last:
                O.memset("pool", buf[:, :, 2 + ntok:4 + ntok], 0.0)
            else:
                O.cp("pool", nextbuf[:, :, 0:2], buf[:, :, ntok:ntok + 2])

        def S2(r0, ntok, buf, need_c):
            W = ntok + 4
            half = W // 2
            nch = 24 if need_c else 20
            for c in range(nch):
                col0 = 2048 + c * 128
                pr = P.pb(2 * (c % 2), 2)
                for hh in range(2):
                    for k in range(8):
                        O.mm((pr[0][:, hh, 0:half], pr[1]), (win[:, k, col0:col0 + 128], wkey(col0)),
                             buf[:, k, hh * half:(hh + 1) * half], start=(k == 0), stop=(k == 7))
                rw = raw[c % 2]
                O.cp("act", rw[:, 0:W].rearrange("p (a b) -> p a b", a=2), (pr[0][:, :, 0:half], pr[1]))
                ac = acc[c % 2]
                if c % 3 != 2:
                    O.ts("dve", ac[:, :ntok], rw[:, 0:ntok], cw[:, c, 0:1], cbv[:, c:c + 1], ALU.mult, ALU.add)
                    for j in range(1, 5):
                        O.stt("dve", ac[:, :ntok], rw[:, j:j + ntok], cw[:, c, j:j + 1], ac[:, :ntok], ALU.mult, ALU.add)
                else:
                    O.ts("pool", ac[:, :ntok], rw[:, 0:ntok], cw[:, c, 0:1], cbv[:, c:c + 1], ALU.mult, ALU.add)
                    for j in range(1, 5):
                        O.ts("pool", ctmp[:, :ntok], rw[:, j:j + ntok], cw[:, c, j:j + 1], None, ALU.mult)
                        O.tt("pool", ac[:, :ntok], ac[:, :ntok], ctmp[:, :ntok], ALU.add)
                O.act((cvo[:, c, :ntok], f"cvo{c}"), ac[:, :ntok], AF.Silu)
            if need_c:
                O.dma("sp", (bt_d[:, :, r0:r0 + ntok], f"bt{r0 // 512}"),
                      (cvo[:, 16:20, :ntok], [f"cvo{c}" for c in range(16, 20)]), grp="cvo_st")
                O.dma("sp", (ct_d[:, :, r0:r0 + ntok], f"ct{r0 // 512}"),
                      (cvo[:, 20:24, :ntok], [f"cvo{c}" for c in range(20, 24)]), grp="cvo_st")
            for t in range(ntok // 128):
                xst = xstg[t % 2]
                bst = bstg[t % 2]
                tb = [P.pb(4, 1, BF16), P.pb(5, 1, BF16), P.pb(6, 1, BF16)]
                tv = [(x[0].rearrange("p (k t) -> p k t", k=8), x[1]) for x in tb]
                for c in range(20):
                    O.tr((tv[c // 8][0][:, c % 8, :], tv[c // 8][1]), (cvo[:, c, t * 128:(t + 1) * 128], f"cvo{c}"), self.ident)
                O.cp("act", xst[:, 0:1024], tb[0])
                O.cp("dve", xst[:, 1024:2048], tb[1])
                O.cp("act", bst[:], (tb[2][0][:, 0:512], tb[2][1]))
                rr = r0 + t * 128
                O.dma("sp", (xs_d[rr:rr + 128, :], f"xs{rr // 512}"), xst[:])
                O.dma("sp", (bs_d[rr:rr + 128, :], f"bs{rr // 512}"), bst[:])
                pdt = P.pb(7)
                pdv = (pdt[0][:, 0:64], pdt[1])
                for k in range(8):
                    O.mm(pdv, buf[:, k, 2 + t * 128:2 + (t + 1) * 128], (win[:, k, 5120:5184], "win4"),
                         start=(k == 0), stop=(k == 7))
                dm = dtm[t % 2]
                O.tt("dve", dm[:], pdv, dtb, ALU.add)
                O.act(dm[:], dm[:], AF.Exp)
                ds = dstg[t % 2]
                O.act(ds[:], dm[:], AF.Ln, bias=1.0)
                O.dma("sp", (dt_d[rr:rr + 128, :], f"dt{rr // 512}"), ds[:])
                if need_c:
                    zt = zstg[t % 2]
                    for cb4 in range(4):
                        pz = P.pb(7 if cb4 % 2 == 0 else 6)
                        for k in range(8):
                            O.mm(pz, buf[:, k, 2 + t * 128:2 + (t + 1) * 128],
                                 (win[:, k, cb4 * 512:(cb4 + 1) * 512], wkey(cb4 * 512)), start=(k == 0), stop=(k == 7))
                        O.act(zt[:, cb4 * 512:(cb4 + 1) * 512], pz, AF.Silu)
                    O.dma("sp", (zs_d[rr:rr + 128, :], f"zs{rr // 512}"), zt[:])

        ng = SEQ // 512
        for g in range(ng + 1):
            if g < ng:
                S1("seq", g * 512, 512, uw[g % 3], g == 0, g == ng - 1, uw[(g - 1) % 3], uw[(g + 1) % 3])
            if g >= 1:
                S2((g - 1) * 512, 512, uw[(g - 1) % 3], (g - 1) < HALF // 512)
        S1("ctx", 0, CTX, uw[0], True, True, None, None)
        S2(SEQ, CTX, uw[0], True)
        P.release(mA)
        P.release_hi()

        mB = P.mark()
        esel = P.sb("esel_sb", [32, 32, 128], BF16)
        O.dma("sp", esel[:], esel_d.rearrange("k (t s) -> k t s", t=32))
        nwb = P.sb("normw_bc", [128, 2048], F32)
        O.dma("sp", nwb[:], normw.broadcast_to([128, 2048]))
        wout = P.sb("wout_ssd", [128, 16, D], BF16)
        O.dma("pool", wout[:], w_out.rearrange("(k p) n -> p k n", p=128))
        st = P.sb("sst", [128, 4, 512], F32)
        stb = P.sb("sstb", [128, 4, 512], BF16)
        xs_t = [P.sb(f"xs_t{i}", [128, 2048], BF16) for i in range(2)]
        bs_t = [P.sb(f"bs_t{i}", [128, 512], BF16) for i in range(2)]
        dt_t = [P.sb(f"dt_t{i}", [128, 64], F32) for i in range(2)]
        bt_t = [P.sb(f"bt_t{i}", [128, 4, 128], BF16) for i in range(2)]
        ct_t = [P.sb(f"ct_t{i}", [128, 4, 128], BF16) for i in range(2)]
        sm = P.sb("ssm", [128, 8, 32], F32)
        hilo = P.sb("hilo", [32, 2, 128], BF16)
        xdt = P.sb("xdt", [128, 2048], BF16)
        xdtd = P.sb("xdtd", [128, 2048], BF16)
        cbs = P.sb("cbs", [128, 512], F32)
        dec = [P.sb(f"dec{i}", [128, 128], F32) for i in range(3)]
        wts = [P.sb(f"wts{i}", [128, 128], BF16) for i in range(3)]
        ytmp = P.sb("ytmp", [128, 512], F32)
        ybuf = [P.sb(f"ybuf{i}", [128, 2048], F32) for i in range(2)]
        yf_t = P.sb("yf_t", [128, 2048], F32)
        zs_t = P.sb("zs_t", [128, 2048], BF16)
        tmp2 = P.sb("tmp2k", [128, 2048], F32)
        ynb = P.sb("ynb", [128, 2048], BF16)
        ynT = P.sb("ynT", [128, 16, 128], BF16)
        fjunk = P.sb("sjunk", [128, 512], BF16)
        R = self.resid_setup()
        self._ci = 0

        def core(dirn, row0, need_y, yb):
            i = self._ci
            self._ci += 1
            g5 = row0 // 512
            xt_, bt_, dtt = xs_t[i % 2], bs_t[i % 2], dt_t[i % 2]
            O.dma("sp", xt_[:], (xs_d[row0:row0 + 128, :], f"xs{g5}"))
            O.dma("sp", bt_[:], (bs_d[row0:row0 + 128, :], f"bs{g5}"))
            O.dma("sp", dtt[:], (dt_d[row0:row0 + 128, :], f"dt{g5}"))
            dtv = dtt[:, dirn * 32:(dirn + 1) * 32]
            a, acs, cdec, dtmp, dte, nacs, eacs = [sm[:, j, :] for j in range(7)]
            O.tt("dve", a, dtv, Abc[:, dirn * 32:(dirn + 1) * 32], ALU.mult)
            pm = P.pb(0)
            O.mm((pm[0][:, 0:32], pm[1]), tri[dirn], a)
            O.mm((pm[0][:, 32:64], pm[1]), self.ones_f, a)
            O.cp("act", acs, (pm[0][:, 0:32], pm[1]))
            O.act(cdec, (pm[0][:, 32:64], pm[1]), AF.Exp)
            O.tt("dve", dtmp, (pm[0][:, 32:64], pm[1]), acs, ALU.subtract)
            O.act(dte, dtmp, AF.Exp)
            x3 = xt_[:].rearrange("p (h q) -> p h q", h=32)
            O.tt("dve", xdt[:].rearrange("p (h q) -> p h q", h=32), x3,
                 dtv.unsqueeze(2).to_broadcast([128, 32, 64]), ALU.mult)
            O.tt("pool", xdtd[:].rearrange("p (h q) -> p h q", h=32), xdt[:].rearrange("p (h q) -> p h q", h=32),
                 dte.unsqueeze(2).to_broadcast([128, 32, 64]), ALU.mult)
            if need_y:
                btt, ctt = bt_t[i % 2], ct_t[i % 2]
                O.dma("sp", btt[:], (bt_d[:, :, row0:row0 + 128], f"bt{g5}"))
                O.dma("sp", ctt[:], (ct_d[:, :, row0:row0 + 128], f"ct{g5}"))
                O.mm((pm[0][0:32, 64:192], pm[1]), a, tri[dirn])
                O.cp("act", hilo[:, 0, :], (pm[0][0:32, 64:192], pm[1]))
                O.tt("dve", hilo[:, 1, :], (pm[0][0:32, 64:192], pm[1]), hilo[:, 0, :], ALU.subtract)
                O.ts("dve", nacs, acs, -1.0, None, ALU.mult)
                O.act(eacs, acs, AF.Exp)
                pcb = P.pb(3)
                for g in range(4):
                    O.mm((pcb[0][:, g * 128:(g + 1) * 128], pcb[1]), btt[:, g, :], ctt[:, g, :])
                O.cp("act", cbs[:], pcb)
                for g in range(4):
                    pyd = P.pb(6)
                    pyo = P.pb(7)
                    O.mm(pyo, ctt[:, g, :], stb[:, g, :])
                    def seg(h):
                        psg_ = P.pb(4 + (h // 4) % 2)
                        pr_ = (psg_[0][:, (h % 4) * 128:(h % 4 + 1) * 128], psg_[1])
                        O.mm(pr_, esel[:, h, :], hilo[:, 0, :], start=True, stop=False)
                        O.mm(pr_, esel[:, h, :], hilo[:, 1, :], start=False, stop=False)
                        O.mm(pr_, self.ident, nm[dirn], start=False, stop=True)
                        return pr_
                    prs = {g * 8: seg(g * 8)}
                    for hh in range(8):
                        h = g * 8 + hh
                        if hh + 1 < 8:
                            prs[h + 1] = seg(h + 1)
                        pr = prs[h]
                        dc = dec[h % 3]
                        O.act(dc[:], pr, AF.Exp, bias=nacs[:, h:h + 1])
                        wt = wts[h % 3]
                        O.tt("dve", wt[:], dc[:], cbs[:, g * 128:(g + 1) * 128], ALU.mult)
                        O.mm((pyd[0][:, hh * 64:(hh + 1) * 64], pyd[1]), wt[:], xdt[:, h * 64:(h + 1) * 64])
                    O.tt("dve", ytmp[:].rearrange("p (h q) -> p h q", h=8),
                         (pyo[0].rearrange("p (h q) -> p h q", h=8), pyo[1]),
                         eacs[:, g * 8:(g + 1) * 8].unsqueeze(2).to_broadcast([128, 8, 64]), ALU.mult)
                    O.tt("dve", yb[:, g * 512:(g + 1) * 512], ytmp[:], pyd, ALU.add)
            for g in range(4):
                pcs = P.pb(1 + g % 2)
                O.mm(pcs, bt_[:, g * 128:(g + 1) * 128], xdtd[:, g * 512:(g + 1) * 512])
                sv = st[:, g, :].rearrange("p (h q) -> p h q", h=8)
                O.tt("pool", (sv, f"sst{g}"), (sv, f"sst{g}"),
                     cdec[:, g * 8:(g + 1) * 8].unsqueeze(2).to_broadcast([128, 8, 64]), ALU.mult)
                O.tt("dve", (st[:, g, :], f"sst{g}"), (st[:, g, :], f"sst{g}"), pcs, ALU.add)
                O.cp("act", (stb[:, g, :], f"sstb{g}"), (st[:, g, :], f"sst{g}"))

        def finalize(kind, t, row0, yb, xt_):
            g5 = row0 // 512
            O.dma("sp", yf_t[:], (yf_d[row0:row0 + 128, :], f"yf{g5}"))
            O.dma("sp", zs_t[:], (zs_d[row0:row0 + 128, :], f"zs{g5}"))
            O.tt("pool", yb[:], yb[:], yf_t[:], ALU.add)
            O.tt("dve", tmp2[:].rearrange("p (h q) -> p h q", h=32), xt_[:].rearrange("p (h q) -> p h q", h=32),
                 dskip.unsqueeze(2).to_broadcast([128, 32, 64]), ALU.mult)
            O.tt("pool", yb[:], yb[:], tmp2[:], ALU.add)
            O.tt("dve", yb[:], yb[:], zs_t[:], ALU.mult)
            ss4 = self.newstat(4)
            for g in range(4):
                O.act(fjunk[:], yb[:, g * 512:(g + 1) * 512], AF.Square, accum_out=(ss4[0][:, g:g + 1], ss4[1]))
            sd4 = self.newstat(4)
            O.act(sd4, ss4, AF.Sqrt, bias=self.eps_t, scale=1.0 / 512)
            rs4 = self.newstat(4)
            O.recip(rs4, sd4)
            for g in range(4):
                O.ts("dve" if g % 2 == 0 else "pool", yb[:, g * 512:(g + 1) * 512], yb[:, g * 512:(g + 1) * 512],
                     (rs4[0][:, g:g + 1], rs4[1]), None, ALU.mult)
            O.tt("dve", ynb[:], yb[:], nwb[:], ALU.mult)
            tb = [P.pb(4, 1, BF16), P.pb(5, 1, BF16)]
            tv = [(x[0].rearrange("p (k t) -> p k t", k=8), x[1]) for x in tb]
            for c in range(16):
                O.tr((tv[c // 8][0][:, c % 8, :], tv[c // 8][1]), ynb[:, c * 128:(c + 1) * 128], self.ident)
            O.cp("act", ynT[:, 0:8, :], tv[0])
            O.cp("dve", ynT[:, 8:16, :], tv[1])
            self.resid_tile(R, kind, t, lambda c: ynT[:, c, :], lambda c: wout[:, c, :], 16, banks=(6, 6))

        def zero_state():
            for g in range(4):
                O.memset("pool", (st[:, g, :], f"sst{g}"), 0.0)
                O.memset("pool", (stb[:, g, :], f"sstb{g}"), 0.0)

        zero_state()
        chain_f = [("ctx", t, SEQ + t * 128) for t in range(CTX // 128)] + [("lat", t, t * 128) for t in range(HALF // 128)]
        for n, (kind, t, row0) in enumerate(chain_f):
            yb = ybuf[n % 2]
            core(0, row0, True, yb)
            O.dma("sp", (yf_d[row0:row0 + 128, :], f"yf{row0 // 512}"), yb[:])
        zero_state()
        chain_b = [("ctx", t, SEQ + t * 128) for t in reversed(range(CTX // 128))]
        chain_b += [("oth", t, HALF + t * 128) for t in reversed(range(HALF // 128))]
        chain_b += [("lat", t, t * 128) for t in reversed(range(HALF // 128))]
        for n, (kind, t, row0) in enumerate(chain_b):
            yb = ybuf[n % 2]
            i = self._ci
            core(1, row0, kind != "oth", yb)
            if kind != "oth":
                finalize(kind, t, row0, yb, xs_t[i % 2])
        P.release(mB)

    def phase_ffn(self):
        P, O = self.P, self.O
        m = P.mark()
        win = P.sb("win", [128, 8, 2 * FFN], BF16)
        wout = P.sb("wout", [128, NFC, D], BF16)
        wiv = self.ffn_w_in.rearrange("(k p) n -> p k n", p=128)
        O.dma("pool", (win[:], [f"win{j}" for j in range(4)]), wiv)
        O.dma("pool", wout[:], self.ffn_w_out.rearrange("(k p) n -> p k n", p=128))
        g2 = [P.sb(f"g2_{i}", [128, D], F32) for i in range(2)]
        O.dma("sp", g2[0][:], (self.gates[2], "gates2"))
        O.dma("sp", g2[1][:], (self.gates[3], "gates3"))
        xts = [P.sb(f"fxt{i}", [128, D], F32) for i in range(2)]
        xhs = [P.sb(f"fxh{i}", [128, D], BF16) for i in range(2)]
        uT = P.sb("fuT", [128, 8, 512], BF16)
        hT = P.sb("fhT", [128, NFC, 512], BF16)
        sg = [P.sb(f"fsg{i}", [128, 512], F32) for i in range(2)]
        tmp = [P.sb(f"ftmp{i}", [128, D], F32) for i in range(2)]
        junk = P.sb("fjunk", [128, D], BF16)
        groups = [("lat", g * 512, 512) for g in range(HALF // 512)]
        if self.want_ctx:
            groups.append(("ctx", HALF, CTX))
        cnt = 0
        for gi, (kind, r0, ntok) in enumerate(groups):
            nt = ntok // 128
            for t in range(nt):
                xt = xts[cnt % 2]
                xh = xhs[cnt % 2]
                row0 = r0 + t * 128
                O.dma("sp", xt[:], (self.hmid[row0:row0 + 128, :], f"hmid{row0 // 512}"))
                self.norm_uT(xt, xh, lambda k, t=t: uT[:, k, t * 128:(t + 1) * 128],
                             2 if kind == "lat" else 6, 7 * (cnt % 2), cnt)
                cnt += 1
            for fc in range(NFC):
                pg = P.pb(1 + 2 * (fc % 2))
                pu = P.pb(2 + 2 * (fc % 2))
                pgv = (pg[0][:, :ntok], pg[1])
                puv = (pu[0][:, :ntok], pu[1])
                cg = fc * 128
                cu = FFN + fc * 128
                for k in range(8):
                    O.mm(pgv, (win[:, k, cg:cg + 128], f"win{cg // 1408}"), uT[:, k, :ntok], start=(k == 0), stop=(k == 7))
                for k in range(8):
                    O.mm(puv, (win[:, k, cu:cu + 128], f"win{cu // 1408}"), uT[:, k, :ntok], start=(k == 0), stop=(k == 7))
                st = sg[fc % 2]
                O.act(st[:, :ntok], pgv, AF.Silu)
                O.tt("dve", hT[:, fc, :ntok], st[:, :ntok], puv, ALU.mult)
            for t in range(nt):
                pf = P.pb(5, 2)
                for cb in range(2):
                    for fc in range(NFC):
                        O.mm((pf[0][:, cb, :], pf[1]), hT[:, fc, t * 128:(t + 1) * 128], wout[:, fc, cb * 512:(cb + 1) * 512],
                             start=(fc == 0), stop=(fc == NFC - 1))
                xt = xts[cnt % 2]
                cnt += 1
                row0 = r0 + t * 128
                O.dma("sp", xt[:], (self.hmid[row0:row0 + 128, :], f"hmid{row0 // 512}"))
                ss = self.newstat()
                O.act(junk[:].rearrange("p (a b) -> p a b", a=2), pf, AF.Square, accum_out=ss)
                rs = self.rstd_from_ss(ss, D)
                tm = tmp[t % 2]
                O.stt("dve", tm[:].rearrange("p (a b) -> p a b", a=2), pf, rs,
                      g2[0 if kind == "lat" else 1][:].rearrange("p (a b) -> p a b", a=2), ALU.mult, ALU.mult)
                O.tt("pool", tm[:], tm[:], xt[:], ALU.add)
                if kind == "lat":
                    O.dma("sp", (self.hout[0][row0:row0 + 128, :], self.hout[1]), tm[:])
                    if not self.last:
                        prv = P.pb(1, 2)
                        for cb in range(2):
                            O.mm((prv[0][:, cb, :], prv[1]), self.cf32[:, 512:640], tm[:, cb * 512:(cb + 1) * 512])
                        rvt = tmp[(t + 1) % 2]
                        O.cp("act", rvt[:].rearrange("p (a b) -> p a b", a=2), prv)
                        rr = HALF - 128 - row0
                        hr = self.hrev[rr // 512]
                        O.dma("sp", (hr[0][rr % 512:rr % 512 + 128, :], hr[1]), rvt[:])
                else:
                    O.dma("sp", (self.hctx_out[0][row0 - HALF:row0 - HALF + 128, :], self.hctx_out[1]), tm[:])
        P.release(m)


PAIRS = [[0, 1], [2, 3], [4, 5], [6, 7]]


class Fused:
    def __init__(self):
        nc = bass.Bass("TRN2", target_bir_lowering=False)
        self.nc = nc
        self.P = Prog(nc)
        self.O = Ops(self.P)
        self.ins = {}
        P, O = self.P, self.O

        def din(name, shape, dt=F32):
            t = nc.dram_tensor(name, list(shape), dt, kind="ExternalInput").ap()
            self.ins[name] = t
            return t
        cbf_d = din("cbf", [128, 1024], BF16)
        cf32_d = din("cf32", [128, 640])
        self.cbf = P.sb("cbf_sb", [128, 1024], BF16)
        self.cf32 = P.sb("cf32_sb", [128, 640], F32)
        O.dma("sp", self.cbf[:], cbf_d)
        O.dma("sp", self.cf32[:], cf32_d)
        hfm_d = din("hfmask", [128, 2])
        self.hfm = P.sb("hfm_sb", [128, 2], F32)
        O.dma("sp", self.hfm[:], hfm_d)
        self.src_seq = (din("x_in", [SEQ, D]), "x_in")
        self.src_ctx = (din("ctx_in", [CTX, D]), "ctx_in")
        self.src_rev = None
        for li in range(NLAYERS):
            L = Layer(li, self)
            if li < NLAYERS - 1:
                hags = []
                for c in range(8):
                    hag = nc.dram_tensor(f"hagrev{li}_{c}", [1024, D], F32, kind="Internal").ap()
                    O.allgather((hag, f"hagrev{li}_{c}"), L.hrev[c], PAIRS)
                    hags.append((hag, f"hagrev{li}_{c}"))
                self.src_seq = L.hout
                self.src_rev = hags
                self.src_ctx = L.hctx_out
        P.emit()


_CONSTS = None
_FUSED = None


def _want(hf):
    if hf == 0:
        return np.arange(SEQ)
    return np.arange(SEQ)[::-1].copy()


def _layer_inputs(li, b, hf, inp, C):
    f32 = np.float32
    kind, j = li % 3, li // 3
    pre = f"L{li}_"
    mp = {}
    want = _want(hf)
    vec = np.concatenate([inp["ada_b"][li].reshape(48, 128), inp["norm_g"][li].reshape(32, 128),
                          inp["c"][b].reshape(8, 128), inp["c_ctx"].reshape(8, 128)], axis=0)
    mp["vecT"] = np.ascontiguousarray(vec.T, dtype=f32)
    mp["rowv"] = np.ascontiguousarray(np.stack([inp["ada_b"][li][2 * D:3 * D], inp["ada_b"][li][5 * D:6 * D],
                                                inp["norm_g"][li][1], inp["norm_g"][li][3]]), dtype=f32)
    mp["ada_w"] = np.ascontiguousarray(inp["ada_w"][li], dtype=f32)
    mp["ffn_w_in"] = np.ascontiguousarray(inp["ffn_w_in"][li], dtype=f32)
    mp["ffn_w_out"] = np.ascontiguousarray(inp["ffn_w_out"][li], dtype=f32)
    if kind == 0:
        mp["w_qkv"] = np.ascontiguousarray(inp["da_w_qkv"][j], dtype=f32)
        mp["w_o"] = np.ascontiguousarray(inp["da_w_o"][j], dtype=f32)
        mp["lamv"] = np.ascontiguousarray(inp["da_lambda"][j].reshape(1, 256), dtype=f32)
        mp["subln"] = np.ascontiguousarray(inp["da_subln"][j].reshape(128, 1), dtype=f32)
        mp["cosT"] = np.ascontiguousarray(C["cosT"][:, want])
        mp["sinT"] = np.ascontiguousarray(C["sinT"][:, want])
    elif kind == 1:
        mp["w_qkv"] = np.ascontiguousarray(inp["sw_w_qkv"][j], dtype=f32)
        mp["w_o"] = np.ascontiguousarray(inp["sw_w_o"][j], dtype=f32)
        mp["sinkv"] = np.ascontiguousarray(inp["sw_sink"][j].reshape(4, 2, 2).transpose(2, 0, 1).reshape(2, 8), dtype=f32)
        em = np.ones((128, 2), f32)
        em[:, 0] = 0.0
        mp["emask"] = em
        wpos = np.concatenate([want[:HALF], want[:128], want[HALF:HALF + 128]])
        mp["cosT"] = np.ascontiguousarray(C["cosT"][:, wpos])
        mp["sinT"] = np.ascontiguousarray(C["sinT"][:, wpos])
    else:
        rev = hf == 1
        w = inp["ssd_w_in"][j]
        cwt = inp["ssd_conv_w"][j]
        alog, dtbias = inp["ssd_a_log"][j], inp["ssd_dt_bias"][j]
        if rev:
            w = np.concatenate([w[:, :5120], w[:, 5152:5184], w[:, 5120:5152]], axis=1)
            cwt = cwt[::-1]
            alog, dtbias = alog[::-1], dtbias[::-1]
        mp["w_in"] = np.ascontiguousarray(w, dtype=f32)
        mp["convw"] = np.ascontiguousarray(cwt.T.reshape(24, 128, 5).transpose(1, 0, 2), dtype=f32)
        mp["convb"] = np.ascontiguousarray(inp["ssd_conv_b"][j].reshape(24, 128).T, dtype=f32)
        mp["ssdrow"] = np.ascontiguousarray(np.concatenate([alog.reshape(64), dtbias.reshape(64),
                                                            inp["ssd_d_skip"][j].reshape(32)]).reshape(1, 160), dtype=f32)
        mp["normw"] = np.ascontiguousarray(inp["ssd_norm"][j].reshape(1, 2048), dtype=f32)
        mp["w_out"] = np.ascontiguousarray(inp["ssd_w_out"][j], dtype=f32)
        mp["esel"] = C["esel"]
    return {pre + k: v for k, v in mp.items()}


def kernel(**inp):
    global _CONSTS, _FUSED
    inp = {k: np.asarray(v) for k, v in inp.items()}
    if _CONSTS is None:
        _CONSTS = _consts()
    if _FUSED is None:
        _FUSED = Fused()
    C, Fz = _CONSTS, _FUSED
    in_maps = []
    for core in range(8):
        b, hf = core // 2, core % 2
        want = _want(hf)
        hfm = np.zeros((128, 2), np.float32)
        hfm[:, hf] = 1.0
        cx = inp["ctx"][b][::-1] if hf == 1 else inp["ctx"][b]
        mp = {"cbf": C["cbf"], "cf32": C["cf32"], "hfmask": hfm,
              "x_in": np.ascontiguousarray(inp["x"][b][want], dtype=np.float32),
              "ctx_in": np.ascontiguousarray(cx, dtype=np.float32)}
        for li in range(NLAYERS):
            mp.update(_layer_inputs(li, b, hf, inp, C))
        in_maps.append({k: mp[k] for k in Fz.ins})
    res = run_bass_kernel_spmd(Fz.nc, in_maps[:NCORES], core_ids=list(range(NCORES)))
    out = np.zeros((4, SEQ, D), np.float32)
    for core in range(NCORES):
        b, hf = core // 2, core % 2
        out[b, _want(hf)[:HALF]] = res.results[core]["hout"]
    return out
```
